# Optimizing a Trainium2 kernel written in Bass

```python
import jax, jax.numpy as jnp
from jax import lax
import numpy as np

D_MODEL = 1024
BATCH = 8
SEQ = 4096
DEPTH = 1

GRID_W = 64
CTX_LEN = 256
D_MIX = D_MODEL
ML_HEADS = 4
ML_HEAD_DIM = 128
ML_WIDTH = ML_HEADS * ML_HEAD_DIM
HG_HEADS = 4
HG_HEAD_DIM = 128
HG_WIDTH = HG_HEADS * HG_HEAD_DIM
ML_COLS = 4 * ML_WIDTH + 4 * ML_HEADS
HG_COLS = 5 * HG_WIDTH
IN_COLS = ML_COLS + HG_COLS
CONV_W = 3
D_FF = 2816
ML_CHUNK = 64
HG_CHUNK = 32
N_MOD = 6
EPS = 1e-6

kernel_name = "hybrid_mlstm_hgrn2_convffn_prefix_block"

F32 = jnp.float32


def _rmsnorm(x, w):
    xf = x.astype(F32)
    y = xf * lax.rsqrt(jnp.mean(xf * xf, axis=-1, keepdims=True) + EPS)
    return (y * w.astype(F32)).astype(x.dtype)


def _modulate(h, shift, scale):
    return h * (1 + scale) + shift


def _head_rmsnorm(h, w):
    B, H, T, d = h.shape
    y = h * lax.rsqrt(jnp.mean(h * h, axis=-1, keepdims=True) + EPS)
    y = jnp.transpose(y, (0, 2, 1, 3)).reshape(B, T, H * d)
    return y * w.astype(F32)


def _to_heads(a, n_heads):
    B, T, C = a.shape
    return jnp.transpose(a.reshape(B, T, n_heads, C // n_heads), (0, 2, 1, 3)).astype(F32)


def _dwconv2d(x, w, rows, cols):
    B, T, C = x.shape
    img = x.reshape(B, rows, cols, C)
    out = lax.conv_general_dilated(img, w[:, :, None, :].astype(x.dtype), (1, 1), 'SAME',
                                   dimension_numbers=('NHWC', 'HWIO', 'NHWC'), feature_group_count=C)
    return out.reshape(B, T, C)


def _chunks(a, L):
    B, H, T = a.shape[:3]
    a = a.reshape(B, H, T // L, L, *a.shape[3:])
    return jnp.moveaxis(a, 2, 0)


def _unchunks(a):
    nc, B, H, L, d = a.shape
    return jnp.moveaxis(a, 0, 2).reshape(B, H, nc * L, d)


def _mlstm_scan(q, k, v, ig, lf, state):
    L = ML_CHUNK
    causal = jnp.tril(jnp.ones((L, L), bool))

    def step(carry, inp):
        C, n, m = carry
        qc, kc, vc, ic, fc = inp
        b = jnp.cumsum(fc, axis=-1)
        dlog = jnp.where(causal, b[..., :, None] - b[..., None, :] + ic[..., None, :], -jnp.inf)
        inter = b + m[..., None]
        mt = jnp.maximum(inter, jnp.max(dlog, axis=-1))
        w_intra = jnp.exp(dlog - mt[..., None])
        w_inter = jnp.exp(inter - mt)
        s = jnp.einsum('bhtd,bhsd->bhts', qc, kc) * w_intra
        num = jnp.einsum('bhts,bhsv->bhtv', s, vc) + w_inter[..., None] * jnp.einsum('bhtd,bhdv->bhtv', qc, C)
        den = jnp.sum(s, axis=-1) + w_inter * jnp.einsum('bhtd,bhd->bht', qc, n)
        h = num / jnp.maximum(jnp.abs(den), jnp.exp(-mt))[..., None]
        bL = b[..., -1]
        wlog = bL[..., None] - b + ic
        m_new = jnp.maximum(bL + m, jnp.max(wlog, axis=-1))
        decay = jnp.exp(bL + m - m_new)
        ws = jnp.exp(wlog - m_new[..., None])
        C_new = decay[..., None, None] * C + jnp.einsum('bhs,bhsd,bhsv->bhdv', ws, kc, vc)
        n_new = decay[..., None] * n + jnp.einsum('bhs,bhsd->bhd', ws, kc)
        return (C_new, n_new, m_new), h

    state, hs = lax.scan(step, state, tuple(_chunks(a, L) for a in (q, k, v, ig, lf)))
    return _unchunks(hs), state


def _hgrn_scan(q, k, v, lf, S):
    L = HG_CHUNK
    causal = jnp.tril(jnp.ones((L, L), bool))[:, :, None]

    def step(S, inp):
        qc, kc, vc, fc = inp
        b = jnp.cumsum(fc, axis=2)
        rel = jnp.where(causal, b[:, :, :, None, :] - b[:, :, None, :, :], -jnp.inf)
        a = jnp.einsum('bhtk,bhtsk,bhsk->bhts', qc, jnp.exp(rel), kc)
        o = jnp.einsum('bhts,bhsv->bhtv', a, vc) + jnp.einsum('bhtk,bhkv->bhtv', qc * jnp.exp(b), S)
        bL = b[:, :, -1]
        S_new = jnp.exp(bL)[..., None] * S + jnp.einsum('bhsk,bhsv->bhkv', kc * jnp.exp(bL[:, :, None] - b), vc)
        return S_new, o

    S, os_ = lax.scan(step, S, tuple(_chunks(a, L) for a in (q, k, v, lf)))
    return _unchunks(os_), S


def _bidir(scan_fn, ctx_f, ctx_b, lat_f, lat_b, init):
    flip = lambda a: jnp.flip(a, axis=2)
    hc_f, st_f = scan_fn(*ctx_f, init)
    hl_f, _ = scan_fn(*lat_f, st_f)
    hc_b, st_b = scan_fn(*(flip(a) for a in ctx_b), init)
    hl_b, _ = scan_fn(*(flip(a) for a in lat_b), st_b)
    return hl_f + flip(hl_b), hc_f + flip(hc_b)


def _mlstm_prep(p, conv_w, gate_b, rows, cols):
    W, H = ML_WIDTH, ML_HEADS
    qk = jax.nn.silu(_dwconv2d(p[..., :2 * W], conv_w, rows, cols))
    q = _to_heads(qk[..., :W], H) * (ML_HEAD_DIM ** -0.5)
    k = _to_heads(qk[..., W:], H)
    v = _to_heads(p[..., 2 * W:3 * W], H)
    o = p[..., 3 * W:4 * W]
    g = jnp.transpose(p[..., 4 * W:].astype(F32) + gate_b.astype(F32), (0, 2, 1))
    i_f, f_f, i_b, f_b = jnp.split(g, 4, axis=1)
    return q, k, v, o, (i_f, jax.nn.log_sigmoid(f_f)), (i_b, jax.nn.log_sigmoid(f_b))


def _mlstm_mixer(pl, pc, conv_w, gate_b, norm_w, rows, need_ctx):
    ql, kl, vl, ol, gfl, gbl = _mlstm_prep(pl, conv_w, gate_b, rows, GRID_W)
    qc, kc, vc, oc, gfc, gbc = _mlstm_prep(pc, conv_w, gate_b, 1, pc.shape[1])
    B = pl.shape[0]
    init = (jnp.zeros((B, ML_HEADS, ML_HEAD_DIM, ML_HEAD_DIM), F32),
            jnp.zeros((B, ML_HEADS, ML_HEAD_DIM), F32),
            jnp.zeros((B, ML_HEADS), F32))
    hl, hc = _bidir(_mlstm_scan, (qc, kc, vc) + gfc, (qc, kc, vc) + gbc,
                    (ql, kl, vl) + gfl, (ql, kl, vl) + gbl, init)
    out = lambda h, o: (_head_rmsnorm(h, norm_w) * jax.nn.sigmoid(o.astype(F32))).astype(pl.dtype)
    return out(hl, ol), (out(hc, oc) if need_ctx else None)


def _hgrn_prep(p, lb):
    W, H = HG_WIDTH, HG_HEADS
    pf = p.astype(F32)
    q = _to_heads(jax.nn.silu(pf[..., :W]), H) * (HG_HEAD_DIM ** -0.5)
    v = _to_heads(pf[..., 3 * W:4 * W], H)
    g = p[..., 4 * W:5 * W]
    dirs = []
    for j in range(2):
        z = pf[..., (1 + j) * W:(2 + j) * W]
        lbj = lb[j]
        lf = jnp.logaddexp(jnp.log(lbj), jnp.log1p(-lbj) + jax.nn.log_sigmoid(z))
        k = (1 - lbj) * jax.nn.sigmoid(-z)
        dirs.append((q, _to_heads(k, H), v, _to_heads(lf, H)))
    return dirs[0], dirs[1], g


def _hgrn_mixer(pl, pc, lb, norm_w, need_ctx):
    lat_f, lat_b, gl = _hgrn_prep(pl, lb)
    ctx_f, ctx_b, gc = _hgrn_prep(pc, lb)
    B = pl.shape[0]
    init = jnp.zeros((B, HG_HEADS, HG_HEAD_DIM, HG_HEAD_DIM), F32)
    hl, hc = _bidir(_hgrn_scan, ctx_f, ctx_b, lat_f, lat_b, init)
    out = lambda h, g: (_head_rmsnorm(h, norm_w) * jax.nn.silu(g.astype(F32))).astype(pl.dtype)
    return out(hl, gl), (out(hc, gc) if need_ctx else None)


def _conv_ffn(h, w_up, conv_w, w_down, rows, cols):
    u = _dwconv2d(h @ w_up, conv_w, rows, cols)
    a, b = jnp.split(u, 2, axis=-1)
    return (jax.nn.silu(a) * b) @ w_down


def setup_inputs(seed: int = 0) -> dict:
    key = jax.random.key(seed)
    ks = jax.random.split(key, 24)
    nrm = lambda k, shape, s: jax.random.normal(k, shape, F32) * s
    L, D = DEPTH, D_MODEL
    gate_b = jnp.concatenate([
        nrm(ks[8], (L, ML_HEADS), 0.1),
        jnp.linspace(3.0, 6.0, ML_HEADS, dtype=F32)[None] + nrm(ks[9], (L, ML_HEADS), 0.1),
        nrm(ks[10], (L, ML_HEADS), 0.1),
        jnp.linspace(3.0, 6.0, ML_HEADS, dtype=F32)[None] + nrm(ks[11], (L, ML_HEADS), 0.1)], axis=-1)
    return {
        "x": nrm(ks[0], (BATCH, SEQ, D), 1.0),
        "c": nrm(ks[1], (BATCH, D), 1.0),
        "ctx": nrm(ks[2], (BATCH, CTX_LEN, D), 1.0),
        "c_ctx": nrm(ks[3], (D,), 1.0),
        "w_mod": nrm(ks[4], (L, D, N_MOD * D), D ** -0.5),
        "b_mod": nrm(ks[5], (L, N_MOD * D), 0.02),
        "norm1_w": 1.0 + nrm(ks[6], (L, D), 0.05),
        "w_in": nrm(ks[7], (L, D, IN_COLS), D ** -0.5),
        "mlstm_gate_b": gate_b,
        "mlstm_conv_w": nrm(ks[12], (L, CONV_W, CONV_W, 2 * ML_WIDTH), 1.0 / CONV_W),
        "mlstm_norm_w": 1.0 + nrm(ks[13], (L, ML_WIDTH), 0.05),
        "hgrn_lb_logits": nrm(ks[14], (2, L + 1, HG_WIDTH), 0.1),
        "hgrn_norm_w": 1.0 + nrm(ks[15], (L, HG_WIDTH), 0.05),
        "w_out": nrm(ks[16], (L, D_MIX, D), D_MIX ** -0.5),
        "norm2_w": 1.0 + nrm(ks[17], (L, D), 0.05),
        "w_up": nrm(ks[18], (L, D, 2 * D_FF), D ** -0.5),
        "ffn_conv_w": nrm(ks[19], (L, CONV_W, CONV_W, 2 * D_FF), 1.0 / CONV_W),
        "w_down": nrm(ks[20], (L, D_FF, D), D_FF ** -0.5),
        "final_norm_w": 1.0 + nrm(ks[21], (D,), 0.05),
    }


def reference(x, c, ctx, c_ctx, w_mod, b_mod, norm1_w, w_in, mlstm_gate_b, mlstm_conv_w, mlstm_norm_w,
              hgrn_lb_logits, hgrn_norm_w, w_out, norm2_w, w_up, ffn_conv_w, w_down, final_norm_w):
    rows = x.shape[1] // GRID_W
    ctx_len = ctx.shape[1]
    lower_bounds = jnp.cumsum(jax.nn.softmax(hgrn_lb_logits.astype(F32), axis=1), axis=1)
    xl, xc = x, ctx
    for l in range(DEPTH):
        last = l == DEPTH - 1
        mod_l = (jax.nn.silu(c) @ w_mod[l] + b_mod[l])[:, None, :]
        mod_c = (jax.nn.silu(c_ctx) @ w_mod[l] + b_mod[l])[None, None, :]
        sh1, sc1, g1, sh2, sc2, g2 = jnp.split(mod_l, N_MOD, axis=-1)
        sh1c, sc1c, g1c, sh2c, sc2c, g2c = jnp.split(mod_c, N_MOD, axis=-1)
        pl = _modulate(_rmsnorm(xl, norm1_w[l]), sh1, sc1) @ w_in[l]
        pc = _modulate(_rmsnorm(xc, norm1_w[l]), sh1c, sc1c) @ w_in[l]
        ml_l, ml_c = _mlstm_mixer(pl[..., :ML_COLS], pc[..., :ML_COLS], mlstm_conv_w[l], mlstm_gate_b[l],
                                  mlstm_norm_w[l], rows, not last)
        hg_l, hg_c = _hgrn_mixer(pl[..., ML_COLS:], pc[..., ML_COLS:], lower_bounds[:, l], hgrn_norm_w[l], not last)
        xl = xl + g1 * (jnp.concatenate([ml_l, hg_l], axis=-1) @ w_out[l])
        xl = xl + g2 * _conv_ffn(_modulate(_rmsnorm(xl, norm2_w[l]), sh2, sc2), w_up[l], ffn_conv_w[l],
                                 w_down[l], rows, GRID_W)
        if not last:
            xc = xc + g1c * (jnp.concatenate([ml_c, hg_c], axis=-1) @ w_out[l])
            xc = xc + g2c * _conv_ffn(_modulate(_rmsnorm(xc, norm2_w[l]), sh2c, sc2c), w_up[l], ffn_conv_w[l],
                                      w_down[l], 1, ctx_len)
    return _rmsnorm(xl, final_norm_w)
```

```python
import math
from contextlib import ExitStack

import numpy as np
import ml_dtypes
import concourse.bass as bass
import concourse.mybir as mybir
from concourse.bass_utils import run_bass_kernel_spmd

F32 = mybir.dt.float32
BF16 = mybir.dt.bfloat16
AF = mybir.ActivationFunctionType
ALU = mybir.AluOpType
AX = mybir.AxisListType

ENGS = ("pe", "act", "dve", "pool", "sp")
SIG_ROT = 6000

NT = 34
NTOK = 4352
EPS = 1e-6
LN_SQRT128 = 0.5 * math.log(128.0)
QS = 128.0 ** -0.5


class Res:
    __slots__ = ("name", "last_w", "rd_eng", "rd_dma")

    def __init__(self, name=""):
        self.name = name
        self.last_w = None
        self.rd_eng = {}
        self.rd_dma = []


class Op:
    __slots__ = ("eng", "fn", "deps", "is_dma", "dkey", "dval", "sig", "nsig")

    def __init__(self, eng, fn, is_dma):
        self.eng = eng
        self.fn = fn
        self.deps = []
        self.is_dma = is_dma
        self.dkey = None
        self.dval = 0
        self.sig = None
        self.nsig = False


class Prog:
    def __init__(self, nc):
        self.nc = nc
        self.ops = {e: [] for e in ENGS}
        self.dma_cnt = {}
        self.last_dma = {}

    def add(self, eng, fn, reads=(), writes=(), dma=None):
        op = Op(eng, fn, dma is not None)
        deps = {}
        for r in reads:
            if r.last_w is not None:
                deps[id(r.last_w)] = (r.last_w, True)
        for w in writes:
            if w.last_w is not None and id(w.last_w) not in deps:
                deps[id(w.last_w)] = (w.last_w, False)
            for rd in list(w.rd_eng.values()) + w.rd_dma:
                if id(rd) not in deps:
                    deps[id(rd)] = (rd, False)
        for (p, raw) in deps.values():
            if p.eng == eng and not p.is_dma:
                if eng == "pe" or not raw:
                    continue
            op.deps.append(p)
            p.nsig = True
        for r in reads:
            if op.is_dma:
                r.rd_dma.append(op)
            else:
                r.rd_eng[eng] = op
        for w in writes:
            w.last_w = op
            w.rd_eng = {}
            w.rd_dma = []
        if dma is not None:
            c = self.dma_cnt.get(dma, 0) + 1
            self.dma_cnt[dma] = c
            op.dkey = dma
            op.dval = 16 * c
            self.last_dma[dma] = op
        self.ops[eng].append(op)
        return op

    def barrier(self):
        lasts = []
        for e in ENGS:
            for op in reversed(self.ops[e]):
                if not op.is_dma and op.fn is not None:
                    lasts.append(op)
                    break
        lasts += list(self.last_dma.values())
        for e in ENGS:
            op = Op(e, None, False)
            for p in lasts:
                if p.eng == e and not p.is_dma:
                    continue
                op.deps.append(p)
                p.nsig = True
            self.ops[e].append(op)

    def emit(self, stack):
        nc = self.nc
        nsems = {}
        for e in ENGS:
            cnt = 0
            for op in self.ops[e]:
                if op.is_dma or not op.nsig:
                    continue
                op.sig = (cnt // SIG_ROT, cnt % SIG_ROT + 1)
                cnt += 1
            nsems[e] = (cnt + SIG_ROT - 1) // SIG_ROT
        sems = {}
        for e in ENGS:
            for s in range(nsems[e]):
                sems[(e, s)] = stack.enter_context(nc.semaphore(f"s_{e}_{s}"))
        dsems = {}
        for i, k in enumerate(self.dma_cnt.keys()):
            dsems[k] = stack.enter_context(nc.semaphore(f"d_{i}"))
        block = stack.enter_context(nc.Block())
        engobj = {"pe": block.tensor, "act": block.scalar, "dve": block.vector,
                  "pool": block.gpsimd, "sp": block.sync}

        def make(e):
            def body(eng):
                waited = {}
                for op in self.ops[e]:
                    for p in op.deps:
                        if p.is_dma:
                            key = ("d", p.dkey)
                            sem = dsems[p.dkey]
                            val = p.dval
                        else:
                            key = (p.eng, p.sig[0])
                            sem = sems[key]
                            val = p.sig[1]
                        if waited.get(key, 0) >= val:
                            continue
                        waited[key] = val
                        eng.wait_ge(sem, val)
                    if op.fn is None:
                        continue
                    ins = op.fn(eng)
                    if op.is_dma:
                        ins.then_inc(dsems[op.dkey], 16)
                    elif op.nsig:
                        ins.then_inc(sems[(e, op.sig[0])], 1)
            return body

        for e in ENGS:
            engobj[e](make(e))


def build_program():
    nc = bass.Bass("TRN2", target_bir_lowering=False)
    D = {}

    def din(name, shape, dt=F32):
        D[name] = nc.dram_tensor(name, list(shape), dt, kind="ExternalInput").ap()

    din("xin", [NTOK, 1024])
    din("c2", [128, 8, 2])
    din("w_mod", [1024, 6144])
    din("b_modT", [128, 48])
    din("n1T", [128, 8])
    din("n2T", [128, 8])
    din("fnw", [128, 1024])
    din("w_in", [1024, 4624])
    din("gate_b", [128, 16])
    din("mcw", [128, 8, 9])
    din("fcw", [128, 44, 9])
    din("nwT", [128, 8])
    din("lbl", [128, 2, 2, 512])
    din("w_out", [1024, 1024])
    din("w_up", [1024, 5632])
    din("w_down", [2816, 1024])
    din("cf32", [128, 8, 128])
    din("cwd", [128, 4])
    din("identb", [128, 128], BF16)
    out = nc.dram_tensor("out", [4096, 1024], F32, kind="ExternalOutput").ap()
    ysc = nc.dram_tensor("ysc", [4096, 1024], BF16, kind="Internal").ap()
    x1sc = nc.dram_tensor("x1sc", [4096, 1024], F32, kind="Internal").ap()

    with ExitStack() as st:
        P = Prog(nc)
        A = P.add

        def T(name, shape, dt):
            return st.enter_context(nc.sbuf_tensor(name, list(shape), dt))

        xT = T("xT", [128, 8, NTOK], BF16)
        RxT = [Res(f"xT{t}") for t in range(NT)]
        cf = T("cf", [128, 8, 128], F32)
        identf, onesf, maskf, maskb, SU, SL, Mdf, Mdb = [cf[:, i, :] for i in range(8)]
        cwd = T("cwd_sb", [128, 4], F32)
        identb = T("identb_sb", [128, 128], BF16)
        Rc = Res("consts")
        modT = T("modT", [128, 48, 2], F32)
        prm = T("prm", [128, 6, 8], F32)
        Rprm = Res("prm")
        n12 = T("n12", [128, 2, 8], F32)
        bmT = T("bmT", [128, 48], F32)
        FA = T("FA", [128, 12400], F32)
        BA = T("BA", [128, 44000], BF16)
        pbs = [st.enter_context(nc.psum_tensor(f"pb{i}", [128, 512], F32)) for i in range(8)]
        pts = [pbs[6 + i][:, :].bitcast(BF16) for i in range(2)]
        Rpb = [Res(f"pb{i}") for i in range(8)]
        Rpt = [Rpb[6], Rpb[7]]

        fa_off = [0]
        ba_off = [0]

        def fa(n, shape=None):
            o = fa_off[0]
            fa_off[0] += n
            assert fa_off[0] <= 12400, fa_off[0]
            v = FA[:, o:o + n]
            return v

        def ba(n):
            o = ba_off[0]
            ba_off[0] += n
            assert ba_off[0] <= 44000, ba_off[0]
            return BA[:, o:o + n]

        def reset_arenas():
            P.barrier()
            fa_off[0] = 0
            ba_off[0] = 0

        A("sp", lambda e: e.dma_start(out=cf[:], in_=D["cf32"][:, :, :]), writes=[Rc], dma="c0")
        A("sp", lambda e: e.dma_start(out=cwd[:], in_=D["cwd"][:, :]), writes=[Rc], dma="c1")
        A("sp", lambda e: e.dma_start(out=identb[:], in_=D["identb"][:, :]), writes=[Rc], dma="c2")
        A("sp", lambda e: e.dma_start(out=n12[:, 0, :], in_=D["n1T"][:, :]), writes=[Rprm], dma="c3")
        A("sp", lambda e: e.dma_start(out=n12[:, 1, :], in_=D["n2T"][:, :]), writes=[Rprm], dma="c4")
        A("sp", lambda e: e.dma_start(out=bmT[:], in_=D["b_modT"][:, :]), writes=[Rprm], dma="c5")

        c2t = fa(16).rearrange("p (k m) -> p k m", m=2)
        sct = fa(16).rearrange("p (k m) -> p k m", m=2)
        tmpc = fa(16).rearrange("p (k m) -> p k m", m=2)
        Rc2 = Res()
        A("sp", lambda e: e.dma_start(out=c2t, in_=D["c2"][:, :, :]), writes=[Rc2], dma="c6")
        A("act", lambda e: e.activation(out=tmpc, in_=c2t, func=AF.Exp, scale=-1.0), reads=[Rc2], writes=[Rc2])
        A("dve", lambda e: e.tensor_scalar_add(out=tmpc, in0=tmpc, scalar1=1.0), reads=[Rc2], writes=[Rc2])
        A("dve", lambda e: e.reciprocal(out=tmpc, in_=tmpc), reads=[Rc2], writes=[Rc2])
        A("dve", lambda e: e.tensor_tensor(out=sct, in0=c2t, in1=tmpc, op=ALU.mult), reads=[Rc2], writes=[Rc2])
        wm = [ba(4096).rearrange("p (k n) -> p k n", n=512) for _ in range(3)]
        Rwm = [Res() for _ in range(3)]
        sctb = ba(16).rearrange("p (k m) -> p k m", m=2)
        A("dve", lambda e: e.tensor_copy(out=sctb, in_=sct), reads=[Rc2], writes=[Rc2])
        w_mod_v = D["w_mod"].rearrange("(k p) n -> p k n", p=128)
        for jj in range(12):
            s = jj % 3
            A("pool", lambda e, jj=jj, s=s: e.dma_start(out=wm[s], in_=w_mod_v[:, :, jj * 512:(jj + 1) * 512]),
              writes=[Rwm[s]], dma=f"wm{s}")

            def mm(e, jj=jj, s=s):
                for q in range(4):
                    j = jj * 4 + q
                    for k in range(8):
                        ins = e.matmul(pbs[0][:, 2 * j:2 * j + 2], lhsT=wm[s][:, k, q * 128:(q + 1) * 128], rhs=sctb[:, k, :],
                                       start=(k == 0), stop=(k == 7))
                return ins
            A("pe", mm, reads=[Rwm[s], Rc2], writes=[Rpb[0]])
        A("dve", lambda e: e.tensor_tensor(out=modT[:], in0=pbs[0][:, 0:96].rearrange("p (j m) -> p j m", m=2),
                                           in1=bmT[:].unsqueeze(2).to_broadcast([128, 48, 2]), op=ALU.add),
          reads=[Rpb[0], Rprm], writes=[Rprm])
        for (pi, nidx, scj, col) in ((0, 0, 8, 0), (2, 0, 8, 1), (4, 1, 32, 0)):
            A("dve", lambda e, pi=pi, nidx=nidx, scj=scj, col=col: e.scalar_tensor_tensor(
                out=prm[:, pi, :], in0=modT[:, scj:scj + 8, col], scalar=1.0, in1=n12[:, nidx, :],
                op0=ALU.add, op1=ALU.mult), reads=[Rprm], writes=[Rprm])
        for (pi, shj, col) in ((1, 0, 0), (3, 0, 1), (5, 24, 0)):
            A("dve", lambda e, pi=pi, shj=shj, col=col: e.tensor_copy(out=prm[:, pi, :], in_=modT[:, shj:shj + 8, col]),
              reads=[Rprm], writes=[Rprm])

        xts = [fa(1024) for _ in range(3)]
        Rxt = [Res() for _ in range(3)]
        junk = fa(1024)
        Rjunk = Res()
        ssb = fa(8)
        Rss = [Res() for _ in range(4)]
        xnb = [ba(1024) for _ in range(2)]
        Rxn = [Res() for _ in range(2)]
        modtmp = [fa(1024) for _ in range(2)]
        Rmt = [Res() for _ in range(2)]

        def p1_stageA(t):
            s = t % 3
            s4 = t % 4
            s2 = t % 2
            ssv = ssb[:, s4:s4 + 1]
            A("sp", lambda e: e.dma_start(out=xts[s], in_=D["xin"][t * 128:(t + 1) * 128, :]), writes=[Rxt[s]], dma=f"xt{s}")
            A("act", lambda e: e.activation(out=junk, in_=xts[s], func=AF.Square, scale=1.0 / 32.0, accum_out=ssv),
              reads=[Rxt[s]], writes=[Rjunk, Rss[s4]])
            A("act", lambda e: e.activation(out=ssv, in_=ssv, func=AF.Ln, bias=EPS), reads=[Rss[s4]], writes=[Rss[s4]])
            A("act", lambda e: e.activation(out=ssv, in_=ssv, func=AF.Exp, scale=-0.5), reads=[Rss[s4]], writes=[Rss[s4]])
            A("dve", lambda e: e.tensor_scalar_mul(out=xnb[s2], in0=xts[s], scalar1=ssv),
              reads=[Rss[s4], Rxt[s]], writes=[Rxn[s2]])

        def p1_stageB(t):
            s2 = t % 2
            pa, psh = (2, 3) if t < 2 else (0, 1)

            def tr(e):
                for k in range(8):
                    ins = e.transpose(out=pts[s2][:, k * 128:(k + 1) * 128], in_=xnb[s2][:, k * 128:(k + 1) * 128],
                                      identity=identb[:])
                return ins
            A("pe", tr, reads=[Rxn[s2], Rc], writes=[Rpt[s2]])

            A("dve", lambda e: e.tensor_tensor(out=modtmp[s2].rearrange("p (k c) -> p k c", k=8),
                                               in0=pts[s2][:, :].rearrange("p (k c) -> p k c", k=8),
                                               in1=prm[:, pa, :].unsqueeze(2).to_broadcast([128, 8, 128]), op=ALU.mult),
              reads=[Rpt[s2], Rprm], writes=[Rmt[s2]])
            A("dve", lambda e: e.tensor_tensor(out=xT[:, :, t * 128:(t + 1) * 128],
                                               in0=modtmp[s2].rearrange("p (k c) -> p k c", k=8),
                                               in1=prm[:, psh, :].unsqueeze(2).to_broadcast([128, 8, 128]), op=ALU.add),
              reads=[Rmt[s2], Rprm], writes=[RxT[t]])

        for i in range(NT + 1):
            if i < NT:
                p1_stageA(i)
            if i >= 1:
                p1_stageB(i - 1)

        reset_arenas()
        WK = fa(NT * 8).rearrange("p (t g) -> p t g", g=8)
        FL = fa(NT * 8).rearrange("p (t g) -> p t g", g=8)
        DC = fa(NT * 8).rearrange("p (t g) -> p t g", g=8)
        Rgs = Res("gatescal")
        gbt = fa(16)
        Rgb = Res()
        A("sp", lambda e: e.dma_start(out=gbt, in_=D["gate_b"][:, :]), writes=[Rgb], dma="gb")
        wg = ba(128).rearrange("p (k n) -> p k n", n=16)
        Rwg = Res()
        w_in_v = D["w_in"].rearrange("(k p) n -> p k n", p=128)
        A("pool", lambda e: e.dma_start(out=wg, in_=w_in_v[:, :, 2048:2064]), writes=[Rwg], dma="wg")
        gps = [fa(16) for _ in range(2)]
        nls = [fa(16) for _ in range(2)]
        tm8 = [fa(8) for _ in range(2)]
        Rgp = [Res() for _ in range(2)]
        def p2a_A(t):
            s = t % 2
            b0 = 0 if s == 0 else 2

            def mmg(e):
                for k in range(8):
                    ins = e.matmul(pbs[b0][:, 0:16], lhsT=xT[:, k, t * 128:(t + 1) * 128], rhs=wg[:, k, :],
                                   start=(k == 0), stop=(k == 7))
                return ins
            A("pe", mmg, reads=[RxT[t], Rwg], writes=[Rpb[b0]])
            A("dve", lambda e: e.tensor_tensor(out=gps[s], in0=pbs[b0][:, 0:16], in1=gbt, op=ALU.add),
              reads=[Rpb[b0], Rgb], writes=[Rgp[s]])
            A("act", lambda e: e.activation(out=nls[s], in_=gps[s], func=AF.Exp, scale=-1.0),
              reads=[Rgp[s]], writes=[Rnl[s]])
            A("act", lambda e: e.activation(out=nls[s], in_=nls[s], func=AF.Ln, bias=1.0),
              reads=[Rnl[s]], writes=[Rnl[s]])

        def p2a_B(t):
            s = t % 2
            b1 = 1 if s == 0 else 3

            def mmc(e):
                e.matmul(pbs[b1][:, 0:16], lhsT=SU, rhs=nls[s], start=True, stop=True)
                e.matmul(pbs[b1][:, 16:32], lhsT=SL, rhs=nls[s], start=True, stop=True)
                return e.matmul(pbs[b1][:, 32:48], lhsT=onesf, rhs=nls[s], start=True, stop=True)
            A("pe", mmc, reads=[Rnl[s], Rc], writes=[Rpb[b1]])
            A("dve", lambda e: e.tensor_tensor(out=tm8[s][:, 0:4], in0=gps[s][:, 0:4], in1=pbs[b1][:, 4:8],
                                               op=ALU.subtract), reads=[Rgp[s], Rpb[b1]], writes=[Rtm[s]])
            A("dve", lambda e: e.tensor_tensor(out=tm8[s][:, 4:8], in0=gps[s][:, 8:12], in1=pbs[b1][:, 28:32],
                                               op=ALU.subtract), reads=[Rgp[s], Rpb[b1]], writes=[Rtm[s]])
            A("act", lambda e: e.activation(out=WK[:, t, :], in_=tm8[s], func=AF.Exp),
              reads=[Rtm[s]], writes=[Rgs])
            for (dst, c0, c1, bias) in ((FL, 4, 0, LN_SQRT128), (FL, 28, 4, LN_SQRT128), (DC, 36, 0, 0.0), (DC, 44, 4, 0.0)):
                A("act", lambda e, dst=dst, c0=c0, c1=c1, bias=bias: e.activation(
                    out=dst[:, t, c1:c1 + 4], in_=pbs[b1][:, c0:c0 + 4], func=AF.Exp, scale=-1.0, bias=bias),
                  reads=[Rpb[b1]], writes=[Rgs])

        Rnl = [Res() for _ in range(2)]
        Rtm = [Res() for _ in range(2)]
        for i in range(NT + 1):
            if i < NT:
                p2a_A(i)
            if i >= 1:
                p2a_B(i - 1)

        RING = 4
        LA = 2

        def alloc_scan(Nv):
            return {
                "Z": [[fa(Nv) for _ in range(RING)] for _ in range(2)],
                "U": [[fa(Nv) for _ in range(RING)] for _ in range(2)],
                "S": [[ba(Nv) for _ in range(RING)] for _ in range(2)],
                "M": [[ba(128) for _ in range(RING)] for _ in range(2)],
                "RZ": [[Res() for _ in range(RING)] for _ in range(2)],
                "RU": [[Res() for _ in range(RING)] for _ in range(2)],
                "RS": [[Res() for _ in range(RING)] for _ in range(2)],
                "RM": [[Res() for _ in range(RING)] for _ in range(2)],
            }

        scan_bufs = alloc_scan(129)
        order_f = list(range(NT))
        order_b = [1, 0] + list(range(NT - 1, 1, -1))

        def run_scans(qT, kT, ktok, vt, Nv, Gfn, Pevac, Rin, Rg):
            sb = scan_bufs
            LA1 = 1
            for it_ in range(NT + LA1):
                m = it_
                if m < NT:
                    for d in range(2):
                        tl = (order_f if d == 0 else order_b)[m]
                        r = m % RING
                        bsc = 0 if d == 0 else 3
                        bU = ((2, 6) if d == 0 else (5, 7))[m % 2]
                        mask = maskf if d == 0 else maskb
                        if tl >= 2:
                            A("pe", lambda e, d=d, tl=tl, bsc=bsc: e.matmul(pbs[bsc][:, 0:128], lhsT=kT(d, tl), rhs=qT(d, tl),
                                                                            start=True, stop=True),
                              reads=Rin(tl), writes=[Rpb[bsc]])
                            A("dve", lambda e, d=d, r=r, bsc=bsc, mask=mask: e.tensor_tensor(
                                out=sb["M"][d][r], in0=pbs[bsc][:, 0:128], in1=mask, op=ALU.mult),
                              reads=[Rpb[bsc], Rc], writes=[sb["RM"][d][r]])
                        A("pe", lambda e, d=d, tl=tl, bU=bU: e.matmul(pbs[bU][:, 0:Nv], lhsT=ktok(d, tl), rhs=vt(d, tl),
                                                                      start=True, stop=True),
                          reads=Rin(tl), writes=[Rpb[bU]])
                m = it_ - LA1
                if m >= 0:
                    for d in range(2):
                        tl = (order_f if d == 0 else order_b)[m]
                        r = m % RING
                        rp = (m - 1) % RING
                        rn = (m + 1) % RING
                        bP = 1 if d == 0 else 4
                        bU = ((2, 6) if d == 0 else (5, 7))[m % 2]
                        if m > 0:
                            gprev = Gfn(d, m - 1)
                            A("dve", lambda e, d=d, r=r, rp=rp, gprev=gprev, bU=bU: e.scalar_tensor_tensor(
                                out=sb["Z"][d][r], in0=sb["Z"][d][rp], scalar=gprev, in1=pbs[bU][:, 0:Nv],
                                op0=ALU.mult, op1=ALU.add),
                              reads=[sb["RZ"][d][rp], Rpb[bU]] + Rg, writes=[sb["RZ"][d][r]])
                        else:
                            A("dve", lambda e, d=d, r=r, bU=bU: e.tensor_copy(out=sb["Z"][d][r], in_=pbs[bU][:, 0:Nv]),
                              reads=[Rpb[bU]], writes=[sb["RZ"][d][r]])
                        if m < NT - 1:
                            gthis = Gfn(d, m)
                            A("act", lambda e, d=d, r=r, rn=rn, gthis=gthis: e.activation(
                                out=sb["S"][d][rn], in_=sb["Z"][d][r], func=AF.Copy, scale=gthis),
                              reads=[sb["RZ"][d][r]] + Rg, writes=[sb["RS"][d][rn]])
                        if tl >= 2:
                            def mmP(e, d=d, tl=tl, bP=bP, m=m, r=r):
                                ins = e.matmul(pbs[bP][:, 0:Nv], lhsT=sb["M"][d][r], rhs=vt(d, tl), start=True, stop=(m == 0))
                                if m > 0:
                                    ins = e.matmul(pbs[bP][:, 0:Nv], lhsT=qT(d, tl), rhs=sb["S"][d][r], start=False, stop=True)
                                return ins
                            A("pe", mmP, reads=[sb["RM"][d][r], sb["RS"][d][r]] + Rin(tl), writes=[Rpb[bP]])
                            Pevac(d, tl, pbs[bP][:, 0:Nv], Rpb[bP])

        PAD = ba(66 * 66).rearrange("p (r c) -> p r c", c=66)
        PADC = ba(258)
        RPAD = Res()
        A("pool", lambda e: e.memset(PAD, 0.0), writes=[RPAD])
        A("pool", lambda e: e.memset(PADC, 0.0), writes=[RPAD])
        qkT = [ba(NTOK) for _ in range(2)]
        KTOK = ba(NT * 128).rearrange("p (t c) -> p t c", c=128)
        VT = [ba(NT * 129).rearrange("p (t c) -> p t c", c=129) for _ in range(2)]
        OGs = [ba(32 * 128).rearrange("p (t c) -> p t c", c=128) for _ in range(2)]
        W4 = [ba(4 * 8 * 128).rearrange("p (w k n) -> p w k n", w=4, n=128)] * 2
        DGm = ba(2 * 9 * 128).rearrange("p (w t n) -> p w t n", w=2, n=128)
        PST = [fa(32 * 129).rearrange("p (t c) -> p t c", c=129) for _ in range(2)]
        mcwt = fa(72).rearrange("p (c t) -> p c t", t=9)
        sqjM = fa(128)
        dent = [fa(32) for _ in range(2)]
        ssqM = fa(32)
        Rmcw = Res()
        RW4 = [Res()] * 2
        RDG = Res()
        Rhd = Res("headdata")
        ROGs = [Res("og0"), Res("og1")]
        RssM = Res()
        RPST = Res()
        Ryst = Res()
        A("sp", lambda e: e.dma_start(out=mcwt, in_=D["mcw"][:, :, :]), writes=[Rmcw], dma="mcw")
        ysc_v = ysc.rearrange("(t p) c -> p t c", p=128)

        def silu_evac(ps_ap, ps_view_fn, dst_ap, n, bank, scale=1.0):
            A("act", lambda e: e.activation(out=dst_ap, in_=ps_ap, func=AF.Silu), reads=[Rpb[bank]], writes=[Rhd])

        def fin_gen(h):
            OG = OGs[h % 2]
            ROG = ROGs[h % 2]
            for d in range(2):
                A("act", lambda e, d=d: e.activation(out=dent[d], in_=PST[d][:, :, 128], func=AF.Abs), reads=[RPST], writes=[Rdn[d]])
                yield
                A("dve", lambda e, d=d, h=h: e.tensor_tensor(out=dent[d], in0=dent[d], in1=FL[:, 2:NT, h + 4 * d], op=ALU.max),
                  reads=[Rdn[d], Rgs], writes=[Rdn[d]])
                A("dve", lambda e, d=d: e.reciprocal(out=dent[d], in_=dent[d]), reads=[Rdn[d]], writes=[Rdn[d]])
                yield
                A("dve", lambda e, d=d: e.tensor_tensor(out=PST[d][:, :, 0:128], in0=PST[d][:, :, 0:128],
                                                        in1=dent[d].unsqueeze(2).to_broadcast([128, 32, 128]), op=ALU.mult),
                  reads=[RPST, Rdn[d]], writes=[RPST])
                yield
            hs = PST[0][:, :, 0:128]
            A("dve", lambda e: e.tensor_tensor(out=hs, in0=hs, in1=PST[1][:, :, 0:128], op=ALU.add), reads=[RPST], writes=[RPST])
            yield
            for i in range(32):
                A("act", lambda e, i=i: e.activation(out=sqjM, in_=PST[0][:, i, 0:128], func=AF.Square, accum_out=ssqM[:, i:i + 1]),
                  reads=[RPST], writes=[RssM])
                if i % 4 == 3:
                    yield
            A("act", lambda e: e.activation(out=ssqM, in_=ssqM, func=AF.Ln, scale=1.0 / 128.0, bias=EPS), reads=[RssM], writes=[RssM])
            A("act", lambda e: e.activation(out=ssqM, in_=ssqM, func=AF.Exp, scale=-0.5, bias=math.log(0.5)), reads=[RssM], writes=[RssM])
            yield
            A("dve", lambda e: e.tensor_tensor(out=hs, in0=hs, in1=ssqM.unsqueeze(2).to_broadcast([128, 32, 128]), op=ALU.mult),
              reads=[RPST, RssM], writes=[RPST])
            yield
            A("dve", lambda e: e.scalar_tensor_tensor(out=OG, in0=OG, scalar=1.0, in1=hs, op0=ALU.add, op1=ALU.mult),
              reads=[RPST, ROG], writes=[ROG])
            A("sp", lambda e, h=h: e.dma_start(out=ysc_v[:, :, h * 128:(h + 1) * 128], in_=OG), reads=[ROG], writes=[],
              dma="yst")
            yield

        pending_fin = [None]
        Rdn = [Res(), Res()]

        def next_fin():
            g = pending_fin[0]
            if g is None:
                return
            try:
                next(g)
            except StopIteration:
                pending_fin[0] = None

        def drain_fin():
            while pending_fin[0] is not None:
                next_fin()

        for h in range(4):
            wslot = h % 2
            for wi, c0 in enumerate((h * 128, 512 + h * 128, 1024 + h * 128, 1536 + h * 128)):
                A("pool", lambda e, wi=wi, c0=c0, wslot=wslot: e.dma_start(out=W4[wslot][:, wi], in_=w_in_v[:, :, c0:c0 + 128]),
                  writes=[RW4[wslot]], dma=f"w4_{wslot}_{wi}")
            for wi in range(2):
                ch = h + 4 * wi
                A("dve", lambda e, wi=wi, ch=ch: e.tensor_tensor(
                    out=DGm[:, wi], in0=identb[:].unsqueeze(1).to_broadcast([128, 9, 128]),
                    in1=mcwt[:, ch, :].unsqueeze(2).to_broadcast([128, 9, 128]), op=ALU.mult),
                  reads=[Rmcw, Rc], writes=[RDG])
            def v_step(t, wslot=wslot, h=h):
                bank = 4 + t % 2

                def mmv(e):
                    for wi2, c0 in ((2, 0), (3, 128)):
                        if wi2 == 3 and t < 2:
                            continue
                        for k in range(8):
                            ins = e.matmul(pbs[bank][:, c0:c0 + 128], lhsT=xT[:, k, t * 128:(t + 1) * 128],
                                           rhs=W4[wslot][:, wi2, k, :], start=(k == 0), stop=(k == 7))
                    return ins
                A("pe", mmv, reads=[RxT[t], RW4[wslot]], writes=[Rpb[bank]])
                for d in range(2):
                    A("act", lambda e, d=d: e.activation(
                        out=VT[d][:, t, 0:128], in_=pbs[bank][:, 0:128], func=AF.Copy, scale=WK[:, t, h + 4 * d:h + 4 * d + 1]),
                      reads=[Rpb[bank], Rgs], writes=[Rhd])
                if t >= 2:
                    A("act", lambda e, OGh=OGs[h % 2]: e.activation(out=OGh[:, t - 2, :], in_=pbs[bank][:, 128:256], func=AF.Tanh, scale=0.5),
                      reads=[Rpb[bank]], writes=[ROGs[h % 2]])

            vcnt = [0]

            def next_v():
                if vcnt[0] < NT:
                    v_step(vcnt[0])
                    vcnt[0] += 1

            for wi in range(2):
                W = W4[wslot][:, wi]
                dstT = qkT[wi]
                for i in range(9):
                    bank = i % 2
                    if i == 0:
                        tok0, ntk = 0, 256
                    else:
                        tok0, ntk = 256 + (i - 1) * 512, 512
                    tiles_needed = [RxT[tt] for tt in range(tok0 // 128, (tok0 + ntk) // 128)]

                    def mmq(e, W=W, tok0=tok0, ntk=ntk, bank=bank):
                        for k in range(8):
                            ins = e.matmul(pbs[bank][:, 0:ntk], lhsT=W[:, k, :], rhs=xT[:, k, tok0:tok0 + ntk],
                                           start=(k == 0), stop=(k == 7))
                        return ins
                    A("pe", mmq, reads=tiles_needed + [RW4[wslot]], writes=[Rpb[bank]])
                    if i == 0:
                        A("act", lambda e, bank=bank: e.activation(out=PADC[:, 1:257], in_=pbs[bank][:, 0:256], func=AF.Copy),
                          reads=[Rpb[bank]], writes=[RPAD])
                    else:
                        r0 = 1 + 8 * (i - 1)
                        A("act", lambda e, bank=bank, r0=r0: e.activation(
                            out=PAD[:, r0:r0 + 8, 1:65], in_=pbs[bank][:, 0:512].rearrange("p (r c) -> p r c", c=64), func=AF.Copy),
                          reads=[Rpb[bank]], writes=[RPAD])
                    next_v()
                    next_fin()
                for i in range(9):
                    bank = 2 + i % 2
                    if i == 0:
                        def mmc0(e, wi=wi, bank=bank):
                            for j, tap in enumerate((3, 4, 5)):
                                ins = e.matmul(pbs[bank][:, 0:256], lhsT=DGm[:, wi, tap, :], rhs=PADC[:, j:j + 256],
                                               start=(j == 0), stop=(j == 2))
                            return ins
                        A("pe", mmc0, reads=[RPAD, RDG], writes=[Rpb[bank]])
                        silu_evac(pbs[bank][:, 0:256], None, dstT[:, 0:256], 256, bank)
                    else:
                        r0 = 8 * (i - 1)

                        def mmc1(e, wi=wi, bank=bank, r0=r0):
                            for tap in range(9):
                                dr, dc = tap // 3, tap % 3
                                ins = e.matmul(pbs[bank][:, 0:512].rearrange("p (r c) -> p r c", c=64),
                                               lhsT=DGm[:, wi, tap, :], rhs=PAD[:, r0 + dr:r0 + dr + 8, dc:dc + 64],
                                               start=(tap == 0), stop=(tap == 8))
                            return ins
                        A("pe", mmc1, reads=[RPAD, RDG], writes=[Rpb[bank]])
                        t0 = 256 + (i - 1) * 512
                        silu_evac(pbs[bank][:, 0:512], None, dstT[:, t0:t0 + 512], 512, bank)
                    next_v()
                    next_fin()
            while vcnt[0] < NT:
                next_v()
            drain_fin()
            for g in range(5):
                t0g = g * 8
                ng = min(8, NT - t0g)
                pslot = g % 2

                def trk(e, t0g=t0g, ng=ng, pslot=pslot):
                    for j in range(ng):
                        ins = e.transpose(out=pts[pslot][:, j * 128:(j + 1) * 128],
                                          in_=qkT[1][:, (t0g + j) * 128:(t0g + j + 1) * 128], identity=identb[:])
                    return ins
                A("pe", trk, reads=[Rhd, Rc], writes=[Rpt[pslot]])
                A("act", lambda e, t0g=t0g, ng=ng, pslot=pslot: e.activation(
                    out=KTOK[:, t0g:t0g + ng, :], in_=pts[pslot][:, 0:ng * 128].rearrange("p (t c) -> p t c", c=128), func=AF.Copy),
                  reads=[Rpt[pslot]], writes=[Rhd])
            for d in range(2):
                A("dve", lambda e, d=d, h=h: e.tensor_copy(out=VT[d][:, :, 128:129], in_=WK[:, :, h + 4 * d:h + 4 * d + 1]),
                  reads=[Rgs], writes=[Rhd])

            def Gm(d, n, h=h):
                tln = (order_f if d == 0 else order_b)[n + 1]
                return DC[:, tln, h + 4 * d:h + 4 * d + 1]

            def Pev(d, tl, ps_ap, bres):
                A("act", lambda e: e.activation(out=PST[d][:, tl - 2, :], in_=ps_ap, func=AF.Copy), reads=[bres], writes=[RPST])
            run_scans(lambda d, tl: qkT[0][:, tl * 128:(tl + 1) * 128], lambda d, tl: qkT[1][:, tl * 128:(tl + 1) * 128],
                      lambda d, tl: KTOK[:, tl, :], lambda d, tl: VT[d][:, tl, :], 129, Gm, Pev, lambda tl: [Rhd], [Rgs])

            pending_fin[0] = fin_gen(h)
        drain_fin()

        reset_arenas()
        scan_bufs = alloc_scan(128)
        QKT = ba(NT * 512).rearrange("p (t c) -> p t c", c=512)
        KHa = ba(NT * 256).rearrange("p (t d c) -> p t d c", d=2, c=128)
        VH = ba(NT * 128).rearrange("p (t c) -> p t c", c=128)
        GG = ba(32 * 128).rearrange("p (t c) -> p t c", c=128)
        W5 = [ba(5 * 8 * 128).rearrange("p (w k n) -> p w k n", w=5, n=128)] * 2
        qtok = [ba(256).rearrange("p (d c) -> p d c", d=2) for _ in range(2)]
        PSTh = fa(32 * 128).rearrange("p (t c) -> p t c", c=128)
        sqj = fa(128)
        GCb = fa(NT * 4).rearrange("p (t c) -> p t c", c=4)
        GT = [fa(NT) for _ in range(2)]
        LB = fa(256).rearrange("p (d c) -> p d c", d=2)
        OML = fa(256).rearrange("p (d c) -> p d c", d=2)
        sgb = [fa(512) for _ in range(2)]
        qsb = [fa(128) for _ in range(3)]
        fgt = [fa(256) for _ in range(2)]
        lft = [fa(256) for _ in range(2)]
        kkt = [fa(256) for _ in range(3)]
        ept = [fa(256) for _ in range(2)]
        emt = [fa(256) for _ in range(2)]
        ssqH = fa(32)
        lfh = [ba(256) for _ in range(2)]
        lfl = [ba(256) for _ in range(2)]
        Rlh = [Res() for _ in range(2)]
        Mdb16 = ba(256).rearrange("p (d c) -> p d c", d=2)
        cwd16 = ba(4)
        Rc16 = Res()
        A("dve", lambda e: e.tensor_copy(out=Mdb16[:, 0, :], in_=Mdf), reads=[Rc], writes=[Rc16])
        A("dve", lambda e: e.tensor_copy(out=Mdb16[:, 1, :], in_=Mdb), reads=[Rc], writes=[Rc16])
        A("dve", lambda e: e.tensor_copy(out=cwd16, in_=cwd[:]), reads=[Rc], writes=[Rc16])
        Rlb = Res()
        RW5 = [Res()] * 2
        RGG = [Res() for _ in range(32)]
        RVH = [Res() for _ in range(NT)]
        RKH = [Res() for _ in range(NT)]
        RQK = [Res() for _ in range(NT)]
        RGT = Res()
        Rsg = [Res() for _ in range(2)]
        Rqs = [Res() for _ in range(3)]
        Rfg = [Res() for _ in range(2)]
        Rlf = [Res() for _ in range(2)]
        Rkk = [Res() for _ in range(3)]
        Rex = [Res() for _ in range(2)]
        Rqk = [Res() for _ in range(2)]
        RGC = Res()
        RPh = [Res() for _ in range(32)]
        RPall = Res()

        HG0 = 2064
        for h in range(4):
            hsl = slice(h * 128, (h + 1) * 128)
            A("sp", lambda e, hsl=hsl: e.dma_start(out=OML[:], in_=D["lbl"][:, :, 0, hsl]), writes=[Rlb], dma="lbl0")
            A("sp", lambda e, hsl=hsl: e.dma_start(out=LB[:], in_=D["lbl"][:, :, 1, hsl]), writes=[Rlb], dma="lbl1")
            A("dve", lambda e: e.tensor_tensor(out=LB[:], in0=LB[:], in1=OML[:], op=ALU.subtract), reads=[Rlb], writes=[Rlb])
            A("act", lambda e: e.activation(out=LB[:], in_=LB[:], func=AF.Exp), reads=[Rlb], writes=[Rlb])
            A("dve", lambda e: e.tensor_scalar_add(out=LB[:], in0=LB[:], scalar1=1.0), reads=[Rlb], writes=[Rlb])
            A("dve", lambda e: e.reciprocal(out=LB[:], in_=LB[:]), reads=[Rlb], writes=[Rlb])
            A("act", lambda e: e.activation(out=OML[:], in_=LB[:], func=AF.Identity, scale=-1.0, bias=1.0), reads=[Rlb], writes=[Rlb])
            cols = [HG0 + h * 128, HG0 + 512 + h * 128, HG0 + 1024 + h * 128, HG0 + 2048 + h * 128, HG0 + 1536 + h * 128]
            for wi, c0 in enumerate(cols):
                A("pool", lambda e, wi=wi, c0=c0: e.dma_start(out=W5[0][:, wi], in_=w_in_v[:, :, c0:c0 + 128]),
                  writes=[RW5[0]], dma=f"w5_{wi}")

            def st_mm5(t):
                bA = t % 2
                bV = 4 + t % 2

                def mm5(e):
                    for k in range(8):
                        e.matmul(pbs[bA][:, 0:512].rearrange("p (w n) -> p w n", w=4),
                                 lhsT=xT[:, k, t * 128:(t + 1) * 128], rhs=W5[0][:, 0:4, k, :],
                                 start=(k == 0), stop=(k == 7))
                    for k in range(8):
                        ins = e.matmul(pbs[bV][:, 0:128], lhsT=xT[:, k, t * 128:(t + 1) * 128], rhs=W5[0][:, 4, k, :],
                                       start=(k == 0), stop=(k == 7))
                    return ins
                A("pe", mm5, reads=[RxT[t], RW5[0]], writes=[Rpb[bA], Rpb[bV]])

            def st_sig(t):
                par = t % 2
                bA = par
                sg = sgb[par]
                A("act", lambda e: e.activation(out=sg, in_=pbs[bA][:, 0:512], func=AF.Exp, scale=-1.0),
                  reads=[Rpb[bA]], writes=[Rsg[par]])
                A("act", lambda e: e.activation(out=sg, in_=sg, func=AF.Ln, bias=1.0), reads=[Rsg[par]], writes=[Rsg[par]])
                A("act", lambda e: e.activation(out=sg, in_=sg, func=AF.Exp, scale=-1.0), reads=[Rsg[par]], writes=[Rsg[par]])

            def st_dvea(t):
                par = t % 2
                q3 = t % 3
                bA = par
                bV = 4 + par
                fv = fgt[par].rearrange("p (d c) -> p d c", d=2)
                A("dve", lambda e: e.tensor_tensor(
                    out=fv, in0=sgb[par][:, 128:384].rearrange("p (d c) -> p d c", d=2), in1=OML[:], op=ALU.mult),
                  reads=[Rsg[par], Rlb], writes=[Rfg[par]])
                A("dve", lambda e: e.tensor_tensor(out=fv, in0=fv, in1=LB[:], op=ALU.add),
                  reads=[Rfg[par], Rlb], writes=[Rfg[par]])
                A("dve", lambda e: e.scalar_tensor_tensor(out=qsb[q3], in0=pbs[bA][:, 0:128], scalar=QS,
                                                          in1=sgb[par][:, 0:128], op0=ALU.mult, op1=ALU.mult),
                  reads=[Rpb[bA], Rsg[par]], writes=[Rqs[q3]])
                if t >= 2:
                    A("dve", lambda e: e.tensor_tensor(out=GG[:, t - 2, :], in0=pbs[bA][:, 384:512],
                                                       in1=sgb[par][:, 384:512], op=ALU.mult),
                      reads=[Rpb[bA], Rsg[par]], writes=[RGG[t - 2]])
                A("dve", lambda e: e.tensor_copy(out=VH[:, t, :], in_=pbs[bV][:, 0:128]),
                  reads=[Rpb[bV]], writes=[RVH[t]])

            def st_lnf(t):
                par = t % 2
                bE = 2 + par
                A("act", lambda e: e.activation(out=lft[par], in_=fgt[par], func=AF.Ln), reads=[Rfg[par]], writes=[Rlf[par]])
                A("act", lambda e: e.activation(out=kkt[t % 3], in_=fgt[par], func=AF.Identity, scale=-1.0, bias=1.0),
                  reads=[Rfg[par]], writes=[Rkk[t % 3]])
                A("dve", lambda e: e.tensor_copy(out=lfh[par], in_=lft[par]), reads=[Rlf[par]], writes=[Rlh[par]])
                A("dve", lambda e: e.tensor_tensor(out=lfl[par], in0=lft[par], in1=lfh[par], op=ALU.subtract),
                  reads=[Rlf[par], Rlh[par]], writes=[Rlh[par]])

                def mme(e):
                    e.matmul(pbs[bE][:, 0:128], lhsT=Mdb16[:, 0, :], rhs=lfh[par][:, 0:128], start=True, stop=False)
                    e.matmul(pbs[bE][:, 0:128], lhsT=Mdb16[:, 0, :], rhs=lfl[par][:, 0:128], start=False, stop=True)
                    e.matmul(pbs[bE][:, 128:256], lhsT=Mdb16[:, 1, :], rhs=lfh[par][:, 128:256], start=True, stop=False)
                    e.matmul(pbs[bE][:, 128:256], lhsT=Mdb16[:, 1, :], rhs=lfl[par][:, 128:256], start=False, stop=True)
                    e.matmul(pbs[bE][:, 256:258], lhsT=lfh[par][:, 0:128], rhs=cwd16[:, 0:2], start=True, stop=False)
                    e.matmul(pbs[bE][:, 256:258], lhsT=lfl[par][:, 0:128], rhs=cwd16[:, 0:2], start=False, stop=True)
                    e.matmul(pbs[bE][:, 258:260], lhsT=lfh[par][:, 128:256], rhs=cwd16[:, 2:4], start=True, stop=False)
                    return e.matmul(pbs[bE][:, 258:260], lhsT=lfl[par][:, 128:256], rhs=cwd16[:, 2:4], start=False, stop=True)
                A("pe", mme, reads=[Rlh[par], Rc16], writes=[Rpb[bE]])

            def st_s2(t):
                par = t % 2
                q3 = t % 3
                bE = 2 + par
                A("act", lambda e: e.activation(out=ept[par], in_=pbs[bE][:, 0:256], func=AF.Exp),
                  reads=[Rpb[bE]], writes=[Rex[par]])
                A("act", lambda e: e.activation(out=emt[par], in_=pbs[bE][:, 0:256], func=AF.Exp, scale=-1.0),
                  reads=[Rpb[bE]], writes=[Rex[par]])
                A("act", lambda e: e.activation(out=GCb[:, t, :], in_=pbs[bE][:, 256:260], func=AF.Copy),
                  reads=[Rpb[bE]], writes=[RGC])

            def st_s2b(t):
                par = t % 2
                q3 = t % 3
                A("dve", lambda e: e.tensor_tensor(out=qtok[par], in0=ept[par].rearrange("p (d c) -> p d c", d=2),
                                                   in1=qsb[q3].unsqueeze(1).to_broadcast([128, 2, 128]), op=ALU.mult),
                  reads=[Rex[par], Rqs[q3]], writes=[Rqk[par]])
                A("dve", lambda e: e.tensor_tensor(out=KHa[:, t], in0=kkt[q3].rearrange("p (d c) -> p d c", d=2),
                                                   in1=emt[par].rearrange("p (d c) -> p d c", d=2), op=ALU.mult),
                  reads=[Rex[par], Rkk[q3]], writes=[Rqk[par], RKH[t]])

                def trq(e):
                    e.transpose(out=pts[par][:, 0:128], in_=qtok[par][:, 0, :], identity=identb[:])
                    e.transpose(out=pts[par][:, 128:256], in_=KHa[:, t, 0, :], identity=identb[:])
                    e.transpose(out=pts[par][:, 256:384], in_=qtok[par][:, 1, :], identity=identb[:])
                    return e.transpose(out=pts[par][:, 384:512], in_=KHa[:, t, 1, :], identity=identb[:])
                A("pe", trq, reads=[Rqk[par], Rc], writes=[Rpt[par]])
                A("dve", lambda e: e.tensor_copy(out=QKT[:, t, :], in_=pts[par][:, 0:512]),
                  reads=[Rpt[par]], writes=[RQK[t]])

            for i in range(NT + 2):
                if i < NT:
                    st_mm5(i)
                if 0 <= i - 2 < NT:
                    st_s2(i - 2)
                if 0 <= i - 1 < NT:
                    st_lnf(i - 1)
                if 0 <= i - 2 < NT:
                    st_s2b(i - 2)
                if i < NT:
                    st_sig(i)
                    st_dvea(i)
            A("dve", lambda e: e.tensor_tensor(out=GT[0][:, 0:NT - 1], in0=GCb[:, 0:NT - 1, 0], in1=GCb[:, 1:NT, 1],
                                               op=ALU.add), reads=[RGC], writes=[RGC])
            A("dve", lambda e: e.tensor_tensor(out=GT[1][:, 1:NT], in0=GCb[:, 1:NT, 2], in1=GCb[:, 0:NT - 1, 3],
                                               op=ALU.add), reads=[RGC], writes=[RGC])
            A("dve", lambda e: e.tensor_tensor(out=GT[1][:, 0:1], in0=GCb[:, 0, 2:3], in1=GCb[:, NT - 1, 3:4],
                                               op=ALU.add), reads=[RGC], writes=[RGC])
            A("act", lambda e: e.activation(out=GT[0][:, 0:NT - 1], in_=GT[0][:, 0:NT - 1], func=AF.Exp), reads=[RGC], writes=[RGT])
            A("act", lambda e: e.activation(out=GT[1], in_=GT[1], func=AF.Exp), reads=[RGC], writes=[RGT])

            def Gh(d, n):
                tl = (order_f if d == 0 else order_b)[n]
                return GT[d][:, tl:tl + 1]

            seen = set()

            def Pevh(d, tl, ps_ap, bres, seen=seen):
                i = tl - 2
                if i not in seen:
                    seen.add(i)
                    A("act", lambda e: e.activation(out=PSTh[:, i, :], in_=ps_ap, func=AF.Copy), reads=[bres], writes=[RPh[i]])
                else:
                    A("dve", lambda e: e.tensor_tensor(out=PSTh[:, i, :], in0=PSTh[:, i, :], in1=ps_ap, op=ALU.add),
                      reads=[bres, RPh[i]], writes=[RPh[i]])
            run_scans(lambda d, tl: QKT[:, tl, d * 256:d * 256 + 128], lambda d, tl: QKT[:, tl, d * 256 + 128:d * 256 + 256],
                      lambda d, tl: KHa[:, tl, d, :], lambda d, tl: VH[:, tl, :], 128, Gh, Pevh,
                      lambda tl: [RQK[tl], RKH[tl], RVH[tl]], [RGT])
            for i in range(32):
                A("act", lambda e, i=i: e.activation(out=sqj, in_=PSTh[:, i, :], func=AF.Square, accum_out=ssqH[:, i:i + 1]),
                  reads=[RPh[i]], writes=[RPall])
            A("act", lambda e: e.activation(out=ssqH, in_=ssqH, func=AF.Ln, scale=1.0 / 128.0, bias=EPS), reads=[RPall], writes=[RPall])
            A("act", lambda e: e.activation(out=ssqH, in_=ssqH, func=AF.Exp, scale=-0.5), reads=[RPall], writes=[RPall])
            A("dve", lambda e: e.tensor_tensor(out=PSTh, in0=PSTh, in1=ssqH.unsqueeze(2).to_broadcast([128, 32, 128]), op=ALU.mult),
              reads=RPh + [RPall], writes=[RPall])
            A("dve", lambda e: e.tensor_tensor(out=GG, in0=PSTh, in1=GG, op=ALU.mult), reads=[RPall] + RGG, writes=RGG + RPh)
            A("sp", lambda e, h=h: e.dma_start(out=ysc_v[:, :, 512 + h * 128:512 + (h + 1) * 128], in_=GG), reads=RGG,
              writes=[], dma="ysth")

        reset_arenas()
        G2 = fa(1024)

        Dg = fa(1024).rearrange("p (k n) -> p k n", n=128)
        RDg = Res()

        def build_G(Gdst, j0, RGd):
            for k in range(8):
                A("dve", lambda e, k=k: e.tensor_scalar_mul(out=Dg[:, k, :], in0=identf, scalar1=modT[:, j0 + k, 0:1]),
                  reads=[Rprm, Rc], writes=[RDg])
            for half in range(2):
                A("pe", lambda e, half=half: e.matmul(pbs[half][:, 0:512], lhsT=onesf,
                                                      rhs=Dg[:, 4 * half:4 * half + 4, :], start=True, stop=True),
                  reads=[RDg, Rc], writes=[Rpb[half]])
                A("act", lambda e, half=half: e.activation(out=Gdst[:, half * 512:(half + 1) * 512], in_=pbs[half][:, 0:512],
                                                           func=AF.Copy), reads=[Rpb[half]], writes=[RGd])

        wd = ba(22 * 1024).rearrange("p (k n) -> p k n", n=1024)
        Rwd3 = Res()
        G1 = fa(1024)
        RG1 = Res()
        build_G(G1, 16, RG1)
        RG2 = Res()
        build_G(G2, 40, RG2)
        wst4 = [fa(1024) for _ in range(2)]
        Rwst4 = [Res() for _ in range(2)]
        wo = ba(8 * 1024).rearrange("p (k n) -> p k n", n=1024)
        Rwo = Res()
        nwTt = fa(8)
        RnwT = Res()
        A("sp", lambda e: e.dma_start(out=nwTt, in_=D["nwT"][:, :]), writes=[RnwT], dma="nwT")
        wst3 = wst4[0]
        Rwst3 = Rwst4[0]
        for k in range(8):
            A("sp", lambda e, k=k: e.dma_start(out=wst3, in_=D["w_out"][k * 128:(k + 1) * 128, :]), writes=[Rwst3], dma="wst3")
            A("dve", lambda e, k=k: e.scalar_tensor_tensor(out=wo[:, k, :], in0=wst3, scalar=nwTt[:, k:k + 1], in1=G1,
                                                           op0=ALU.mult, op1=ALU.mult),
              reads=[Rwst3, RG1, RnwT], writes=[Rwo])
        xts3 = [fa(1024) for _ in range(3)]
        Rxt3 = [Res() for _ in range(3)]
        junk3 = fa(1024)
        Rjunk3 = Res()
        ssb3 = fa(8)
        Rss3 = [Res() for _ in range(4)]
        xnb3 = [ba(1024) for _ in range(2)]
        Rxn3 = [Res() for _ in range(2)]
        ytl = [ba(1024) for _ in range(2)]
        Ryt = [Res() for _ in range(2)]
        yTt = [ba(1024).rearrange("p (k t) -> p k t", t=128) for _ in range(2)]
        RyT = [Res() for _ in range(2)]
        Rx1 = [Res() for _ in range(32)]
        modtmp3 = [fa(1024) for _ in range(2)]
        Rmt3 = [Res() for _ in range(2)]

        def p3_A1(t):
            s = t % 2
            x3 = t % 3
            if t < 22:
                A("pool", lambda e: e.dma_start(out=wd[:, t, :], in_=D["w_down"][t * 128:(t + 1) * 128, :]), writes=[Rwd3],
                  dma="wdld")
            A("sp", lambda e: e.dma_start(out=ytl[s], in_=ysc[t * 128:(t + 1) * 128, :]), writes=[Ryt[s]], dma=f"yt{s}")
            A("sp", lambda e: e.dma_start(out=xts3[x3], in_=D["xin"][256 + t * 128:256 + (t + 1) * 128, :]),
              writes=[Rxt3[x3]], dma=f"x1t{x3}")

            def try_(e):
                for k in range(8):
                    ins = e.transpose(out=pts[0][:, k * 128:(k + 1) * 128], in_=ytl[s][:, k * 128:(k + 1) * 128], identity=identb[:])
                return ins
            A("pe", try_, reads=[Ryt[s], Rc], writes=[Rpt[0]])
            A("act", lambda e: e.activation(out=yTt[s].rearrange("p k t -> p (k t)"), in_=pts[0][:, :], func=AF.Copy),
              reads=[Rpt[0]], writes=[RyT[s]])

        def p3_A2(t):
            s = t % 2
            x3 = t % 3
            for half in range(2):
                bank = 2 * s + half

                def mmo(e, half=half, bank=bank):
                    for k in range(8):
                        ins = e.matmul(pbs[bank][:, 0:512], lhsT=yTt[s][:, k, :], rhs=wo[:, k, half * 512:(half + 1) * 512],
                                       start=(k == 0), stop=(k == 7))
                    return ins
                A("pe", mmo, reads=[RyT[s], Rwo], writes=[Rpb[bank]])
                A("dve", lambda e, half=half, bank=bank: e.tensor_tensor(
                    out=xts3[x3][:, half * 512:(half + 1) * 512], in0=xts3[x3][:, half * 512:(half + 1) * 512], in1=pbs[bank][:, 0:512],
                    op=ALU.add), reads=[Rpb[bank], Rxt3[x3]], writes=[Rxt3[x3]])
            A("pool", lambda e: e.dma_start(out=x1sc[t * 128:(t + 1) * 128, :], in_=xts3[x3]), reads=[Rxt3[x3]], writes=[Rx1[t]],
              dma=f"x1w{x3}")
            s4 = t % 4
            ssv = ssb3[:, s4:s4 + 1]
            A("act", lambda e: e.activation(out=junk3, in_=xts3[x3], func=AF.Square, scale=1.0 / 32.0, accum_out=ssv),
              reads=[Rxt3[x3]], writes=[Rjunk3, Rss3[s4]])
            A("act", lambda e: e.activation(out=ssv, in_=ssv, func=AF.Ln, bias=EPS), reads=[Rss3[s4]], writes=[Rss3[s4]])
            A("act", lambda e: e.activation(out=ssv, in_=ssv, func=AF.Exp, scale=-0.5), reads=[Rss3[s4]], writes=[Rss3[s4]])

        def p3_A2b(t):
            s = t % 2
            x3 = t % 3
            s4 = t % 4
            ssv = ssb3[:, s4:s4 + 1]
            A("dve", lambda e: e.tensor_scalar_mul(out=xnb3[s], in0=xts3[x3], scalar1=ssv),
              reads=[Rss3[s4], Rxt3[x3]], writes=[Rxn3[s]])

        def p3_B(t):
            s = t % 2

            def tr2(e):
                for k in range(8):
                    ins = e.transpose(out=pts[1][:, k * 128:(k + 1) * 128], in_=xnb3[s][:, k * 128:(k + 1) * 128], identity=identb[:])
                return ins
            A("pe", tr2, reads=[Rxn3[s], Rc], writes=[Rpt[1]])
            A("dve", lambda e: e.tensor_tensor(out=modtmp3[s].rearrange("p (k c) -> p k c", k=8),
                                               in0=pts[1][:, :].rearrange("p (k c) -> p k c", k=8),
                                               in1=prm[:, 4, :].unsqueeze(2).to_broadcast([128, 8, 128]), op=ALU.mult),
              reads=[Rpt[1], Rprm], writes=[Rmt3[s]])
            A("dve", lambda e: e.tensor_tensor(out=xT[:, :, (t + 2) * 128:(t + 3) * 128],
                                               in0=modtmp3[s].rearrange("p (k c) -> p k c", k=8),
                                               in1=prm[:, 5, :].unsqueeze(2).to_broadcast([128, 8, 128]), op=ALU.add),
              reads=[Rmt3[s], Rprm], writes=[RxT[t + 2]])

        for i in range(34):
            if i < 32:
                p3_A1(i)
            if 0 <= i - 1 < 32:
                p3_A2(i - 1)
            if 0 <= i - 2 < 32:
                p3_B(i - 2)
            if 0 <= i - 1 < 32:
                p3_A2b(i - 1)

        reset_arenas()
        G2 = fa(1024)
        gtmp = [fa(512) for _ in range(2)]
        Rgt = [Res() for _ in range(2)]
        wd = ba(22 * 1024).rearrange("p (k n) -> p k n", n=1024)
        Rwd = Res()
        HT = ba(22 * 512).rearrange("p (j t) -> p j t", t=512)
        RHT = Res()
        UP = [ba(10 * 66).rearrange("p (r c) -> p r c", c=66) for _ in range(2)]
        RUP = [Res() for _ in range(2)]
        WU = [ba(8 * 256).rearrange("p (k n) -> p k n", n=256) for _ in range(2)]
        RWU = [Res() for _ in range(2)]
        DGf = [ba(18 * 128).rearrange("p (w t n) -> p w t n", w=2, n=128) for _ in range(2)]
        RDGf = [Res() for _ in range(2)]
        fcwt = fa(44 * 9).rearrange("p (c t) -> p c t", t=9)
        fnwt = fa(1024)
        Rfc = Res()
        sat = fa(512)
        Rsa = Res()
        x1t = [fa(1024) for _ in range(4)]
        Rx1t = [Res() for _ in range(4)]
        HALO = fa(2816).bitcast(BF16).rearrange("p (j a r c) -> p j a r c", j=22, a=2, r=2)
        RHALO = [[Res() for _ in range(2)] for _ in range(22)]
        junk4 = fa(1024)
        Rjunk4 = Res()
        ssb4 = fa(8)
        Rss4 = [Res() for _ in range(4)]
        A("sp", lambda e: e.dma_start(out=fcwt, in_=D["fcw"][:, :, :]), writes=[Rfc], dma="fcw")
        A("sp", lambda e: e.dma_start(out=fnwt, in_=D["fnw"][:, :]), writes=[Rfc], dma="fnw")
        for ab in range(2):
            A("pool", lambda e, ab=ab: e.memset(UP[ab], 0.0), writes=[RUP[ab]])
        w_up_v = D["w_up"].rearrange("(k p) n -> p k n", p=128)
        it = 0
        NB = 8
        for blk in range(NB):
            R0 = 8 * blk
            if blk == NB - 1:
                for ab in range(2):
                    A("pool", lambda e, ab=ab: e.memset(UP[ab][:, 9, :], 0.0), writes=[RUP[ab]])
            for tt in range(4):
                gt = blk * 4 + tt
                A("sp", lambda e, gt=gt, tt=tt: e.dma_start(out=x1t[tt], in_=x1sc[gt * 128:(gt + 1) * 128, :]), reads=[Rx1[gt]],
                  writes=[Rx1t[tt]], dma=f"x1r{tt}")
            for j in range(22):
                ws = it % 2
                it += 1
                if blk > 0:
                    for ab in range(2):
                        A("dve", lambda e, ab=ab, j=j: e.tensor_copy(out=UP[ab][:, 0:2, 1:65], in_=HALO[:, j, ab]),
                          reads=[RHALO[j][ab]], writes=[RUP[ab]])
                for ab in range(2):
                    c0 = ab * 2816 + j * 128
                    A("pool", lambda e, ws=ws, ab=ab, c0=c0: e.dma_start(out=WU[ws][:, :, ab * 128:(ab + 1) * 128],
                                                                          in_=w_up_v[:, :, c0:c0 + 128]),
                      writes=[RWU[ws]], dma=f"wu{ws}{ab}")
                    ch = ab * 22 + j
                    A("dve", lambda e, ws=ws, ab=ab, ch=ch: e.tensor_tensor(
                        out=DGf[ws][:, ab], in0=identb[:].unsqueeze(1).to_broadcast([128, 9, 128]),
                        in1=fcwt[:, ch, :].unsqueeze(2).to_broadcast([128, 9, 128]), op=ALU.mult),
                      reads=[Rfc, Rc], writes=[RDGf[ws]])
                if blk == 0:
                    groups = ((1, 8), (9, 1))
                elif blk == NB - 1:
                    groups = ((2, 7),)
                else:
                    groups = ((2, 8),)
                for ab in range(2):
                    for gi, (lo, nrow) in enumerate(groups):
                        tok0 = 256 + (R0 - 1 + lo) * 64
                        ntk = nrow * 64
                        bank = (ab + gi) % 2
                        tiles_needed = [RxT[tt] for tt in range(tok0 // 128, (tok0 + ntk + 127) // 128)]

                        def mmu(e, ws=ws, ab=ab, tok0=tok0, ntk=ntk, bank=bank):
                            for k in range(8):
                                ins = e.matmul(pbs[bank][:, 0:ntk], lhsT=WU[ws][:, k, ab * 128:(ab + 1) * 128],
                                               rhs=xT[:, k, tok0:tok0 + ntk], start=(k == 0), stop=(k == 7))
                            return ins
                        A("pe", mmu, reads=tiles_needed + [RWU[ws]], writes=[Rpb[bank]])
                        A("act", lambda e, ab=ab, lo=lo, nrow=nrow, ntk=ntk, bank=bank: e.activation(
                            out=UP[ab][:, lo:lo + nrow, 1:65], in_=pbs[bank][:, 0:ntk].rearrange("p (r c) -> p r c", c=64),
                            func=AF.Copy), reads=[Rpb[bank]], writes=[RUP[ab]])
                    if blk < NB - 1:
                        A("dve", lambda e, ab=ab, j=j: e.tensor_copy(out=HALO[:, j, ab], in_=UP[ab][:, 8:10, 1:65]),
                          reads=[RUP[ab]], writes=[RHALO[j][ab]])
                for ab in range(2):
                    bank = 2 + ab

                    def mmcf(e, ws=ws, ab=ab, bank=bank):
                        for tap in range(9):
                            dr, dc = tap // 3, tap % 3
                            ins = e.matmul(pbs[bank][:, 0:512].rearrange("p (r c) -> p r c", c=64),
                                           lhsT=DGf[ws][:, ab, tap, :],
                                           rhs=UP[ab][:, dr:dr + 8, dc:dc + 64],
                                           start=(tap == 0), stop=(tap == 8))
                        return ins
                    A("pe", mmcf, reads=[RUP[ab], RDGf[ws]], writes=[Rpb[bank]])
                A("act", lambda e: e.activation(out=sat, in_=pbs[2][:, 0:512], func=AF.Silu), reads=[Rpb[2]], writes=[Rsa])
                A("dve", lambda e, j=j: e.tensor_tensor(out=HT[:, j, :], in0=sat, in1=pbs[3][:, 0:512], op=ALU.mult),
                  reads=[Rsa, Rpb[3]], writes=[RHT])
            for tt in range(4):
                gt = blk * 4 + tt
                s = tt
                for half in range(2):
                    bank = 4 + half

                    def mmd(e, tt=tt, half=half, bank=bank):
                        for j in range(22):
                            ins = e.matmul(pbs[bank][:, 0:512], lhsT=HT[:, j, tt * 128:(tt + 1) * 128],
                                           rhs=wd[:, j, half * 512:(half + 1) * 512], start=(j == 0), stop=(j == 21))
                        return ins
                    A("pe", mmd, reads=[RHT, Rwd], writes=[Rpb[bank]])
                    A("dve", lambda e, half=half, bank=bank: e.tensor_tensor(
                        out=gtmp[half], in0=pbs[bank][:, 0:512], in1=G2[:, half * 512:(half + 1) * 512], op=ALU.mult),
                      reads=[Rpb[bank]], writes=[Rgt[half]])
                    A("dve", lambda e, s=s, half=half: e.tensor_tensor(
                        out=x1t[s][:, half * 512:(half + 1) * 512], in0=x1t[s][:, half * 512:(half + 1) * 512],
                        in1=gtmp[half], op=ALU.add), reads=[Rgt[half], Rx1t[s]], writes=[Rx1t[s]])
                s4 = gt % 4
                ssv = ssb4[:, s4:s4 + 1]
                A("act", lambda e, s=s, ssv=ssv: e.activation(out=junk4, in_=x1t[s], func=AF.Square, scale=1.0 / 32.0, accum_out=ssv),
                  reads=[Rx1t[s]], writes=[Rjunk4, Rss4[s4]])
                A("act", lambda e, ssv=ssv: e.activation(out=ssv, in_=ssv, func=AF.Ln, bias=EPS), reads=[Rss4[s4]], writes=[Rss4[s4]])
                A("act", lambda e, ssv=ssv: e.activation(out=ssv, in_=ssv, func=AF.Exp, scale=-0.5), reads=[Rss4[s4]], writes=[Rss4[s4]])
                A("dve", lambda e, s=s, ssv=ssv: e.scalar_tensor_tensor(out=x1t[s], in0=x1t[s], scalar=ssv, in1=fnwt,
                                                                        op0=ALU.mult, op1=ALU.mult),
                  reads=[Rss4[s4], Rx1t[s], Rfc], writes=[Rx1t[s]])
                A("sp", lambda e, gt=gt, s=s: e.dma_start(out=out[gt * 128:(gt + 1) * 128, :], in_=x1t[s]), reads=[Rx1t[s]], writes=[],
                  dma=f"out{s}")
        P.barrier()
        P.emit(st)
    return nc


def _consts():
    u = np.arange(128)[:, None]
    t = np.arange(128)[None, :]
    cf = np.zeros((8, 128, 128), np.float32)
    cf[0] = np.eye(128)
    cf[1] = 1.0
    cf[2] = (u <= t)
    cf[3] = (u >= t)
    cf[4] = (u > t)
    cf[5] = (u < t)
    cf[6] = (u <= t).astype(np.float32) - (u <= 63).astype(np.float32)
    cf[7] = (u >= t).astype(np.float32) - (u >= 64).astype(np.float32)
    cwd = np.zeros((128, 4), np.float32)
    uu = np.arange(128)
    cwd[:, 0] = uu > 63
    cwd[:, 1] = uu <= 63
    cwd[:, 2] = uu < 64
    cwd[:, 3] = uu >= 64
    return np.ascontiguousarray(cf.transpose(1, 0, 2)), cwd


_NC_CACHE = {}


def kernel(x, c, ctx, c_ctx, w_mod, b_mod, norm1_w, w_in, mlstm_gate_b, mlstm_conv_w, mlstm_norm_w,
           hgrn_lb_logits, hgrn_norm_w, w_out, norm2_w, w_up, ffn_conv_w, w_down, final_norm_w):
    f = lambda a: np.ascontiguousarray(np.asarray(a, dtype=np.float32))
    x, c, ctx, c_ctx = f(x), f(c), f(ctx), f(c_ctx)
    cf, cwd = _consts()
    shared = {
        "w_mod": f(w_mod[0]),
        "b_modT": f(np.asarray(b_mod[0]).reshape(48, 128).T),
        "n1T": f(np.asarray(norm1_w[0]).reshape(8, 128).T),
        "n2T": f(np.asarray(norm2_w[0]).reshape(8, 128).T),
        "fnw": f(np.broadcast_to(np.asarray(final_norm_w)[None, :], (128, 1024))),
        "w_in": f(w_in[0]),
        "gate_b": f(np.broadcast_to(np.asarray(mlstm_gate_b[0])[None, :], (128, 16))),
        "mcw": f(np.asarray(mlstm_conv_w[0]).reshape(9, 8, 128).transpose(2, 1, 0)),
        "fcw": f(np.asarray(ffn_conv_w[0]).reshape(9, 44, 128).transpose(2, 1, 0)),
        "nwT": f(np.concatenate([np.asarray(mlstm_norm_w[0]), np.asarray(hgrn_norm_w[0])]).reshape(8, 128).T),
        "lbl": f(np.broadcast_to(np.asarray(hgrn_lb_logits)[None], (128, 2, 2, 512))),
        "w_out": f(w_out[0]),
        "w_up": f(w_up[0]),
        "w_down": f(w_down[0]),
        "cf32": cf,
        "cwd": cwd,
        "identb": np.eye(128).astype(ml_dtypes.bfloat16),
    }
    in_maps = []
    for b in range(8):
        m = dict(shared)
        m["xin"] = np.ascontiguousarray(np.concatenate([ctx[b], x[b]], axis=0))
        m["c2"] = np.ascontiguousarray(np.stack([c[b], c_ctx], axis=1).reshape(8, 128, 2).transpose(1, 0, 2))
        in_maps.append(m)
    if "nc" not in _NC_CACHE:
        _NC_CACHE["nc"] = build_program()
    res = run_bass_kernel_spmd(_NC_CACHE["nc"], in_maps, core_ids=list(range(8)))
    return np.stack([np.asarray(r["out"], dtype=np.float32) for r in res.results], axis=0)
```

```python
import math
from contextlib import ExitStack

import numpy as np
import ml_dtypes
import concourse.bass as bass
import concourse.mybir as mybir
from concourse.bass_utils import run_bass_kernel_spmd

F32 = mybir.dt.float32
BF16 = mybir.dt.bfloat16
AF = mybir.ActivationFunctionType
ALU = mybir.AluOpType
AX = mybir.AxisListType

ENGS = ("pe", "act", "dve", "pool", "sp")
SIG_ROT = 6000

NT = 34
NTOK = 4352
EPS = 1e-6
LN_SQRT128 = 0.5 * math.log(128.0)
QS = 128.0 ** -0.5


class Res:
    __slots__ = ("name", "last_w", "rd_eng", "rd_dma")

    def __init__(self, name=""):
        self.name = name
        self.last_w = None
        self.rd_eng = {}
        self.rd_dma = []


class Op:
    __slots__ = ("eng", "fn", "deps", "is_dma", "dkey", "dval", "sig", "nsig")

    def __init__(self, eng, fn, is_dma):
        self.eng = eng
        self.fn = fn
        self.deps = []
        self.is_dma = is_dma
        self.dkey = None
        self.dval = 0
        self.sig = None
        self.nsig = False


class Prog:
    def __init__(self, nc):
        self.nc = nc
        self.ops = {e: [] for e in ENGS}
        self.dma_cnt = {}
        self.last_dma = {}

    def add(self, eng, fn, reads=(), writes=(), dma=None):
        op = Op(eng, fn, dma is not None)
        deps = {}
        for r in reads:
            if r.last_w is not None:
                deps[id(r.last_w)] = (r.last_w, True)
        for w in writes:
            if w.last_w is not None and id(w.last_w) not in deps:
                deps[id(w.last_w)] = (w.last_w, False)
            for rd in list(w.rd_eng.values()) + w.rd_dma:
                if id(rd) not in deps:
                    deps[id(rd)] = (rd, False)
        for (p, raw) in deps.values():
            if p.eng == eng and not p.is_dma:
                if eng == "pe" or not raw:
                    continue
            op.deps.append(p)
            p.nsig = True
        for r in reads:
            if op.is_dma:
                r.rd_dma.append(op)
            else:
                r.rd_eng[eng] = op
        for w in writes:
            w.last_w = op
            w.rd_eng = {}
            w.rd_dma = []
        if dma is not None:
            c = self.dma_cnt.get(dma, 0) + 1
            self.dma_cnt[dma] = c
            op.dkey = dma
            op.dval = 16 * c
            self.last_dma[dma] = op
        self.ops[eng].append(op)
        return op

    def barrier(self):
        lasts = []
        for e in ENGS:
            for op in reversed(self.ops[e]):
                if not op.is_dma and op.fn is not None:
                    lasts.append(op)
                    break
        lasts += list(self.last_dma.values())
        for e in ENGS:
            op = Op(e, None, False)
            for p in lasts:
                if p.eng == e and not p.is_dma:
                    continue
                op.deps.append(p)
                p.nsig = True
            self.ops[e].append(op)

    def emit(self, stack):
        nc = self.nc
        nsems = {}
        for e in ENGS:
            cnt = 0
            for op in self.ops[e]:
                if op.is_dma or not op.nsig:
                    continue
                op.sig = (cnt // SIG_ROT, cnt % SIG_ROT + 1)
                cnt += 1
            nsems[e] = (cnt + SIG_ROT - 1) // SIG_ROT
        sems = {}
        for e in ENGS:
            for s in range(nsems[e]):
                sems[(e, s)] = stack.enter_context(nc.semaphore(f"s_{e}_{s}"))
        dsems = {}
        for i, k in enumerate(self.dma_cnt.keys()):
            dsems[k] = stack.enter_context(nc.semaphore(f"d_{i}"))
        block = stack.enter_context(nc.Block())
        engobj = {"pe": block.tensor, "act": block.scalar, "dve": block.vector,
                  "pool": block.gpsimd, "sp": block.sync}

        def make(e):
            def body(eng):
                waited = {}
                for op in self.ops[e]:
                    for p in op.deps:
                        if p.is_dma:
                            key = ("d", p.dkey)
                            sem = dsems[p.dkey]
                            val = p.dval
                        else:
                            key = (p.eng, p.sig[0])
                            sem = sems[key]
                            val = p.sig[1]
                        if waited.get(key, 0) >= val:
                            continue
                        waited[key] = val
                        eng.wait_ge(sem, val)
                    if op.fn is None:
                        continue
                    ins = op.fn(eng)
                    if op.is_dma:
                        ins.then_inc(dsems[op.dkey], 16)
                    elif op.nsig:
                        ins.then_inc(sems[(e, op.sig[0])], 1)
            return body

        for e in ENGS:
            engobj[e](make(e))


def build_program():
    nc = bass.Bass("TRN2", target_bir_lowering=False)
    D = {}

    def din(name, shape, dt=F32):
        D[name] = nc.dram_tensor(name, list(shape), dt, kind="ExternalInput").ap()

    din("xin", [NTOK, 1024])
    din("c2", [128, 8, 2])
    din("w_mod", [1024, 6144])
    din("b_modT", [128, 48])
    din("n1T", [128, 8])
    din("n2T", [128, 8])
    din("fnw", [128, 1024])
    din("w_in", [1024, 4624])
    din("gate_b", [128, 16])
    din("mcw", [128, 8, 9])
    din("fcw", [128, 44, 9])
    din("nwT", [128, 8])
    din("lbl", [128, 2, 2, 512])
    din("w_out", [1024, 1024])
    din("w_up", [1024, 5632])
    din("w_down", [2816, 1024])
    din("cf32", [128, 8, 128])
    din("cwd", [128, 4])
    din("identb", [128, 128], BF16)
    out = nc.dram_tensor("out", [4096, 1024], F32, kind="ExternalOutput").ap()
    ysc = nc.dram_tensor("ysc", [4096, 1024], BF16, kind="Internal").ap()
    x1sc = nc.dram_tensor("x1sc", [4096, 1024], F32, kind="Internal").ap()

    with ExitStack() as st:
        P = Prog(nc)
        A = P.add

        def T(name, shape, dt):
            return st.enter_context(nc.sbuf_tensor(name, list(shape), dt))

        xT = T("xT", [128, 8, NTOK], BF16)
        RxT = [Res(f"xT{t}") for t in range(NT)]
        cf = T("cf", [128, 8, 128], F32)
        identf, onesf, maskf, maskb, SU, SL, Mdf, Mdb = [cf[:, i, :] for i in range(8)]
        cwd = T("cwd_sb", [128, 4], F32)
        identb = T("identb_sb", [128, 128], BF16)
        Rc = Res("consts")
        modT = T("modT", [128, 48, 2], F32)
        prm = T("prm", [128, 6, 8], F32)
        Rprm = Res("prm")
        n12 = T("n12", [128, 2, 8], F32)
        bmT = T("bmT", [128, 48], F32)
        FA = T("FA", [128, 12400], F32)
        BA = T("BA", [128, 44000], BF16)
        pbs = [st.enter_context(nc.psum_tensor(f"pb{i}", [128, 512], F32)) for i in range(8)]
        pts = [pbs[6 + i][:, :].bitcast(BF16) for i in range(2)]
        Rpb = [Res(f"pb{i}") for i in range(8)]
        Rpt = [Rpb[6], Rpb[7]]

        fa_off = [0]
        ba_off = [0]

        def fa(n, shape=None):
            o = fa_off[0]
            fa_off[0] += n
            assert fa_off[0] <= 12400, fa_off[0]
            v = FA[:, o:o + n]
            return v

        def ba(n):
            o = ba_off[0]
            ba_off[0] += n
            assert ba_off[0] <= 44000, ba_off[0]
            return BA[:, o:o + n]

        def reset_arenas():
            P.barrier()
            fa_off[0] = 0
            ba_off[0] = 0

        A("sp", lambda e: e.dma_start(out=cf[:], in_=D["cf32"][:, :, :]), writes=[Rc], dma="c0")
        A("sp", lambda e: e.dma_start(out=cwd[:], in_=D["cwd"][:, :]), writes=[Rc], dma="c1")
        A("sp", lambda e: e.dma_start(out=identb[:], in_=D["identb"][:, :]), writes=[Rc], dma="c2")
        A("sp", lambda e: e.dma_start(out=n12[:, 0, :], in_=D["n1T"][:, :]), writes=[Rprm], dma="c3")
        A("sp", lambda e: e.dma_start(out=n12[:, 1, :], in_=D["n2T"][:, :]), writes=[Rprm], dma="c4")
        A("sp", lambda e: e.dma_start(out=bmT[:], in_=D["b_modT"][:, :]), writes=[Rprm], dma="c5")

        c2t = fa(16).rearrange("p (k m) -> p k m", m=2)
        sct = fa(16).rearrange("p (k m) -> p k m", m=2)
        tmpc = fa(16).rearrange("p (k m) -> p k m", m=2)
        Rc2 = Res()
        A("sp", lambda e: e.dma_start(out=c2t, in_=D["c2"][:, :, :]), writes=[Rc2], dma="c6")
        A("act", lambda e: e.activation(out=tmpc, in_=c2t, func=AF.Exp, scale=-1.0), reads=[Rc2], writes=[Rc2])
        A("dve", lambda e: e.tensor_scalar_add(out=tmpc, in0=tmpc, scalar1=1.0), reads=[Rc2], writes=[Rc2])
        A("dve", lambda e: e.reciprocal(out=tmpc, in_=tmpc), reads=[Rc2], writes=[Rc2])
        A("dve", lambda e: e.tensor_tensor(out=sct, in0=c2t, in1=tmpc, op=ALU.mult), reads=[Rc2], writes=[Rc2])
        wm = [ba(4096).rearrange("p (k n) -> p k n", n=512) for _ in range(3)]
        Rwm = [Res() for _ in range(3)]
        sctb = ba(16).rearrange("p (k m) -> p k m", m=2)
        A("dve", lambda e: e.tensor_copy(out=sctb, in_=sct), reads=[Rc2], writes=[Rc2])
        w_mod_v = D["w_mod"].rearrange("(k p) n -> p k n", p=128)
        for jj in range(12):
            s = jj % 3
            A("pool", lambda e, jj=jj, s=s: e.dma_start(out=wm[s], in_=w_mod_v[:, :, jj * 512:(jj + 1) * 512]),
              writes=[Rwm[s]], dma=f"wm{s}")

            def mm(e, jj=jj, s=s):
                for q in range(4):
                    j = jj * 4 + q
                    for k in range(8):
                        ins = e.matmul(pbs[0][:, 2 * j:2 * j + 2], lhsT=wm[s][:, k, q * 128:(q + 1) * 128], rhs=sctb[:, k, :],
                                       start=(k == 0), stop=(k == 7))
                return ins
            A("pe", mm, reads=[Rwm[s], Rc2], writes=[Rpb[0]])
        A("dve", lambda e: e.tensor_tensor(out=modT[:], in0=pbs[0][:, 0:96].rearrange("p (j m) -> p j m", m=2),
                                           in1=bmT[:].unsqueeze(2).to_broadcast([128, 48, 2]), op=ALU.add),
          reads=[Rpb[0], Rprm], writes=[Rprm])
        for (pi, nidx, scj, col) in ((0, 0, 8, 0), (2, 0, 8, 1), (4, 1, 32, 0)):
            A("dve", lambda e, pi=pi, nidx=nidx, scj=scj, col=col: e.scalar_tensor_tensor(
                out=prm[:, pi, :], in0=modT[:, scj:scj + 8, col], scalar=1.0, in1=n12[:, nidx, :],
                op0=ALU.add, op1=ALU.mult), reads=[Rprm], writes=[Rprm])
        for (pi, shj, col) in ((1, 0, 0), (3, 0, 1), (5, 24, 0)):
            A("dve", lambda e, pi=pi, shj=shj, col=col: e.tensor_copy(out=prm[:, pi, :], in_=modT[:, shj:shj + 8, col]),
              reads=[Rprm], writes=[Rprm])

        xts = [fa(1024) for _ in range(3)]
        Rxt = [Res() for _ in range(3)]
        junk = fa(1024)
        Rjunk = Res()
        ssb = fa(8)
        Rss = [Res() for _ in range(4)]
        xnb = [ba(1024) for _ in range(2)]
        Rxn = [Res() for _ in range(2)]
        modtmp = [fa(1024) for _ in range(2)]
        Rmt = [Res() for _ in range(2)]

        def p1_stageA(t):
            s = t % 3
            s4 = t % 4
            s2 = t % 2
            ssv = ssb[:, s4:s4 + 1]
            A("sp", lambda e: e.dma_start(out=xts[s], in_=D["xin"][t * 128:(t + 1) * 128, :]), writes=[Rxt[s]], dma=f"xt{s}")
            A("act", lambda e: e.activation(out=junk, in_=xts[s], func=AF.Square, scale=1.0 / 32.0, accum_out=ssv),
              reads=[Rxt[s]], writes=[Rjunk, Rss[s4]])
            A("act", lambda e: e.activation(out=ssv, in_=ssv, func=AF.Ln, bias=EPS), reads=[Rss[s4]], writes=[Rss[s4]])
            A("act", lambda e: e.activation(out=ssv, in_=ssv, func=AF.Exp, scale=-0.5), reads=[Rss[s4]], writes=[Rss[s4]])
            A("dve", lambda e: e.tensor_scalar_mul(out=xnb[s2], in0=xts[s], scalar1=ssv),
              reads=[Rss[s4], Rxt[s]], writes=[Rxn[s2]])

        def p1_stageB(t):
            s2 = t % 2
            pa, psh = (2, 3) if t < 2 else (0, 1)

            def tr(e):
                for k in range(8):
                    ins = e.transpose(out=pts[s2][:, k * 128:(k + 1) * 128], in_=xnb[s2][:, k * 128:(k + 1) * 128],
                                      identity=identb[:])
                return ins
            A("pe", tr, reads=[Rxn[s2], Rc], writes=[Rpt[s2]])

            A("dve", lambda e: e.tensor_tensor(out=modtmp[s2].rearrange("p (k c) -> p k c", k=8),
                                               in0=pts[s2][:, :].rearrange("p (k c) -> p k c", k=8),
                                               in1=prm[:, pa, :].unsqueeze(2).to_broadcast([128, 8, 128]), op=ALU.mult),
              reads=[Rpt[s2], Rprm], writes=[Rmt[s2]])
            A("dve", lambda e: e.tensor_tensor(out=xT[:, :, t * 128:(t + 1) * 128],
                                               in0=modtmp[s2].rearrange("p (k c) -> p k c", k=8),
                                               in1=prm[:, psh, :].unsqueeze(2).to_broadcast([128, 8, 128]), op=ALU.add),
              reads=[Rmt[s2], Rprm], writes=[RxT[t]])

        for i in range(NT + 1):
            if i < NT:
                p1_stageA(i)
            if i >= 1:
                p1_stageB(i - 1)

        reset_arenas()
        WK = fa(NT * 8).rearrange("p (t g) -> p t g", g=8)
        FL = fa(NT * 8).rearrange("p (t g) -> p t g", g=8)
        DC = fa(NT * 8).rearrange("p (t g) -> p t g", g=8)
        Rgs = Res("gatescal")
        gbt = fa(16)
        Rgb = Res()
        A("sp", lambda e: e.dma_start(out=gbt, in_=D["gate_b"][:, :]), writes=[Rgb], dma="gb")
        wg = ba(128).rearrange("p (k n) -> p k n", n=16)
        Rwg = Res()
        w_in_v = D["w_in"].rearrange("(k p) n -> p k n", p=128)
        A("pool", lambda e: e.dma_start(out=wg, in_=w_in_v[:, :, 2048:2064]), writes=[Rwg], dma="wg")
        gps = [fa(16) for _ in range(2)]
        nls = [fa(16) for _ in range(2)]
        tm8 = [fa(8) for _ in range(2)]
        Rgp = [Res() for _ in range(2)]
        def p2a_A(t):
            s = t % 2
            b0 = 0 if s == 0 else 2

            def mmg(e):
                for k in range(8):
                    ins = e.matmul(pbs[b0][:, 0:16], lhsT=xT[:, k, t * 128:(t + 1) * 128], rhs=wg[:, k, :],
                                   start=(k == 0), stop=(k == 7))
                return ins
            A("pe", mmg, reads=[RxT[t], Rwg], writes=[Rpb[b0]])
            A("dve", lambda e: e.tensor_tensor(out=gps[s], in0=pbs[b0][:, 0:16], in1=gbt, op=ALU.add),
              reads=[Rpb[b0], Rgb], writes=[Rgp[s]])
            A("act", lambda e: e.activation(out=nls[s], in_=gps[s], func=AF.Exp, scale=-1.0),
              reads=[Rgp[s]], writes=[Rnl[s]])
            A("act", lambda e: e.activation(out=nls[s], in_=nls[s], func=AF.Ln, bias=1.0),
              reads=[Rnl[s]], writes=[Rnl[s]])

        def p2a_B(t):
            s = t % 2
            b1 = 1 if s == 0 else 3

            def mmc(e):
                e.matmul(pbs[b1][:, 0:16], lhsT=SU, rhs=nls[s], start=True, stop=True)
                e.matmul(pbs[b1][:, 16:32], lhsT=SL, rhs=nls[s], start=True, stop=True)
                return e.matmul(pbs[b1][:, 32:48], lhsT=onesf, rhs=nls[s], start=True, stop=True)
            A("pe", mmc, reads=[Rnl[s], Rc], writes=[Rpb[b1]])
            A("dve", lambda e: e.tensor_tensor(out=tm8[s][:, 0:4], in0=gps[s][:, 0:4], in1=pbs[b1][:, 4:8],
                                               op=ALU.subtract), reads=[Rgp[s], Rpb[b1]], writes=[Rtm[s]])
            A("dve", lambda e: e.tensor_tensor(out=tm8[s][:, 4:8], in0=gps[s][:, 8:12], in1=pbs[b1][:, 28:32],
                                               op=ALU.subtract), reads=[Rgp[s], Rpb[b1]], writes=[Rtm[s]])
            A("act", lambda e: e.activation(out=WK[:, t, :], in_=tm8[s], func=AF.Exp),
              reads=[Rtm[s]], writes=[Rgs])
            for (dst, c0, c1, bias) in ((FL, 4, 0, LN_SQRT128), (FL, 28, 4, LN_SQRT128), (DC, 36, 0, 0.0), (DC, 44, 4, 0.0)):
                A("act", lambda e, dst=dst, c0=c0, c1=c1, bias=bias: e.activation(
                    out=dst[:, t, c1:c1 + 4], in_=pbs[b1][:, c0:c0 + 4], func=AF.Exp, scale=-1.0, bias=bias),
                  reads=[Rpb[b1]], writes=[Rgs])

        Rnl = [Res() for _ in range(2)]
        Rtm = [Res() for _ in range(2)]
        for i in range(NT + 1):
            if i < NT:
                p2a_A(i)
            if i >= 1:
                p2a_B(i - 1)

        RING = 4
        LA = 2

        def alloc_scan(Nv):
            return {
                "Z": [[fa(Nv) for _ in range(RING)] for _ in range(2)],
                "U": [[fa(Nv) for _ in range(RING)] for _ in range(2)],
                "S": [[ba(Nv) for _ in range(RING)] for _ in range(2)],
                "M": [[ba(128) for _ in range(RING)] for _ in range(2)],
                "RZ": [[Res() for _ in range(RING)] for _ in range(2)],
                "RU": [[Res() for _ in range(RING)] for _ in range(2)],
                "RS": [[Res() for _ in range(RING)] for _ in range(2)],
                "RM": [[Res() for _ in range(RING)] for _ in range(2)],
            }

        scan_bufs = alloc_scan(129)
        order_f = list(range(NT))
        order_b = [1, 0] + list(range(NT - 1, 1, -1))

        def run_scans(qT, kT, ktok, vt, Nv, Gfn, Pevac, Rin, Rg):
            sb = scan_bufs
            LA1 = 1
            for it_ in range(NT + LA1):
                m = it_
                if m < NT:
                    for d in range(2):
                        tl = (order_f if d == 0 else order_b)[m]
                        r = m % RING
                        bsc = 0 if d == 0 else 3
                        bU = ((2, 6) if d == 0 else (5, 7))[m % 2]
                        mask = maskf if d == 0 else maskb
                        if tl >= 2:
                            A("pe", lambda e, d=d, tl=tl, bsc=bsc: e.matmul(pbs[bsc][:, 0:128], lhsT=kT(d, tl), rhs=qT(d, tl),
                                                                            start=True, stop=True),
                              reads=Rin(tl), writes=[Rpb[bsc]])
                            A("dve", lambda e, d=d, r=r, bsc=bsc, mask=mask: e.tensor_tensor(
                                out=sb["M"][d][r], in0=pbs[bsc][:, 0:128], in1=mask, op=ALU.mult),
                              reads=[Rpb[bsc], Rc], writes=[sb["RM"][d][r]])
                        A("pe", lambda e, d=d, tl=tl, bU=bU: e.matmul(pbs[bU][:, 0:Nv], lhsT=ktok(d, tl), rhs=vt(d, tl),
                                                                      start=True, stop=True),
                          reads=Rin(tl), writes=[Rpb[bU]])
                m = it_ - LA1
                if m >= 0:
                    for d in range(2):
                        tl = (order_f if d == 0 else order_b)[m]
                        r = m % RING
                        rp = (m - 1) % RING
                        rn = (m + 1) % RING
                        bP = 1 if d == 0 else 4
                        bU = ((2, 6) if d == 0 else (5, 7))[m % 2]
                        if m > 0:
                            gprev = Gfn(d, m - 1)
                            A("dve", lambda e, d=d, r=r, rp=rp, gprev=gprev, bU=bU: e.scalar_tensor_tensor(
                                out=sb["Z"][d][r], in0=sb["Z"][d][rp], scalar=gprev, in1=pbs[bU][:, 0:Nv],
                                op0=ALU.mult, op1=ALU.add),
                              reads=[sb["RZ"][d][rp], Rpb[bU]] + Rg, writes=[sb["RZ"][d][r]])
                        else:
                            A("dve", lambda e, d=d, r=r, bU=bU: e.tensor_copy(out=sb["Z"][d][r], in_=pbs[bU][:, 0:Nv]),
                              reads=[Rpb[bU]], writes=[sb["RZ"][d][r]])
                        if tl >= 2:
                            def mmP(e, d=d, tl=tl, bP=bP, m=m, r=r):
                                ins = e.matmul(pbs[bP][:, 0:Nv], lhsT=sb["M"][d][r], rhs=vt(d, tl), start=True, stop=(m == 0))
                                if m > 0:
                                    ins = e.matmul(pbs[bP][:, 0:Nv], lhsT=qT(d, tl), rhs=sb["S"][d][r], start=False, stop=True)
                                return ins
                            A("pe", mmP, reads=[sb["RM"][d][r], sb["RS"][d][r]] + Rin(tl), writes=[Rpb[bP]])
                            Pevac(d, tl, pbs[bP][:, 0:Nv], Rpb[bP])
                        if m < NT - 1:
                            gthis = Gfn(d, m)
                            A("act", lambda e, d=d, r=r, rn=rn, gthis=gthis: e.activation(
                                out=sb["S"][d][rn], in_=sb["Z"][d][r], func=AF.Copy, scale=gthis),
                              reads=[sb["RZ"][d][r]] + Rg, writes=[sb["RS"][d][rn]])

        PAD = ba(66 * 66).rearrange("p (r c) -> p r c", c=66)
        PADC = ba(258)
        RPAD = Res()
        A("pool", lambda e: e.memset(PAD, 0.0), writes=[RPAD])
        A("pool", lambda e: e.memset(PADC, 0.0), writes=[RPAD])
        qkT = [ba(NTOK) for _ in range(2)]
        KTOK = ba(NT * 128).rearrange("p (t c) -> p t c", c=128)
        VT = [ba(NT * 129).rearrange("p (t c) -> p t c", c=129) for _ in range(2)]
        OGs = [ba(32 * 128).rearrange("p (t c) -> p t c", c=128) for _ in range(2)]
        W4 = [ba(4 * 8 * 128).rearrange("p (w k n) -> p w k n", w=4, n=128)] * 2
        DGm = ba(2 * 9 * 128).rearrange("p (w t n) -> p w t n", w=2, n=128)
        PST = [fa(32 * 129).rearrange("p (t c) -> p t c", c=129) for _ in range(2)]
        mcwt = fa(72).rearrange("p (c t) -> p c t", t=9)
        sqjM = fa(128)
        dent = [fa(32) for _ in range(2)]
        ssqM = fa(32)
        Rmcw = Res()
        RW4 = [Res()] * 2
        RDG = Res()
        Rhd = Res("headdata")
        ROGs = [Res("og0"), Res("og1")]
        RssM = Res()
        RPST = Res()
        Ryst = Res()
        A("sp", lambda e: e.dma_start(out=mcwt, in_=D["mcw"][:, :, :]), writes=[Rmcw], dma="mcw")
        ysc_v = ysc.rearrange("(t p) c -> p t c", p=128)

        def silu_evac(ps_ap, ps_view_fn, dst_ap, n, bank, scale=1.0):
            A("act", lambda e: e.activation(out=dst_ap, in_=ps_ap, func=AF.Silu), reads=[Rpb[bank]], writes=[Rhd])

        def fin_gen(h):
            OG = OGs[h % 2]
            ROG = ROGs[h % 2]
            for d in range(2):
                A("act", lambda e, d=d: e.activation(out=dent[d], in_=PST[d][:, :, 128], func=AF.Abs), reads=[RPST], writes=[Rdn[d]])
                yield
                A("dve", lambda e, d=d, h=h: e.tensor_tensor(out=dent[d], in0=dent[d], in1=FL[:, 2:NT, h + 4 * d], op=ALU.max),
                  reads=[Rdn[d], Rgs], writes=[Rdn[d]])
                A("dve", lambda e, d=d: e.reciprocal(out=dent[d], in_=dent[d]), reads=[Rdn[d]], writes=[Rdn[d]])
                yield
                A("dve", lambda e, d=d: e.tensor_tensor(out=PST[d][:, :, 0:128], in0=PST[d][:, :, 0:128],
                                                        in1=dent[d].unsqueeze(2).to_broadcast([128, 32, 128]), op=ALU.mult),
                  reads=[RPST, Rdn[d]], writes=[RPST])
                yield
            hs = PST[0][:, :, 0:128]
            A("dve", lambda e: e.tensor_tensor(out=hs, in0=hs, in1=PST[1][:, :, 0:128], op=ALU.add), reads=[RPST], writes=[RPST])
            yield
            for i in range(32):
                A("act", lambda e, i=i: e.activation(out=sqjM, in_=PST[0][:, i, 0:128], func=AF.Square, accum_out=ssqM[:, i:i + 1]),
                  reads=[RPST], writes=[RssM])
                if i % 4 == 3:
                    yield
            A("act", lambda e: e.activation(out=ssqM, in_=ssqM, func=AF.Ln, scale=1.0 / 128.0, bias=EPS), reads=[RssM], writes=[RssM])
            A("act", lambda e: e.activation(out=ssqM, in_=ssqM, func=AF.Exp, scale=-0.5, bias=math.log(0.5)), reads=[RssM], writes=[RssM])
            yield
            A("dve", lambda e: e.tensor_tensor(out=hs, in0=hs, in1=ssqM.unsqueeze(2).to_broadcast([128, 32, 128]), op=ALU.mult),
              reads=[RPST, RssM], writes=[RPST])
            yield
            A("dve", lambda e: e.scalar_tensor_tensor(out=OG, in0=OG, scalar=1.0, in1=hs, op0=ALU.add, op1=ALU.mult),
              reads=[RPST, ROG], writes=[ROG])
            A("sp", lambda e, h=h: e.dma_start(out=ysc_v[:, :, h * 128:(h + 1) * 128], in_=OG), reads=[ROG], writes=[],
              dma="yst")
            yield

        pending_fin = [None]
        Rdn = [Res(), Res()]

        def next_fin():
            g = pending_fin[0]
            if g is None:
                return
            try:
                next(g)
            except StopIteration:
                pending_fin[0] = None

        def drain_fin():
            while pending_fin[0] is not None:
                next_fin()

        for h in range(4):
            wslot = h % 2
            for wi, c0 in enumerate((h * 128, 512 + h * 128, 1024 + h * 128, 1536 + h * 128)):
                A("pool", lambda e, wi=wi, c0=c0, wslot=wslot: e.dma_start(out=W4[wslot][:, wi], in_=w_in_v[:, :, c0:c0 + 128]),
                  writes=[RW4[wslot]], dma=f"w4_{wslot}_{wi}")
            for wi in range(2):
                ch = h + 4 * wi
                A("dve", lambda e, wi=wi, ch=ch: e.tensor_tensor(
                    out=DGm[:, wi], in0=identb[:].unsqueeze(1).to_broadcast([128, 9, 128]),
                    in1=mcwt[:, ch, :].unsqueeze(2).to_broadcast([128, 9, 128]), op=ALU.mult),
                  reads=[Rmcw, Rc], writes=[RDG])
            def v_step(t, wslot=wslot, h=h):
                bank = 4 + t % 2

                def mmv(e):
                    for wi2, c0 in ((2, 0), (3, 128)):
                        if wi2 == 3 and t < 2:
                            continue
                        for k in range(8):
                            ins = e.matmul(pbs[bank][:, c0:c0 + 128], lhsT=xT[:, k, t * 128:(t + 1) * 128],
                                           rhs=W4[wslot][:, wi2, k, :], start=(k == 0), stop=(k == 7))
                    return ins
                A("pe", mmv, reads=[RxT[t], RW4[wslot]], writes=[Rpb[bank]])
                for d in range(2):
                    A("act", lambda e, d=d: e.activation(
                        out=VT[d][:, t, 0:128], in_=pbs[bank][:, 0:128], func=AF.Copy, scale=WK[:, t, h + 4 * d:h + 4 * d + 1]),
                      reads=[Rpb[bank], Rgs], writes=[Rhd])
                if t >= 2:
                    A("act", lambda e, OGh=OGs[h % 2]: e.activation(out=OGh[:, t - 2, :], in_=pbs[bank][:, 128:256], func=AF.Tanh, scale=0.5),
                      reads=[Rpb[bank]], writes=[ROGs[h % 2]])

            vcnt = [0]

            def next_v():
                if vcnt[0] < NT:
                    v_step(vcnt[0])
                    vcnt[0] += 1

            for wi in range(2):
                W = W4[wslot][:, wi]
                dstT = qkT[wi]
                for i in range(9):
                    bank = i % 2
                    if i == 0:
                        tok0, ntk = 0, 256
                    else:
                        tok0, ntk = 256 + (i - 1) * 512, 512
                    tiles_needed = [RxT[tt] for tt in range(tok0 // 128, (tok0 + ntk) // 128)]

                    def mmq(e, W=W, tok0=tok0, ntk=ntk, bank=bank):
                        for k in range(8):
                            ins = e.matmul(pbs[bank][:, 0:ntk], lhsT=W[:, k, :], rhs=xT[:, k, tok0:tok0 + ntk],
                                           start=(k == 0), stop=(k == 7))
                        return ins
                    A("pe", mmq, reads=tiles_needed + [RW4[wslot]], writes=[Rpb[bank]])
                    if i == 0:
                        A("act", lambda e, bank=bank: e.activation(out=PADC[:, 1:257], in_=pbs[bank][:, 0:256], func=AF.Copy),
                          reads=[Rpb[bank]], writes=[RPAD])
                    else:
                        r0 = 1 + 8 * (i - 1)
                        A("act", lambda e, bank=bank, r0=r0: e.activation(
                            out=PAD[:, r0:r0 + 8, 1:65], in_=pbs[bank][:, 0:512].rearrange("p (r c) -> p r c", c=64), func=AF.Copy),
                          reads=[Rpb[bank]], writes=[RPAD])
                    next_v()
                    next_fin()
                for i in range(9):
                    bank = 2 + i % 2
                    if i == 0:
                        def mmc0(e, wi=wi, bank=bank):
                            for j, tap in enumerate((3, 4, 5)):
                                ins = e.matmul(pbs[bank][:, 0:256], lhsT=DGm[:, wi, tap, :], rhs=PADC[:, j:j + 256],
                                               start=(j == 0), stop=(j == 2))
                            return ins
                        A("pe", mmc0, reads=[RPAD, RDG], writes=[Rpb[bank]])
                        silu_evac(pbs[bank][:, 0:256], None, dstT[:, 0:256], 256, bank)
                    else:
                        r0 = 8 * (i - 1)

                        def mmc1(e, wi=wi, bank=bank, r0=r0):
                            for tap in range(9):
                                dr, dc = tap // 3, tap % 3
                                ins = e.matmul(pbs[bank][:, 0:512].rearrange("p (r c) -> p r c", c=64),
                                               lhsT=DGm[:, wi, tap, :], rhs=PAD[:, r0 + dr:r0 + dr + 8, dc:dc + 64],
                                               start=(tap == 0), stop=(tap == 8))
                            return ins
                        A("pe", mmc1, reads=[RPAD, RDG], writes=[Rpb[bank]])
                        t0 = 256 + (i - 1) * 512
                        silu_evac(pbs[bank][:, 0:512], None, dstT[:, t0:t0 + 512], 512, bank)
                    next_v()
                    next_fin()
            while vcnt[0] < NT:
                next_v()
            drain_fin()
            for g in range(5):
                t0g = g * 8
                ng = min(8, NT - t0g)
                pslot = g % 2

                def trk(e, t0g=t0g, ng=ng, pslot=pslot):
                    for j in range(ng):
                        ins = e.transpose(out=pts[pslot][:, j * 128:(j + 1) * 128],
                                          in_=qkT[1][:, (t0g + j) * 128:(t0g + j + 1) * 128], identity=identb[:])
                    return ins
                A("pe", trk, reads=[Rhd, Rc], writes=[Rpt[pslot]])
                A("act", lambda e, t0g=t0g, ng=ng, pslot=pslot: e.activation(
                    out=KTOK[:, t0g:t0g + ng, :], in_=pts[pslot][:, 0:ng * 128].rearrange("p (t c) -> p t c", c=128), func=AF.Copy),
                  reads=[Rpt[pslot]], writes=[Rhd])
            for d in range(2):
                A("dve", lambda e, d=d, h=h: e.tensor_copy(out=VT[d][:, :, 128:129], in_=WK[:, :, h + 4 * d:h + 4 * d + 1]),
                  reads=[Rgs], writes=[Rhd])

            def Gm(d, n, h=h):
                tln = (order_f if d == 0 else order_b)[n + 1]
                return DC[:, tln, h + 4 * d:h + 4 * d + 1]

            def Pev(d, tl, ps_ap, bres):
                A("act", lambda e: e.activation(out=PST[d][:, tl - 2, :], in_=ps_ap, func=AF.Copy), reads=[bres], writes=[RPST])
            run_scans(lambda d, tl: qkT[0][:, tl * 128:(tl + 1) * 128], lambda d, tl: qkT[1][:, tl * 128:(tl + 1) * 128],
                      lambda d, tl: KTOK[:, tl, :], lambda d, tl: VT[d][:, tl, :], 129, Gm, Pev, lambda tl: [Rhd], [Rgs])

            pending_fin[0] = fin_gen(h)
        drain_fin()

        reset_arenas()
        scan_bufs = alloc_scan(128)
        QKT = ba(NT * 512).rearrange("p (t c) -> p t c", c=512)
        KHa = ba(NT * 256).rearrange("p (t d c) -> p t d c", d=2, c=128)
        VH = ba(NT * 128).rearrange("p (t c) -> p t c", c=128)
        GG = ba(32 * 128).rearrange("p (t c) -> p t c", c=128)
        W5 = [ba(5 * 8 * 128).rearrange("p (w k n) -> p w k n", w=5, n=128)] * 2
        qtok = [ba(256).rearrange("p (d c) -> p d c", d=2) for _ in range(2)]
        PSTh = fa(32 * 128).rearrange("p (t c) -> p t c", c=128)
        sqj = fa(128)
        GCb = fa(NT * 4).rearrange("p (t c) -> p t c", c=4)
        GT = [fa(NT) for _ in range(2)]
        LB = fa(256).rearrange("p (d c) -> p d c", d=2)
        OML = fa(256).rearrange("p (d c) -> p d c", d=2)
        sgb = [fa(512) for _ in range(2)]
        qsb = [fa(128) for _ in range(3)]
        fgt = [fa(256) for _ in range(2)]
        lft = [fa(256) for _ in range(2)]
        kkt = [fa(256) for _ in range(3)]
        ept = [fa(256) for _ in range(2)]
        emt = [fa(256) for _ in range(2)]
        ssqH = fa(32)
        lfh = [ba(256) for _ in range(2)]
        lfl = [ba(256) for _ in range(2)]
        Rlh = [Res() for _ in range(2)]
        Mdb16 = ba(256).rearrange("p (d c) -> p d c", d=2)
        cwd16 = ba(4)
        Rc16 = Res()
        A("dve", lambda e: e.tensor_copy(out=Mdb16[:, 0, :], in_=Mdf), reads=[Rc], writes=[Rc16])
        A("dve", lambda e: e.tensor_copy(out=Mdb16[:, 1, :], in_=Mdb), reads=[Rc], writes=[Rc16])
        A("dve", lambda e: e.tensor_copy(out=cwd16, in_=cwd[:]), reads=[Rc], writes=[Rc16])
        Rlb = Res()
        RW5 = [Res()] * 2
        RGG = [Res() for _ in range(32)]
        RVH = [Res() for _ in range(NT)]
        RKH = [Res() for _ in range(NT)]
        RQK = [Res() for _ in range(NT)]
        RGT = Res()
        Rsg = [Res() for _ in range(2)]
        Rqs = [Res() for _ in range(3)]
        Rfg = [Res() for _ in range(2)]
        Rlf = [Res() for _ in range(2)]
        Rkk = [Res() for _ in range(3)]
        Rex = [Res() for _ in range(2)]
        Rqk = [Res() for _ in range(2)]
        RGC = Res()
        RPh = [Res() for _ in range(32)]
        RPall = Res()

        HG0 = 2064
        for h in range(4):
            hsl = slice(h * 128, (h + 1) * 128)
            A("sp", lambda e, hsl=hsl: e.dma_start(out=OML[:], in_=D["lbl"][:, :, 0, hsl]), writes=[Rlb], dma="lbl0")
            A("sp", lambda e, hsl=hsl: e.dma_start(out=LB[:], in_=D["lbl"][:, :, 1, hsl]), writes=[Rlb], dma="lbl1")
            A("dve", lambda e: e.tensor_tensor(out=LB[:], in0=LB[:], in1=OML[:], op=ALU.subtract), reads=[Rlb], writes=[Rlb])
            A("act", lambda e: e.activation(out=LB[:], in_=LB[:], func=AF.Exp), reads=[Rlb], writes=[Rlb])
            A("dve", lambda e: e.tensor_scalar_add(out=LB[:], in0=LB[:], scalar1=1.0), reads=[Rlb], writes=[Rlb])
            A("dve", lambda e: e.reciprocal(out=LB[:], in_=LB[:]), reads=[Rlb], writes=[Rlb])
            A("act", lambda e: e.activation(out=OML[:], in_=LB[:], func=AF.Identity, scale=-1.0, bias=1.0), reads=[Rlb], writes=[Rlb])
            cols = [HG0 + h * 128, HG0 + 512 + h * 128, HG0 + 1024 + h * 128, HG0 + 2048 + h * 128, HG0 + 1536 + h * 128]
            for wi, c0 in enumerate(cols):
                A("pool", lambda e, wi=wi, c0=c0: e.dma_start(out=W5[0][:, wi], in_=w_in_v[:, :, c0:c0 + 128]),
                  writes=[RW5[0]], dma=f"w5_{wi}")

            def st_mm5(t):
                bA = t % 2
                bV = 4 + t % 2

                def mm5(e):
                    for k in range(8):
                        e.matmul(pbs[bA][:, 0:512].rearrange("p (w n) -> p w n", w=4),
                                 lhsT=xT[:, k, t * 128:(t + 1) * 128], rhs=W5[0][:, 0:4, k, :],
                                 start=(k == 0), stop=(k == 7))
                    for k in range(8):
                        ins = e.matmul(pbs[bV][:, 0:128], lhsT=xT[:, k, t * 128:(t + 1) * 128], rhs=W5[0][:, 4, k, :],
                                       start=(k == 0), stop=(k == 7))
                    return ins
                A("pe", mm5, reads=[RxT[t], RW5[0]], writes=[Rpb[bA], Rpb[bV]])

            def st_sig(t):
                par = t % 2
                bA = par
                sg = sgb[par]
                A("act", lambda e: e.activation(out=sg, in_=pbs[bA][:, 0:512], func=AF.Exp, scale=-1.0),
                  reads=[Rpb[bA]], writes=[Rsg[par]])
                A("act", lambda e: e.activation(out=sg, in_=sg, func=AF.Ln, bias=1.0), reads=[Rsg[par]], writes=[Rsg[par]])
                A("act", lambda e: e.activation(out=sg, in_=sg, func=AF.Exp, scale=-1.0), reads=[Rsg[par]], writes=[Rsg[par]])

            def st_dvea(t):
                par = t % 2
                q3 = t % 3
                bA = par
                bV = 4 + par
                fv = fgt[par].rearrange("p (d c) -> p d c", d=2)
                A("dve", lambda e: e.tensor_tensor(
                    out=fv, in0=sgb[par][:, 128:384].rearrange("p (d c) -> p d c", d=2), in1=OML[:], op=ALU.mult),
                  reads=[Rsg[par], Rlb], writes=[Rfg[par]])
                A("dve", lambda e: e.tensor_tensor(out=fv, in0=fv, in1=LB[:], op=ALU.add),
                  reads=[Rfg[par], Rlb], writes=[Rfg[par]])
                A("dve", lambda e: e.scalar_tensor_tensor(out=qsb[q3], in0=pbs[bA][:, 0:128], scalar=QS,
                                                          in1=sgb[par][:, 0:128], op0=ALU.mult, op1=ALU.mult),
                  reads=[Rpb[bA], Rsg[par]], writes=[Rqs[q3]])
                if t >= 2:
                    A("dve", lambda e: e.tensor_tensor(out=GG[:, t - 2, :], in0=pbs[bA][:, 384:512],
                                                       in1=sgb[par][:, 384:512], op=ALU.mult),
                      reads=[Rpb[bA], Rsg[par]], writes=[RGG[t - 2]])
                A("dve", lambda e: e.tensor_copy(out=VH[:, t, :], in_=pbs[bV][:, 0:128]),
                  reads=[Rpb[bV]], writes=[RVH[t]])

            def st_lnf(t):
                par = t % 2
                bE = 2 + par
                A("act", lambda e: e.activation(out=lft[par], in_=fgt[par], func=AF.Ln), reads=[Rfg[par]], writes=[Rlf[par]])
                A("act", lambda e: e.activation(out=kkt[t % 3], in_=fgt[par], func=AF.Identity, scale=-1.0, bias=1.0),
                  reads=[Rfg[par]], writes=[Rkk[t % 3]])
                A("dve", lambda e: e.tensor_copy(out=lfh[par], in_=lft[par]), reads=[Rlf[par]], writes=[Rlh[par]])
                A("dve", lambda e: e.tensor_tensor(out=lfl[par], in0=lft[par], in1=lfh[par], op=ALU.subtract),
                  reads=[Rlf[par], Rlh[par]], writes=[Rlh[par]])

                def mme(e):
                    e.matmul(pbs[bE][:, 0:128], lhsT=Mdb16[:, 0, :], rhs=lfh[par][:, 0:128], start=True, stop=False)
                    e.matmul(pbs[bE][:, 0:128], lhsT=Mdb16[:, 0, :], rhs=lfl[par][:, 0:128], start=False, stop=True)
                    e.matmul(pbs[bE][:, 128:256], lhsT=Mdb16[:, 1, :], rhs=lfh[par][:, 128:256], start=True, stop=False)
                    e.matmul(pbs[bE][:, 128:256], lhsT=Mdb16[:, 1, :], rhs=lfl[par][:, 128:256], start=False, stop=True)
                    e.matmul(pbs[bE][:, 256:258], lhsT=lfh[par][:, 0:128], rhs=cwd16[:, 0:2], start=True, stop=False)
                    e.matmul(pbs[bE][:, 256:258], lhsT=lfl[par][:, 0:128], rhs=cwd16[:, 0:2], start=False, stop=True)
                    e.matmul(pbs[bE][:, 258:260], lhsT=lfh[par][:, 128:256], rhs=cwd16[:, 2:4], start=True, stop=False)
                    return e.matmul(pbs[bE][:, 258:260], lhsT=lfl[par][:, 128:256], rhs=cwd16[:, 2:4], start=False, stop=True)
                A("pe", mme, reads=[Rlh[par], Rc16], writes=[Rpb[bE]])

            def st_s2(t):
                par = t % 2
                q3 = t % 3
                bE = 2 + par
                A("act", lambda e: e.activation(out=ept[par], in_=pbs[bE][:, 0:256], func=AF.Exp),
                  reads=[Rpb[bE]], writes=[Rex[par]])
                A("act", lambda e: e.activation(out=emt[par], in_=pbs[bE][:, 0:256], func=AF.Exp, scale=-1.0),
                  reads=[Rpb[bE]], writes=[Rex[par]])
                A("act", lambda e: e.activation(out=GCb[:, t, :], in_=pbs[bE][:, 256:260], func=AF.Copy),
                  reads=[Rpb[bE]], writes=[RGC])

            def st_s2b(t):
                par = t % 2
                q3 = t % 3
                A("dve", lambda e: e.tensor_tensor(out=qtok[par], in0=ept[par].rearrange("p (d c) -> p d c", d=2),
                                                   in1=qsb[q3].unsqueeze(1).to_broadcast([128, 2, 128]), op=ALU.mult),
                  reads=[Rex[par], Rqs[q3]], writes=[Rqk[par]])
                A("dve", lambda e: e.tensor_tensor(out=KHa[:, t], in0=kkt[q3].rearrange("p (d c) -> p d c", d=2),
                                                   in1=emt[par].rearrange("p (d c) -> p d c", d=2), op=ALU.mult),
                  reads=[Rex[par], Rkk[q3]], writes=[Rqk[par], RKH[t]])

                def trq(e):
                    e.transpose(out=pts[par][:, 0:128], in_=qtok[par][:, 0, :], identity=identb[:])
                    e.transpose(out=pts[par][:, 128:256], in_=KHa[:, t, 0, :], identity=identb[:])
                    e.transpose(out=pts[par][:, 256:384], in_=qtok[par][:, 1, :], identity=identb[:])
                    return e.transpose(out=pts[par][:, 384:512], in_=KHa[:, t, 1, :], identity=identb[:])
                A("pe", trq, reads=[Rqk[par], Rc], writes=[Rpt[par]])
                A("dve", lambda e: e.tensor_copy(out=QKT[:, t, :], in_=pts[par][:, 0:512]),
                  reads=[Rpt[par]], writes=[RQK[t]])

            for i in range(NT + 2):
                if i < NT:
                    st_mm5(i)
                if 0 <= i - 2 < NT:
                    st_s2(i - 2)
                if 0 <= i - 1 < NT:
                    st_lnf(i - 1)
                if 0 <= i - 2 < NT:
                    st_s2b(i - 2)
                if i < NT:
                    st_sig(i)
                    st_dvea(i)
            A("dve", lambda e: e.tensor_tensor(out=GT[0][:, 0:NT - 1], in0=GCb[:, 0:NT - 1, 0], in1=GCb[:, 1:NT, 1],
                                               op=ALU.add), reads=[RGC], writes=[RGC])
            A("dve", lambda e: e.tensor_tensor(out=GT[1][:, 1:NT], in0=GCb[:, 1:NT, 2], in1=GCb[:, 0:NT - 1, 3],
                                               op=ALU.add), reads=[RGC], writes=[RGC])
            A("dve", lambda e: e.tensor_tensor(out=GT[1][:, 0:1], in0=GCb[:, 0, 2:3], in1=GCb[:, NT - 1, 3:4],
                                               op=ALU.add), reads=[RGC], writes=[RGC])
            A("act", lambda e: e.activation(out=GT[0][:, 0:NT - 1], in_=GT[0][:, 0:NT - 1], func=AF.Exp), reads=[RGC], writes=[RGT])
            A("act", lambda e: e.activation(out=GT[1], in_=GT[1], func=AF.Exp), reads=[RGC], writes=[RGT])

            def Gh(d, n):
                tl = (order_f if d == 0 else order_b)[n]
                return GT[d][:, tl:tl + 1]

            seen = set()

            def Pevh(d, tl, ps_ap, bres, seen=seen):
                i = tl - 2
                if i not in seen:
                    seen.add(i)
                    A("act", lambda e: e.activation(out=PSTh[:, i, :], in_=ps_ap, func=AF.Copy), reads=[bres], writes=[RPh[i]])
                else:
                    A("dve", lambda e: e.tensor_tensor(out=PSTh[:, i, :], in0=PSTh[:, i, :], in1=ps_ap, op=ALU.add),
                      reads=[bres, RPh[i]], writes=[RPh[i]])
            run_scans(lambda d, tl: QKT[:, tl, d * 256:d * 256 + 128], lambda d, tl: QKT[:, tl, d * 256 + 128:d * 256 + 256],
                      lambda d, tl: KHa[:, tl, d, :], lambda d, tl: VH[:, tl, :], 128, Gh, Pevh,
                      lambda tl: [RQK[tl], RKH[tl], RVH[tl]], [RGT])
            for i in range(32):
                A("act", lambda e, i=i: e.activation(out=sqj, in_=PSTh[:, i, :], func=AF.Square, accum_out=ssqH[:, i:i + 1]),
                  reads=[RPh[i]], writes=[RPall])
            A("act", lambda e: e.activation(out=ssqH, in_=ssqH, func=AF.Ln, scale=1.0 / 128.0, bias=EPS), reads=[RPall], writes=[RPall])
            A("act", lambda e: e.activation(out=ssqH, in_=ssqH, func=AF.Exp, scale=-0.5), reads=[RPall], writes=[RPall])
            A("dve", lambda e: e.tensor_tensor(out=PSTh, in0=PSTh, in1=ssqH.unsqueeze(2).to_broadcast([128, 32, 128]), op=ALU.mult),
              reads=RPh + [RPall], writes=[RPall])
            A("dve", lambda e: e.tensor_tensor(out=GG, in0=PSTh, in1=GG, op=ALU.mult), reads=[RPall] + RGG, writes=RGG + RPh)
            A("sp", lambda e, h=h: e.dma_start(out=ysc_v[:, :, 512 + h * 128:512 + (h + 1) * 128], in_=GG), reads=RGG,
              writes=[], dma="ysth")

        reset_arenas()
        G2 = fa(1024)

        Dg = fa(1024).rearrange("p (k n) -> p k n", n=128)
        RDg = Res()

        def build_G(Gdst, j0, RGd):
            for k in range(8):
                A("dve", lambda e, k=k: e.tensor_scalar_mul(out=Dg[:, k, :], in0=identf, scalar1=modT[:, j0 + k, 0:1]),
                  reads=[Rprm, Rc], writes=[RDg])
            for half in range(2):
                A("pe", lambda e, half=half: e.matmul(pbs[half][:, 0:512], lhsT=onesf,
                                                      rhs=Dg[:, 4 * half:4 * half + 4, :], start=True, stop=True),
                  reads=[RDg, Rc], writes=[Rpb[half]])
                A("act", lambda e, half=half: e.activation(out=Gdst[:, half * 512:(half + 1) * 512], in_=pbs[half][:, 0:512],
                                                           func=AF.Copy), reads=[Rpb[half]], writes=[RGd])

        wd = ba(22 * 1024).rearrange("p (k n) -> p k n", n=1024)
        Rwd3 = Res()
        G1 = fa(1024)
        RG1 = Res()
        build_G(G1, 16, RG1)
        RG2 = Res()
        build_G(G2, 40, RG2)
        wst4 = [fa(1024) for _ in range(2)]
        Rwst4 = [Res() for _ in range(2)]
        wo = ba(8 * 1024).rearrange("p (k n) -> p k n", n=1024)
        Rwo = Res()
        nwTt = fa(8)
        RnwT = Res()
        A("sp", lambda e: e.dma_start(out=nwTt, in_=D["nwT"][:, :]), writes=[RnwT], dma="nwT")
        wst3 = wst4[0]
        Rwst3 = Rwst4[0]
        for k in range(8):
            A("sp", lambda e, k=k: e.dma_start(out=wst3, in_=D["w_out"][k * 128:(k + 1) * 128, :]), writes=[Rwst3], dma="wst3")
            A("dve", lambda e, k=k: e.scalar_tensor_tensor(out=wo[:, k, :], in0=wst3, scalar=nwTt[:, k:k + 1], in1=G1,
                                                           op0=ALU.mult, op1=ALU.mult),
              reads=[Rwst3, RG1, RnwT], writes=[Rwo])
        xts3 = [fa(1024) for _ in range(3)]
        Rxt3 = [Res() for _ in range(3)]
        junk3 = fa(1024)
        Rjunk3 = Res()
        ssb3 = fa(8)
        Rss3 = [Res() for _ in range(4)]
        xnb3 = [ba(1024) for _ in range(2)]
        Rxn3 = [Res() for _ in range(2)]
        ytl = [ba(1024) for _ in range(2)]
        Ryt = [Res() for _ in range(2)]
        yTt = [ba(1024).rearrange("p (k t) -> p k t", t=128) for _ in range(2)]
        RyT = [Res() for _ in range(2)]
        Rx1 = [Res() for _ in range(32)]
        modtmp3 = [fa(1024) for _ in range(2)]
        Rmt3 = [Res() for _ in range(2)]

        def p3_A1(t):
            s = t % 2
            x3 = t % 3
            A("sp", lambda e: e.dma_start(out=ytl[s], in_=ysc[t * 128:(t + 1) * 128, :]), writes=[Ryt[s]], dma=f"yt{s}")
            A("sp", lambda e: e.dma_start(out=xts3[x3], in_=D["xin"][256 + t * 128:256 + (t + 1) * 128, :]),
              writes=[Rxt3[x3]], dma=f"x1t{x3}")

            def try_(e):
                for k in range(8):
                    ins = e.transpose(out=pts[0][:, k * 128:(k + 1) * 128], in_=ytl[s][:, k * 128:(k + 1) * 128], identity=identb[:])
                return ins
            A("pe", try_, reads=[Ryt[s], Rc], writes=[Rpt[0]])
            A("act", lambda e: e.activation(out=yTt[s].rearrange("p k t -> p (k t)"), in_=pts[0][:, :], func=AF.Copy),
              reads=[Rpt[0]], writes=[RyT[s]])

        def p3_A2(t):
            s = t % 2
            x3 = t % 3
            for half in range(2):
                bank = 2 * s + half

                def mmo(e, half=half, bank=bank):
                    for k in range(8):
                        ins = e.matmul(pbs[bank][:, 0:512], lhsT=yTt[s][:, k, :], rhs=wo[:, k, half * 512:(half + 1) * 512],
                                       start=(k == 0), stop=(k == 7))
                    return ins
                A("pe", mmo, reads=[RyT[s], Rwo], writes=[Rpb[bank]])
                A("dve", lambda e, half=half, bank=bank: e.tensor_tensor(
                    out=xts3[x3][:, half * 512:(half + 1) * 512], in0=xts3[x3][:, half * 512:(half + 1) * 512], in1=pbs[bank][:, 0:512],
                    op=ALU.add), reads=[Rpb[bank], Rxt3[x3]], writes=[Rxt3[x3]])
            A("pool", lambda e: e.dma_start(out=x1sc[t * 128:(t + 1) * 128, :], in_=xts3[x3]), reads=[Rxt3[x3]], writes=[Rx1[t]],
              dma=f"x1w{x3}")
            s4 = t % 4
            ssv = ssb3[:, s4:s4 + 1]
            A("act", lambda e: e.activation(out=junk3, in_=xts3[x3], func=AF.Square, scale=1.0 / 32.0, accum_out=ssv),
              reads=[Rxt3[x3]], writes=[Rjunk3, Rss3[s4]])
            A("act", lambda e: e.activation(out=ssv, in_=ssv, func=AF.Ln, bias=EPS), reads=[Rss3[s4]], writes=[Rss3[s4]])
            A("act", lambda e: e.activation(out=ssv, in_=ssv, func=AF.Exp, scale=-0.5), reads=[Rss3[s4]], writes=[Rss3[s4]])

        def p3_A2b(t):
            s = t % 2
            x3 = t % 3
            s4 = t % 4
            ssv = ssb3[:, s4:s4 + 1]
            A("dve", lambda e: e.tensor_scalar_mul(out=xnb3[s], in0=xts3[x3], scalar1=ssv),
              reads=[Rss3[s4], Rxt3[x3]], writes=[Rxn3[s]])

        def p3_B(t):
            s = t % 2

            def tr2(e):
                for k in range(8):
                    ins = e.transpose(out=pts[1][:, k * 128:(k + 1) * 128], in_=xnb3[s][:, k * 128:(k + 1) * 128], identity=identb[:])
                return ins
            A("pe", tr2, reads=[Rxn3[s], Rc], writes=[Rpt[1]])
            A("dve", lambda e: e.tensor_tensor(out=modtmp3[s].rearrange("p (k c) -> p k c", k=8),
                                               in0=pts[1][:, :].rearrange("p (k c) -> p k c", k=8),
                                               in1=prm[:, 4, :].unsqueeze(2).to_broadcast([128, 8, 128]), op=ALU.mult),
              reads=[Rpt[1], Rprm], writes=[Rmt3[s]])
            A("dve", lambda e: e.tensor_tensor(out=xT[:, :, (t + 2) * 128:(t + 3) * 128],
                                               in0=modtmp3[s].rearrange("p (k c) -> p k c", k=8),
                                               in1=prm[:, 5, :].unsqueeze(2).to_broadcast([128, 8, 128]), op=ALU.add),
              reads=[Rmt3[s], Rprm], writes=[RxT[t + 2]])

        for i in range(34):
            if i < 32:
                p3_A1(i)
            if 0 <= i - 1 < 32:
                p3_A2(i - 1)
            if 0 <= i - 2 < 32:
                p3_B(i - 2)
            if 0 <= i - 1 < 32:
                p3_A2b(i - 1)

        reset_arenas()
        G2 = fa(1024)
        gtmp = [fa(512) for _ in range(2)]
        Rgt = [Res() for _ in range(2)]
        wd = ba(22 * 1024).rearrange("p (k n) -> p k n", n=1024)
        Rwd = Res()
        HT = ba(22 * 512).rearrange("p (j t) -> p j t", t=512)
        RHT = Res()
        UP = [ba(10 * 66).rearrange("p (r c) -> p r c", c=66) for _ in range(2)]
        RUP = [Res() for _ in range(2)]
        WU = [ba(8 * 256).rearrange("p (k n) -> p k n", n=256) for _ in range(2)]
        RWU = [Res() for _ in range(2)]
        DGf = [ba(18 * 128).rearrange("p (w t n) -> p w t n", w=2, n=128) for _ in range(2)]
        RDGf = [Res() for _ in range(2)]
        fcwt = fa(44 * 9).rearrange("p (c t) -> p c t", t=9)
        fnwt = fa(1024)
        Rfc = Res()
        sat = fa(512)
        Rsa = Res()
        x1t = [fa(1024) for _ in range(4)]
        Rx1t = [Res() for _ in range(4)]
        HALO = fa(2816).bitcast(BF16).rearrange("p (j a r c) -> p j a r c", j=22, a=2, r=2)
        RHALO = [[Res() for _ in range(2)] for _ in range(22)]
        junk4 = fa(1024)
        Rjunk4 = Res()
        ssb4 = fa(8)
        Rss4 = [Res() for _ in range(4)]
        A("sp", lambda e: e.dma_start(out=fcwt, in_=D["fcw"][:, :, :]), writes=[Rfc], dma="fcw")
        A("sp", lambda e: e.dma_start(out=fnwt, in_=D["fnw"][:, :]), writes=[Rfc], dma="fnw")
        for ab in range(2):
            A("pool", lambda e, ab=ab: e.memset(UP[ab], 0.0), writes=[RUP[ab]])
        w_up_v = D["w_up"].rearrange("(k p) n -> p k n", p=128)
        it = 0
        NB = 8
        for blk in range(NB):
            R0 = 8 * blk
            if blk == NB - 1:
                for ab in range(2):
                    A("pool", lambda e, ab=ab: e.memset(UP[ab][:, 9, :], 0.0), writes=[RUP[ab]])
            for tt in range(4):
                gt = blk * 4 + tt
                A("sp", lambda e, gt=gt, tt=tt: e.dma_start(out=x1t[tt], in_=x1sc[gt * 128:(gt + 1) * 128, :]), reads=[Rx1[gt]],
                  writes=[Rx1t[tt]], dma=f"x1r{tt}")
            for j in range(22):
                ws = it % 2
                it += 1
                if blk > 0:
                    for ab in range(2):
                        A("dve", lambda e, ab=ab, j=j: e.tensor_copy(out=UP[ab][:, 0:2, 1:65], in_=HALO[:, j, ab]),
                          reads=[RHALO[j][ab]], writes=[RUP[ab]])
                if blk == 0:
                    A("pool", lambda e, j=j: e.dma_start(out=wd[:, j, :], in_=D["w_down"][j * 128:(j + 1) * 128, :]), writes=[Rwd],
                      dma="wdld")
                for ab in range(2):
                    c0 = ab * 2816 + j * 128
                    A("pool", lambda e, ws=ws, ab=ab, c0=c0: e.dma_start(out=WU[ws][:, :, ab * 128:(ab + 1) * 128],
                                                                          in_=w_up_v[:, :, c0:c0 + 128]),
                      writes=[RWU[ws]], dma=f"wu{ws}{ab}")
                    ch = ab * 22 + j
                    A("dve", lambda e, ws=ws, ab=ab, ch=ch: e.tensor_tensor(
                        out=DGf[ws][:, ab], in0=identb[:].unsqueeze(1).to_broadcast([128, 9, 128]),
                        in1=fcwt[:, ch, :].unsqueeze(2).to_broadcast([128, 9, 128]), op=ALU.mult),
                      reads=[Rfc, Rc], writes=[RDGf[ws]])
                if blk == 0:
                    groups = ((1, 8), (9, 1))
                elif blk == NB - 1:
                    groups = ((2, 7),)
                else:
                    groups = ((2, 8),)
                for ab in range(2):
                    for gi, (lo, nrow) in enumerate(groups):
                        tok0 = 256 + (R0 - 1 + lo) * 64
                        ntk = nrow * 64
                        bank = (ab + gi) % 2
                        tiles_needed = [RxT[tt] for tt in range(tok0 // 128, (tok0 + ntk + 127) // 128)]

                        def mmu(e, ws=ws, ab=ab, tok0=tok0, ntk=ntk, bank=bank):
                            for k in range(8):
                                ins = e.matmul(pbs[bank][:, 0:ntk], lhsT=WU[ws][:, k, ab * 128:(ab + 1) * 128],
                                               rhs=xT[:, k, tok0:tok0 + ntk], start=(k == 0), stop=(k == 7))
                            return ins
                        A("pe", mmu, reads=tiles_needed + [RWU[ws]], writes=[Rpb[bank]])
                        A("act", lambda e, ab=ab, lo=lo, nrow=nrow, ntk=ntk, bank=bank: e.activation(
                            out=UP[ab][:, lo:lo + nrow, 1:65], in_=pbs[bank][:, 0:ntk].rearrange("p (r c) -> p r c", c=64),
                            func=AF.Copy), reads=[Rpb[bank]], writes=[RUP[ab]])
                    if blk < NB - 1:
                        A("dve", lambda e, ab=ab, j=j: e.tensor_copy(out=HALO[:, j, ab], in_=UP[ab][:, 8:10, 1:65]),
                          reads=[RUP[ab]], writes=[RHALO[j][ab]])
                for ab in range(2):
                    bank = 2 + ab

                    def mmcf(e, ws=ws, ab=ab, bank=bank):
                        for tap in range(9):
                            dr, dc = tap // 3, tap % 3
                            ins = e.matmul(pbs[bank][:, 0:512].rearrange("p (r c) -> p r c", c=64),
                                           lhsT=DGf[ws][:, ab, tap, :],
                                           rhs=UP[ab][:, dr:dr + 8, dc:dc + 64],
                                           start=(tap == 0), stop=(tap == 8))
                        return ins
                    A("pe", mmcf, reads=[RUP[ab], RDGf[ws]], writes=[Rpb[bank]])
                A("act", lambda e: e.activation(out=sat, in_=pbs[2][:, 0:512], func=AF.Silu), reads=[Rpb[2]], writes=[Rsa])
                A("dve", lambda e, j=j: e.tensor_tensor(out=HT[:, j, :], in0=sat, in1=pbs[3][:, 0:512], op=ALU.mult),
                  reads=[Rsa, Rpb[3]], writes=[RHT])
            for tt in range(4):
                gt = blk * 4 + tt
                s = tt
                for half in range(2):
                    bank = 4 + half

                    def mmd(e, tt=tt, half=half, bank=bank):
                        for j in range(22):
                            ins = e.matmul(pbs[bank][:, 0:512], lhsT=HT[:, j, tt * 128:(tt + 1) * 128],
                                           rhs=wd[:, j, half * 512:(half + 1) * 512], start=(j == 0), stop=(j == 21))
                        return ins
                    A("pe", mmd, reads=[RHT, Rwd], writes=[Rpb[bank]])
                    A("dve", lambda e, half=half, bank=bank: e.tensor_tensor(
                        out=gtmp[half], in0=pbs[bank][:, 0:512], in1=G2[:, half * 512:(half + 1) * 512], op=ALU.mult),
                      reads=[Rpb[bank]], writes=[Rgt[half]])
                    A("dve", lambda e, s=s, half=half: e.tensor_tensor(
                        out=x1t[s][:, half * 512:(half + 1) * 512], in0=x1t[s][:, half * 512:(half + 1) * 512],
                        in1=gtmp[half], op=ALU.add), reads=[Rgt[half], Rx1t[s]], writes=[Rx1t[s]])
                s4 = gt % 4
                ssv = ssb4[:, s4:s4 + 1]
                A("act", lambda e, s=s, ssv=ssv: e.activation(out=junk4, in_=x1t[s], func=AF.Square, scale=1.0 / 32.0, accum_out=ssv),
                  reads=[Rx1t[s]], writes=[Rjunk4, Rss4[s4]])
                A("act", lambda e, ssv=ssv: e.activation(out=ssv, in_=ssv, func=AF.Ln, bias=EPS), reads=[Rss4[s4]], writes=[Rss4[s4]])
                A("act", lambda e, ssv=ssv: e.activation(out=ssv, in_=ssv, func=AF.Exp, scale=-0.5), reads=[Rss4[s4]], writes=[Rss4[s4]])
                A("dve", lambda e, s=s, ssv=ssv: e.scalar_tensor_tensor(out=x1t[s], in0=x1t[s], scalar=ssv, in1=fnwt,
                                                                        op0=ALU.mult, op1=ALU.mult),
                  reads=[Rss4[s4], Rx1t[s], Rfc], writes=[Rx1t[s]])
                A("sp", lambda e, gt=gt, s=s: e.dma_start(out=out[gt * 128:(gt + 1) * 128, :], in_=x1t[s]), reads=[Rx1t[s]], writes=[],
                  dma=f"out{s}")
        P.barrier()
        P.emit(st)
    return nc


def _consts():
    u = np.arange(128)[:, None]
    t = np.arange(128)[None, :]
    cf = np.zeros((8, 128, 128), np.float32)
    cf[0] = np.eye(128)
    cf[1] = 1.0
    cf[2] = (u <= t)
    cf[3] = (u >= t)
    cf[4] = (u > t)
    cf[5] = (u < t)
    cf[6] = (u <= t).astype(np.float32) - (u <= 63).astype(np.float32)
    cf[7] = (u >= t).astype(np.float32) - (u >= 64).astype(np.float32)
    cwd = np.zeros((128, 4), np.float32)
    uu = np.arange(128)
    cwd[:, 0] = uu > 63
    cwd[:, 1] = uu <= 63
    cwd[:, 2] = uu < 64
    cwd[:, 3] = uu >= 64
    return np.ascontiguousarray(cf.transpose(1, 0, 2)), cwd


_NC_CACHE = {}


def kernel(x, c, ctx, c_ctx, w_mod, b_mod, norm1_w, w_in, mlstm_gate_b, mlstm_conv_w, mlstm_norm_w,
           hgrn_lb_logits, hgrn_norm_w, w_out, norm2_w, w_up, ffn_conv_w, w_down, final_norm_w):
    f = lambda a: np.ascontiguousarray(np.asarray(a, dtype=np.float32))
    x, c, ctx, c_ctx = f(x), f(c), f(ctx), f(c_ctx)
    cf, cwd = _consts()
    shared = {
        "w_mod": f(w_mod[0]),
        "b_modT": f(np.asarray(b_mod[0]).reshape(48, 128).T),
        "n1T": f(np.asarray(norm1_w[0]).reshape(8, 128).T),
        "n2T": f(np.asarray(norm2_w[0]).reshape(8, 128).T),
        "fnw": f(np.broadcast_to(np.asarray(final_norm_w)[None, :], (128, 1024))),
        "w_in": f(w_in[0]),
        "gate_b": f(np.broadcast_to(np.asarray(mlstm_gate_b[0])[None, :], (128, 16))),
        "mcw": f(np.asarray(mlstm_conv_w[0]).reshape(9, 8, 128).transpose(2, 1, 0)),
        "fcw": f(np.asarray(ffn_conv_w[0]).reshape(9, 44, 128).transpose(2, 1, 0)),
        "nwT": f(np.concatenate([np.asarray(mlstm_norm_w[0]), np.asarray(hgrn_norm_w[0])]).reshape(8, 128).T),
        "lbl": f(np.broadcast_to(np.asarray(hgrn_lb_logits)[None], (128, 2, 2, 512))),
        "w_out": f(w_out[0]),
        "w_up": f(w_up[0]),
        "w_down": f(w_down[0]),
        "cf32": cf,
        "cwd": cwd,
        "identb": np.eye(128).astype(ml_dtypes.bfloat16),
    }
    in_maps = []
    for b in range(8):
        m = dict(shared)
        m["xin"] = np.ascontiguousarray(np.concatenate([ctx[b], x[b]], axis=0))
        m["c2"] = np.ascontiguousarray(np.stack([c[b], c_ctx], axis=1).reshape(8, 128, 2).transpose(1, 0, 2))
        in_maps.append(m)
    if "nc" not in _NC_CACHE:
        _NC_CACHE["nc"] = build_program()
    res = run_bass_kernel_spmd(_NC_CACHE["nc"], in_maps, core_ids=list(range(8)))
    return np.stack([np.asarray(r["out"], dtype=np.float32) for r in res.results], axis=0)
```

```python
import math
from contextlib import ExitStack

import numpy as np
import ml_dtypes
import concourse.bass as bass
import concourse.mybir as mybir
from concourse.bass_utils import run_bass_kernel_spmd

F32 = mybir.dt.float32
BF16 = mybir.dt.bfloat16
AF = mybir.ActivationFunctionType
ALU = mybir.AluOpType
AX = mybir.AxisListType

ENGS = ("pe", "act", "dve", "pool", "sp")
SIG_ROT = 6000

NT = 34
NTOK = 4352
EPS = 1e-6
LN_SQRT128 = 0.5 * math.log(128.0)
QS = 128.0 ** -0.5


class Res:
    __slots__ = ("name", "last_w", "rd_eng", "rd_dma")

    def __init__(self, name=""):
        self.name = name
        self.last_w = None
        self.rd_eng = {}
        self.rd_dma = []


class Op:
    __slots__ = ("eng", "fn", "deps", "is_dma", "dkey", "dval", "sig", "nsig")

    def __init__(self, eng, fn, is_dma):
        self.eng = eng
        self.fn = fn
        self.deps = []
        self.is_dma = is_dma
        self.dkey = None
        self.dval = 0
        self.sig = None
        self.nsig = False


class Prog:
    def __init__(self, nc):
        self.nc = nc
        self.ops = {e: [] for e in ENGS}
        self.dma_cnt = {}
        self.last_dma = {}

    def add(self, eng, fn, reads=(), writes=(), dma=None):
        op = Op(eng, fn, dma is not None)
        deps = {}
        for r in reads:
            if r.last_w is not None:
                deps[id(r.last_w)] = (r.last_w, True)
        for w in writes:
            if w.last_w is not None and id(w.last_w) not in deps:
                deps[id(w.last_w)] = (w.last_w, False)
            for rd in list(w.rd_eng.values()) + w.rd_dma:
                if id(rd) not in deps:
                    deps[id(rd)] = (rd, False)
        for (p, raw) in deps.values():
            if p.eng == eng and not p.is_dma:
                if eng == "pe" or not raw:
                    continue
            op.deps.append(p)
            p.nsig = True
        for r in reads:
            if op.is_dma:
                r.rd_dma.append(op)
            else:
                r.rd_eng[eng] = op
        for w in writes:
            w.last_w = op
            w.rd_eng = {}
            w.rd_dma = []
        if dma is not None:
            c = self.dma_cnt.get(dma, 0) + 1
            self.dma_cnt[dma] = c
            op.dkey = dma
            op.dval = 16 * c
            self.last_dma[dma] = op
        self.ops[eng].append(op)
        return op

    def barrier(self):
        lasts = []
        for e in ENGS:
            for op in reversed(self.ops[e]):
                if not op.is_dma and op.fn is not None:
                    lasts.append(op)
                    break
        lasts += list(self.last_dma.values())
        for e in ENGS:
            op = Op(e, None, False)
            for p in lasts:
                if p.eng == e and not p.is_dma:
                    continue
                op.deps.append(p)
                p.nsig = True
            self.ops[e].append(op)

    def emit(self, stack):
        nc = self.nc
        nsems = {}
        for e in ENGS:
            cnt = 0
            for op in self.ops[e]:
                if op.is_dma or not op.nsig:
                    continue
                op.sig = (cnt // SIG_ROT, cnt % SIG_ROT + 1)
                cnt += 1
            nsems[e] = (cnt + SIG_ROT - 1) // SIG_ROT
        sems = {}
        for e in ENGS:
            for s in range(nsems[e]):
                sems[(e, s)] = stack.enter_context(nc.semaphore(f"s_{e}_{s}"))
        dsems = {}
        for i, k in enumerate(self.dma_cnt.keys()):
            dsems[k] = stack.enter_context(nc.semaphore(f"d_{i}"))
        block = stack.enter_context(nc.Block())
        engobj = {"pe": block.tensor, "act": block.scalar, "dve": block.vector,
                  "pool": block.gpsimd, "sp": block.sync}

        def make(e):
            def body(eng):
                waited = {}
                for op in self.ops[e]:
                    for p in op.deps:
                        if p.is_dma:
                            key = ("d", p.dkey)
                            sem = dsems[p.dkey]
                            val = p.dval
                        else:
                            key = (p.eng, p.sig[0])
                            sem = sems[key]
                            val = p.sig[1]
                        if waited.get(key, 0) >= val:
                            continue
                        waited[key] = val
                        eng.wait_ge(sem, val)
                    if op.fn is None:
                        continue
                    ins = op.fn(eng)
                    if op.is_dma:
                        ins.then_inc(dsems[op.dkey], 16)
                    elif op.nsig:
                        ins.then_inc(sems[(e, op.sig[0])], 1)
            return body

        for e in ENGS:
            engobj[e](make(e))


def build_program():
    nc = bass.Bass("TRN2", target_bir_lowering=False)
    D = {}

    def din(name, shape, dt=F32):
        D[name] = nc.dram_tensor(name, list(shape), dt, kind="ExternalInput").ap()

    din("xin", [NTOK, 1024])
    din("c2", [128, 8, 2])
    din("w_mod", [1024, 6144])
    din("b_modT", [128, 48])
    din("n1T", [128, 8])
    din("n2T", [128, 8])
    din("fnw", [128, 1024])
    din("w_in", [1024, 4624])
    din("gate_b", [128, 16])
    din("mcw", [128, 8, 9])
    din("fcw", [128, 44, 9])
    din("nwT", [128, 8])
    din("lbl", [128, 2, 2, 512])
    din("w_out", [1024, 1024])
    din("w_up", [1024, 5632])
    din("w_down", [2816, 1024])
    din("cf32", [128, 8, 128])
    din("cwd", [128, 4])
    din("identb", [128, 128], BF16)
    out = nc.dram_tensor("out", [4096, 1024], F32, kind="ExternalOutput").ap()
    ysc = nc.dram_tensor("ysc", [4096, 1024], BF16, kind="Internal").ap()
    x1sc = nc.dram_tensor("x1sc", [4096, 1024], F32, kind="Internal").ap()

    with ExitStack() as st:
        P = Prog(nc)
        A = P.add

        def T(name, shape, dt):
            return st.enter_context(nc.sbuf_tensor(name, list(shape), dt))

        xT = T("xT", [128, 8, NTOK], BF16)
        RxT = [Res(f"xT{t}") for t in range(NT)]
        cf = T("cf", [128, 8, 128], F32)
        identf, onesf, maskf, maskb, SU, SL, Mdf, Mdb = [cf[:, i, :] for i in range(8)]
        cwd = T("cwd_sb", [128, 4], F32)
        identb = T("identb_sb", [128, 128], BF16)
        Rc = Res("consts")
        modT = T("modT", [128, 48, 2], F32)
        prm = T("prm", [128, 6, 8], F32)
        Rprm = Res("prm")
        n12 = T("n12", [128, 2, 8], F32)
        bmT = T("bmT", [128, 48], F32)
        FA = T("FA", [128, 12400], F32)
        BA = T("BA", [128, 44000], BF16)
        pbs = [st.enter_context(nc.psum_tensor(f"pb{i}", [128, 512], F32)) for i in range(8)]
        pts = [pbs[6 + i][:, :].bitcast(BF16) for i in range(2)]
        Rpb = [Res(f"pb{i}") for i in range(8)]
        Rpt = [Rpb[6], Rpb[7]]

        fa_off = [0]
        ba_off = [0]

        def fa(n, shape=None):
            o = fa_off[0]
            fa_off[0] += n
            assert fa_off[0] <= 12400, fa_off[0]
            v = FA[:, o:o + n]
            return v

        def ba(n):
            o = ba_off[0]
            ba_off[0] += n
            assert ba_off[0] <= 44000, ba_off[0]
            return BA[:, o:o + n]

        def reset_arenas():
            P.barrier()
            fa_off[0] = 0
            ba_off[0] = 0

        A("sp", lambda e: e.dma_start(out=cf[:], in_=D["cf32"][:, :, :]), writes=[Rc], dma="c0")
        A("sp", lambda e: e.dma_start(out=cwd[:], in_=D["cwd"][:, :]), writes=[Rc], dma="c1")
        A("sp", lambda e: e.dma_start(out=identb[:], in_=D["identb"][:, :]), writes=[Rc], dma="c2")
        A("sp", lambda e: e.dma_start(out=n12[:, 0, :], in_=D["n1T"][:, :]), writes=[Rprm], dma="c3")
        A("sp", lambda e: e.dma_start(out=n12[:, 1, :], in_=D["n2T"][:, :]), writes=[Rprm], dma="c4")
        A("sp", lambda e: e.dma_start(out=bmT[:], in_=D["b_modT"][:, :]), writes=[Rprm], dma="c5")

        c2t = fa(16).rearrange("p (k m) -> p k m", m=2)
        sct = fa(16).rearrange("p (k m) -> p k m", m=2)
        tmpc = fa(16).rearrange("p (k m) -> p k m", m=2)
        Rc2 = Res()
        A("sp", lambda e: e.dma_start(out=c2t, in_=D["c2"][:, :, :]), writes=[Rc2], dma="c6")
        A("act", lambda e: e.activation(out=tmpc, in_=c2t, func=AF.Exp, scale=-1.0), reads=[Rc2], writes=[Rc2])
        A("dve", lambda e: e.tensor_scalar_add(out=tmpc, in0=tmpc, scalar1=1.0), reads=[Rc2], writes=[Rc2])
        A("dve", lambda e: e.reciprocal(out=tmpc, in_=tmpc), reads=[Rc2], writes=[Rc2])
        A("dve", lambda e: e.tensor_tensor(out=sct, in0=c2t, in1=tmpc, op=ALU.mult), reads=[Rc2], writes=[Rc2])
        wm = [ba(4096).rearrange("p (k n) -> p k n", n=512) for _ in range(3)]
        Rwm = [Res() for _ in range(3)]
        sctb = ba(16).rearrange("p (k m) -> p k m", m=2)
        A("dve", lambda e: e.tensor_copy(out=sctb, in_=sct), reads=[Rc2], writes=[Rc2])
        w_mod_v = D["w_mod"].rearrange("(k p) n -> p k n", p=128)
        for jj in range(12):
            s = jj % 3
            A("pool", lambda e, jj=jj, s=s: e.dma_start(out=wm[s], in_=w_mod_v[:, :, jj * 512:(jj + 1) * 512]),
              writes=[Rwm[s]], dma=f"wm{s}")

            def mm(e, jj=jj, s=s):
                for q in range(4):
                    j = jj * 4 + q
                    for k in range(8):
                        ins = e.matmul(pbs[0][:, 2 * j:2 * j + 2], lhsT=wm[s][:, k, q * 128:(q + 1) * 128], rhs=sctb[:, k, :],
                                       start=(k == 0), stop=(k == 7))
                return ins
            A("pe", mm, reads=[Rwm[s], Rc2], writes=[Rpb[0]])
        A("dve", lambda e: e.tensor_tensor(out=modT[:], in0=pbs[0][:, 0:96].rearrange("p (j m) -> p j m", m=2),
                                           in1=bmT[:].unsqueeze(2).to_broadcast([128, 48, 2]), op=ALU.add),
          reads=[Rpb[0], Rprm], writes=[Rprm])
        for (pi, nidx, scj, col) in ((0, 0, 8, 0), (2, 0, 8, 1), (4, 1, 32, 0)):
            A("dve", lambda e, pi=pi, nidx=nidx, scj=scj, col=col: e.scalar_tensor_tensor(
                out=prm[:, pi, :], in0=modT[:, scj:scj + 8, col], scalar=1.0, in1=n12[:, nidx, :],
                op0=ALU.add, op1=ALU.mult), reads=[Rprm], writes=[Rprm])
        for (pi, shj, col) in ((1, 0, 0), (3, 0, 1), (5, 24, 0)):
            A("dve", lambda e, pi=pi, shj=shj, col=col: e.tensor_copy(out=prm[:, pi, :], in_=modT[:, shj:shj + 8, col]),
              reads=[Rprm], writes=[Rprm])

        xts = [fa(1024) for _ in range(3)]
        Rxt = [Res() for _ in range(3)]
        junk = fa(1024)
        Rjunk = Res()
        ssb = fa(8)
        Rss = [Res() for _ in range(4)]
        xnb = [ba(1024) for _ in range(2)]
        Rxn = [Res() for _ in range(2)]
        modtmp = [fa(1024) for _ in range(2)]
        Rmt = [Res() for _ in range(2)]

        def p1_stageA(t):
            s = t % 3
            s4 = t % 4
            s2 = t % 2
            ssv = ssb[:, s4:s4 + 1]
            A("sp", lambda e: e.dma_start(out=xts[s], in_=D["xin"][t * 128:(t + 1) * 128, :]), writes=[Rxt[s]], dma=f"xt{s}")
            A("act", lambda e: e.activation(out=junk, in_=xts[s], func=AF.Square, scale=1.0 / 32.0, accum_out=ssv),
              reads=[Rxt[s]], writes=[Rjunk, Rss[s4]])
            A("act", lambda e: e.activation(out=ssv, in_=ssv, func=AF.Ln, bias=EPS), reads=[Rss[s4]], writes=[Rss[s4]])
            A("act", lambda e: e.activation(out=ssv, in_=ssv, func=AF.Exp, scale=-0.5), reads=[Rss[s4]], writes=[Rss[s4]])
            A("dve", lambda e: e.tensor_scalar_mul(out=xnb[s2], in0=xts[s], scalar1=ssv),
              reads=[Rss[s4], Rxt[s]], writes=[Rxn[s2]])

        def p1_stageB(t):
            s2 = t % 2
            pa, psh = (2, 3) if t < 2 else (0, 1)

            def tr(e):
                for k in range(8):
                    ins = e.transpose(out=pts[s2][:, k * 128:(k + 1) * 128], in_=xnb[s2][:, k * 128:(k + 1) * 128],
                                      identity=identb[:])
                return ins
            A("pe", tr, reads=[Rxn[s2], Rc], writes=[Rpt[s2]])

            A("dve", lambda e: e.tensor_tensor(out=modtmp[s2].rearrange("p (k c) -> p k c", k=8),
                                               in0=pts[s2][:, :].rearrange("p (k c) -> p k c", k=8),
                                               in1=prm[:, pa, :].unsqueeze(2).to_broadcast([128, 8, 128]), op=ALU.mult),
              reads=[Rpt[s2], Rprm], writes=[Rmt[s2]])
            A("dve", lambda e: e.tensor_tensor(out=xT[:, :, t * 128:(t + 1) * 128],
                                               in0=modtmp[s2].rearrange("p (k c) -> p k c", k=8),
                                               in1=prm[:, psh, :].unsqueeze(2).to_broadcast([128, 8, 128]), op=ALU.add),
              reads=[Rmt[s2], Rprm], writes=[RxT[t]])

        for i in range(NT + 1):
            if i < NT:
                p1_stageA(i)
            if i >= 1:
                p1_stageB(i - 1)

        reset_arenas()
        WK = fa(NT * 8).rearrange("p (t g) -> p t g", g=8)
        FL = fa(NT * 8).rearrange("p (t g) -> p t g", g=8)
        DC = fa(NT * 8).rearrange("p (t g) -> p t g", g=8)
        Rgs = Res("gatescal")
        gbt = fa(16)
        Rgb = Res()
        A("sp", lambda e: e.dma_start(out=gbt, in_=D["gate_b"][:, :]), writes=[Rgb], dma="gb")
        wg = ba(128).rearrange("p (k n) -> p k n", n=16)
        Rwg = Res()
        w_in_v = D["w_in"].rearrange("(k p) n -> p k n", p=128)
        A("pool", lambda e: e.dma_start(out=wg, in_=w_in_v[:, :, 2048:2064]), writes=[Rwg], dma="wg")
        gps = [fa(16) for _ in range(2)]
        nls = [fa(16) for _ in range(2)]
        tm8 = [fa(8) for _ in range(2)]
        Rgp = [Res() for _ in range(2)]
        def p2a_A(t):
            s = t % 2
            b0 = 0 if s == 0 else 2

            def mmg(e):
                for k in range(8):
                    ins = e.matmul(pbs[b0][:, 0:16], lhsT=xT[:, k, t * 128:(t + 1) * 128], rhs=wg[:, k, :],
                                   start=(k == 0), stop=(k == 7))
                return ins
            A("pe", mmg, reads=[RxT[t], Rwg], writes=[Rpb[b0]])
            A("dve", lambda e: e.tensor_tensor(out=gps[s], in0=pbs[b0][:, 0:16], in1=gbt, op=ALU.add),
              reads=[Rpb[b0], Rgb], writes=[Rgp[s]])
            A("act", lambda e: e.activation(out=nls[s], in_=gps[s], func=AF.Exp, scale=-1.0),
              reads=[Rgp[s]], writes=[Rnl[s]])
            A("act", lambda e: e.activation(out=nls[s], in_=nls[s], func=AF.Ln, bias=1.0),
              reads=[Rnl[s]], writes=[Rnl[s]])

        def p2a_B(t):
            s = t % 2
            b1 = 1 if s == 0 else 3

            def mmc(e):
                e.matmul(pbs[b1][:, 0:16], lhsT=SU, rhs=nls[s], start=True, stop=True)
                e.matmul(pbs[b1][:, 16:32], lhsT=SL, rhs=nls[s], start=True, stop=True)
                return e.matmul(pbs[b1][:, 32:48], lhsT=onesf, rhs=nls[s], start=True, stop=True)
            A("pe", mmc, reads=[Rnl[s], Rc], writes=[Rpb[b1]])
            A("dve", lambda e: e.tensor_tensor(out=tm8[s][:, 0:4], in0=gps[s][:, 0:4], in1=pbs[b1][:, 4:8],
                                               op=ALU.subtract), reads=[Rgp[s], Rpb[b1]], writes=[Rtm[s]])
            A("dve", lambda e: e.tensor_tensor(out=tm8[s][:, 4:8], in0=gps[s][:, 8:12], in1=pbs[b1][:, 28:32],
                                               op=ALU.subtract), reads=[Rgp[s], Rpb[b1]], writes=[Rtm[s]])
            A("act", lambda e: e.activation(out=WK[:, t, :], in_=tm8[s], func=AF.Exp),
              reads=[Rtm[s]], writes=[Rgs])
            for (dst, c0, c1, bias) in ((FL, 4, 0, LN_SQRT128), (FL, 28, 4, LN_SQRT128), (DC, 36, 0, 0.0), (DC, 44, 4, 0.0)):
                A("act", lambda e, dst=dst, c0=c0, c1=c1, bias=bias: e.activation(
                    out=dst[:, t, c1:c1 + 4], in_=pbs[b1][:, c0:c0 + 4], func=AF.Exp, scale=-1.0, bias=bias),
                  reads=[Rpb[b1]], writes=[Rgs])

        Rnl = [Res() for _ in range(2)]
        Rtm = [Res() for _ in range(2)]
        for i in range(NT + 1):
            if i < NT:
                p2a_A(i)
            if i >= 1:
                p2a_B(i - 1)

        RING = 4
        LA = 2

        def alloc_scan(Nv):
            return {
                "Z": [[fa(Nv) for _ in range(RING)] for _ in range(2)],
                "U": [[fa(Nv) for _ in range(RING)] for _ in range(2)],
                "S": [[ba(Nv) for _ in range(RING)] for _ in range(2)],
                "M": [[ba(128) for _ in range(RING)] for _ in range(2)],
                "RZ": [[Res() for _ in range(RING)] for _ in range(2)],
                "RU": [[Res() for _ in range(RING)] for _ in range(2)],
                "RS": [[Res() for _ in range(RING)] for _ in range(2)],
                "RM": [[Res() for _ in range(RING)] for _ in range(2)],
            }

        scan_bufs = alloc_scan(129)
        order_f = list(range(NT))
        order_b = [1, 0] + list(range(NT - 1, 1, -1))

        def run_scans(qT, kT, ktok, vt, Nv, Gfn, Pevac, Rin, Rg):
            sb = scan_bufs
            LA1 = 1
            for it_ in range(NT + LA1):
                m = it_
                if m < NT:
                    for d in range(2):
                        tl = (order_f if d == 0 else order_b)[m]
                        r = m % RING
                        bsc = 0 if d == 0 else 3
                        bU = ((2, 6) if d == 0 else (5, 7))[m % 2]
                        mask = maskf if d == 0 else maskb
                        if tl >= 2:
                            A("pe", lambda e, d=d, tl=tl, bsc=bsc: e.matmul(pbs[bsc][:, 0:128], lhsT=kT(d, tl), rhs=qT(d, tl),
                                                                            start=True, stop=True),
                              reads=Rin(tl), writes=[Rpb[bsc]])
                            A("dve", lambda e, d=d, r=r, bsc=bsc, mask=mask: e.tensor_tensor(
                                out=sb["M"][d][r], in0=pbs[bsc][:, 0:128], in1=mask, op=ALU.mult),
                              reads=[Rpb[bsc], Rc], writes=[sb["RM"][d][r]])
                        A("pe", lambda e, d=d, tl=tl, bU=bU: e.matmul(pbs[bU][:, 0:Nv], lhsT=ktok(d, tl), rhs=vt(d, tl),
                                                                      start=True, stop=True),
                          reads=Rin(tl), writes=[Rpb[bU]])
                m = it_ - LA1
                if m >= 0:
                    for d in range(2):
                        tl = (order_f if d == 0 else order_b)[m]
                        r = m % RING
                        rp = (m - 1) % RING
                        rn = (m + 1) % RING
                        bP = 1 if d == 0 else 4
                        bU = ((2, 6) if d == 0 else (5, 7))[m % 2]
                        if m > 0:
                            gprev = Gfn(d, m - 1)
                            A("dve", lambda e, d=d, r=r, rp=rp, gprev=gprev, bU=bU: e.scalar_tensor_tensor(
                                out=sb["Z"][d][r], in0=sb["Z"][d][rp], scalar=gprev, in1=pbs[bU][:, 0:Nv],
                                op0=ALU.mult, op1=ALU.add),
                              reads=[sb["RZ"][d][rp], Rpb[bU]] + Rg, writes=[sb["RZ"][d][r]])
                        else:
                            A("dve", lambda e, d=d, r=r, bU=bU: e.tensor_copy(out=sb["Z"][d][r], in_=pbs[bU][:, 0:Nv]),
                              reads=[Rpb[bU]], writes=[sb["RZ"][d][r]])
                        if tl >= 2:
                            def mmP(e, d=d, tl=tl, bP=bP, m=m, r=r):
                                ins = e.matmul(pbs[bP][:, 0:Nv], lhsT=sb["M"][d][r], rhs=vt(d, tl), start=True, stop=(m == 0))
                                if m > 0:
                                    ins = e.matmul(pbs[bP][:, 0:Nv], lhsT=qT(d, tl), rhs=sb["S"][d][r], start=False, stop=True)
                                return ins
                            A("pe", mmP, reads=[sb["RM"][d][r], sb["RS"][d][r]] + Rin(tl), writes=[Rpb[bP]])
                            Pevac(d, tl, pbs[bP][:, 0:Nv], Rpb[bP])
                        if m < NT - 1:
                            gthis = Gfn(d, m)
                            A("act", lambda e, d=d, r=r, rn=rn, gthis=gthis: e.activation(
                                out=sb["S"][d][rn], in_=sb["Z"][d][r], func=AF.Copy, scale=gthis),
                              reads=[sb["RZ"][d][r]] + Rg, writes=[sb["RS"][d][rn]])

        PAD = ba(66 * 66).rearrange("p (r c) -> p r c", c=66)
        PADC = ba(258)
        RPAD = Res()
        A("pool", lambda e: e.memset(PAD, 0.0), writes=[RPAD])
        A("pool", lambda e: e.memset(PADC, 0.0), writes=[RPAD])
        qkT = [ba(NTOK) for _ in range(2)]
        KTOK = ba(NT * 128).rearrange("p (t c) -> p t c", c=128)
        VT = [ba(NT * 129).rearrange("p (t c) -> p t c", c=129) for _ in range(2)]
        OGs = [ba(32 * 128).rearrange("p (t c) -> p t c", c=128) for _ in range(2)]
        W4 = [ba(4 * 8 * 128).rearrange("p (w k n) -> p w k n", w=4, n=128)] * 2
        DGm = ba(2 * 9 * 128).rearrange("p (w t n) -> p w t n", w=2, n=128)
        PST = [fa(32 * 129).rearrange("p (t c) -> p t c", c=129) for _ in range(2)]
        mcwt = fa(72).rearrange("p (c t) -> p c t", t=9)
        sqjM = fa(128)
        dent = [fa(32) for _ in range(2)]
        ssqM = fa(32)
        Rmcw = Res()
        RW4 = [Res()] * 2
        RDG = Res()
        Rhd = Res("headdata")
        ROGs = [Res("og0"), Res("og1")]
        RssM = Res()
        RPST = Res()
        Ryst = Res()
        A("sp", lambda e: e.dma_start(out=mcwt, in_=D["mcw"][:, :, :]), writes=[Rmcw], dma="mcw")
        ysc_v = ysc.rearrange("(t p) c -> p t c", p=128)

        def silu_evac(ps_ap, ps_view_fn, dst_ap, n, bank, scale=1.0):
            A("act", lambda e: e.activation(out=dst_ap, in_=ps_ap, func=AF.Silu), reads=[Rpb[bank]], writes=[Rhd])

        def fin_gen(h):
            OG = OGs[h % 2]
            ROG = ROGs[h % 2]
            for d in range(2):
                A("act", lambda e, d=d: e.activation(out=dent[d], in_=PST[d][:, :, 128], func=AF.Abs), reads=[RPST], writes=[Rdn[d]])
                yield
                A("dve", lambda e, d=d, h=h: e.tensor_tensor(out=dent[d], in0=dent[d], in1=FL[:, 2:NT, h + 4 * d], op=ALU.max),
                  reads=[Rdn[d], Rgs], writes=[Rdn[d]])
                A("dve", lambda e, d=d: e.reciprocal(out=dent[d], in_=dent[d]), reads=[Rdn[d]], writes=[Rdn[d]])
                yield
                A("dve", lambda e, d=d: e.tensor_tensor(out=PST[d][:, :, 0:128], in0=PST[d][:, :, 0:128],
                                                        in1=dent[d].unsqueeze(2).to_broadcast([128, 32, 128]), op=ALU.mult),
                  reads=[RPST, Rdn[d]], writes=[RPST])
                yield
            hs = PST[0][:, :, 0:128]
            A("dve", lambda e: e.tensor_tensor(out=hs, in0=hs, in1=PST[1][:, :, 0:128], op=ALU.add), reads=[RPST], writes=[RPST])
            yield
            for i in range(32):
                A("act", lambda e, i=i: e.activation(out=sqjM, in_=PST[0][:, i, 0:128], func=AF.Square, accum_out=ssqM[:, i:i + 1]),
                  reads=[RPST], writes=[RssM])
                if i % 4 == 3:
                    yield
            A("act", lambda e: e.activation(out=ssqM, in_=ssqM, func=AF.Ln, scale=1.0 / 128.0, bias=EPS), reads=[RssM], writes=[RssM])
            A("act", lambda e: e.activation(out=ssqM, in_=ssqM, func=AF.Exp, scale=-0.5, bias=math.log(0.5)), reads=[RssM], writes=[RssM])
            yield
            A("dve", lambda e: e.tensor_tensor(out=hs, in0=hs, in1=ssqM.unsqueeze(2).to_broadcast([128, 32, 128]), op=ALU.mult),
              reads=[RPST, RssM], writes=[RPST])
            yield
            A("dve", lambda e: e.scalar_tensor_tensor(out=OG, in0=OG, scalar=1.0, in1=hs, op0=ALU.add, op1=ALU.mult),
              reads=[RPST, ROG], writes=[ROG])
            A("sp", lambda e, h=h: e.dma_start(out=ysc_v[:, :, h * 128:(h + 1) * 128], in_=OG), reads=[ROG], writes=[],
              dma="yst")
            yield

        pending_fin = [None]
        Rdn = [Res(), Res()]

        def next_fin():
            g = pending_fin[0]
            if g is None:
                return
            try:
                next(g)
            except StopIteration:
                pending_fin[0] = None

        def drain_fin():
            while pending_fin[0] is not None:
                next_fin()

        for h in range(4):
            wslot = h % 2
            for wi, c0 in enumerate((h * 128, 512 + h * 128, 1024 + h * 128, 1536 + h * 128)):
                A("pool", lambda e, wi=wi, c0=c0, wslot=wslot: e.dma_start(out=W4[wslot][:, wi], in_=w_in_v[:, :, c0:c0 + 128]),
                  writes=[RW4[wslot]], dma=f"w4_{wslot}_{wi}")
            for wi in range(2):
                ch = h + 4 * wi
                A("dve", lambda e, wi=wi, ch=ch: e.tensor_tensor(
                    out=DGm[:, wi], in0=identb[:].unsqueeze(1).to_broadcast([128, 9, 128]),
                    in1=mcwt[:, ch, :].unsqueeze(2).to_broadcast([128, 9, 128]), op=ALU.mult),
                  reads=[Rmcw, Rc], writes=[RDG])
            def v_step(t, wslot=wslot, h=h):
                bank = 4 + t % 2

                def mmv(e):
                    for wi2, c0 in ((2, 0), (3, 128)):
                        if wi2 == 3 and t < 2:
                            continue
                        for k in range(8):
                            ins = e.matmul(pbs[bank][:, c0:c0 + 128], lhsT=xT[:, k, t * 128:(t + 1) * 128],
                                           rhs=W4[wslot][:, wi2, k, :], start=(k == 0), stop=(k == 7))
                    return ins
                A("pe", mmv, reads=[RxT[t], RW4[wslot]], writes=[Rpb[bank]])
                for d in range(2):
                    A("act", lambda e, d=d: e.activation(
                        out=VT[d][:, t, 0:128], in_=pbs[bank][:, 0:128], func=AF.Copy, scale=WK[:, t, h + 4 * d:h + 4 * d + 1]),
                      reads=[Rpb[bank], Rgs], writes=[Rhd])
                if t >= 2:
                    A("act", lambda e, OGh=OGs[h % 2]: e.activation(out=OGh[:, t - 2, :], in_=pbs[bank][:, 128:256], func=AF.Tanh, scale=0.5),
                      reads=[Rpb[bank]], writes=[ROGs[h % 2]])

            vcnt = [0]

            def next_v():
                if vcnt[0] < NT:
                    v_step(vcnt[0])
                    vcnt[0] += 1

            for wi in range(2):
                W = W4[wslot][:, wi]
                dstT = qkT[wi]
                for i in range(9):
                    bank = i % 2
                    if i == 0:
                        tok0, ntk = 0, 256
                    else:
                        tok0, ntk = 256 + (i - 1) * 512, 512
                    tiles_needed = [RxT[tt] for tt in range(tok0 // 128, (tok0 + ntk) // 128)]

                    def mmq(e, W=W, tok0=tok0, ntk=ntk, bank=bank):
                        for k in range(8):
                            ins = e.matmul(pbs[bank][:, 0:ntk], lhsT=W[:, k, :], rhs=xT[:, k, tok0:tok0 + ntk],
                                           start=(k == 0), stop=(k == 7))
                        return ins
                    A("pe", mmq, reads=tiles_needed + [RW4[wslot]], writes=[Rpb[bank]])
                    if i == 0:
                        A("act", lambda e, bank=bank: e.activation(out=PADC[:, 1:257], in_=pbs[bank][:, 0:256], func=AF.Copy),
                          reads=[Rpb[bank]], writes=[RPAD])
                    else:
                        r0 = 1 + 8 * (i - 1)
                        A("act", lambda e, bank=bank, r0=r0: e.activation(
                            out=PAD[:, r0:r0 + 8, 1:65], in_=pbs[bank][:, 0:512].rearrange("p (r c) -> p r c", c=64), func=AF.Copy),
                          reads=[Rpb[bank]], writes=[RPAD])
                    next_v()
                    next_fin()
                for i in range(9):
                    bank = 2 + i % 2
                    if i == 0:
                        def mmc0(e, wi=wi, bank=bank):
                            for j, tap in enumerate((3, 4, 5)):
                                ins = e.matmul(pbs[bank][:, 0:256], lhsT=DGm[:, wi, tap, :], rhs=PADC[:, j:j + 256],
                                               start=(j == 0), stop=(j == 2))
                            return ins
                        A("pe", mmc0, reads=[RPAD, RDG], writes=[Rpb[bank]])
                        silu_evac(pbs[bank][:, 0:256], None, dstT[:, 0:256], 256, bank)
                    else:
                        r0 = 8 * (i - 1)

                        def mmc1(e, wi=wi, bank=bank, r0=r0):
                            for tap in range(9):
                                dr, dc = tap // 3, tap % 3
                                ins = e.matmul(pbs[bank][:, 0:512].rearrange("p (r c) -> p r c", c=64),
                                               lhsT=DGm[:, wi, tap, :], rhs=PAD[:, r0 + dr:r0 + dr + 8, dc:dc + 64],
                                               start=(tap == 0), stop=(tap == 8))
                            return ins
                        A("pe", mmc1, reads=[RPAD, RDG], writes=[Rpb[bank]])
                        t0 = 256 + (i - 1) * 512
                        silu_evac(pbs[bank][:, 0:512], None, dstT[:, t0:t0 + 512], 512, bank)
                    next_v()
                    next_fin()
            while vcnt[0] < NT:
                next_v()
            drain_fin()
            for g in range(5):
                t0g = g * 8
                ng = min(8, NT - t0g)
                pslot = g % 2

                def trk(e, t0g=t0g, ng=ng, pslot=pslot):
                    for j in range(ng):
                        ins = e.transpose(out=pts[pslot][:, j * 128:(j + 1) * 128],
                                          in_=qkT[1][:, (t0g + j) * 128:(t0g + j + 1) * 128], identity=identb[:])
                    return ins
                A("pe", trk, reads=[Rhd, Rc], writes=[Rpt[pslot]])
                A("act", lambda e, t0g=t0g, ng=ng, pslot=pslot: e.activation(
                    out=KTOK[:, t0g:t0g + ng, :], in_=pts[pslot][:, 0:ng * 128].rearrange("p (t c) -> p t c", c=128), func=AF.Copy),
                  reads=[Rpt[pslot]], writes=[Rhd])
            for d in range(2):
                A("dve", lambda e, d=d, h=h: e.tensor_copy(out=VT[d][:, :, 128:129], in_=WK[:, :, h + 4 * d:h + 4 * d + 1]),
                  reads=[Rgs], writes=[Rhd])

            def Gm(d, n, h=h):
                tln = (order_f if d == 0 else order_b)[n + 1]
                return DC[:, tln, h + 4 * d:h + 4 * d + 1]

            def Pev(d, tl, ps_ap, bres):
                A("act", lambda e: e.activation(out=PST[d][:, tl - 2, :], in_=ps_ap, func=AF.Copy), reads=[bres], writes=[RPST])
            run_scans(lambda d, tl: qkT[0][:, tl * 128:(tl + 1) * 128], lambda d, tl: qkT[1][:, tl * 128:(tl + 1) * 128],
                      lambda d, tl: KTOK[:, tl, :], lambda d, tl: VT[d][:, tl, :], 129, Gm, Pev, lambda tl: [Rhd], [Rgs])

            pending_fin[0] = fin_gen(h)
        drain_fin()

        reset_arenas()
        scan_bufs = alloc_scan(128)
        QKT = ba(NT * 512).rearrange("p (t c) -> p t c", c=512)
        KHa = ba(NT * 256).rearrange("p (t d c) -> p t d c", d=2, c=128)
        VH = ba(NT * 128).rearrange("p (t c) -> p t c", c=128)
        GG = ba(32 * 128).rearrange("p (t c) -> p t c", c=128)
        W5 = [ba(5 * 8 * 128).rearrange("p (w k n) -> p w k n", w=5, n=128)] * 2
        qtok = [ba(256).rearrange("p (d c) -> p d c", d=2) for _ in range(2)]
        PSTh = fa(32 * 128).rearrange("p (t c) -> p t c", c=128)
        sqj = fa(128)
        GCb = fa(NT * 4).rearrange("p (t c) -> p t c", c=4)
        GT = [fa(NT) for _ in range(2)]
        LB = fa(256).rearrange("p (d c) -> p d c", d=2)
        OML = fa(256).rearrange("p (d c) -> p d c", d=2)
        sgb = [fa(512) for _ in range(2)]
        qsb = [fa(128) for _ in range(3)]
        fgt = [fa(256) for _ in range(2)]
        lft = [fa(256) for _ in range(2)]
        kkt = [fa(256) for _ in range(3)]
        ept = [fa(256) for _ in range(2)]
        emt = [fa(256) for _ in range(2)]
        ssqH = fa(32)
        lfh = [ba(256) for _ in range(2)]
        lfl = [ba(256) for _ in range(2)]
        Rlh = [Res() for _ in range(2)]
        Mdb16 = ba(256).rearrange("p (d c) -> p d c", d=2)
        cwd16 = ba(4)
        Rc16 = Res()
        A("dve", lambda e: e.tensor_copy(out=Mdb16[:, 0, :], in_=Mdf), reads=[Rc], writes=[Rc16])
        A("dve", lambda e: e.tensor_copy(out=Mdb16[:, 1, :], in_=Mdb), reads=[Rc], writes=[Rc16])
        A("dve", lambda e: e.tensor_copy(out=cwd16, in_=cwd[:]), reads=[Rc], writes=[Rc16])
        Rlb = Res()
        RW5 = [Res()] * 2
        RGG = [Res() for _ in range(32)]
        RVH = [Res() for _ in range(NT)]
        RKH = [Res() for _ in range(NT)]
        RQK = [Res() for _ in range(NT)]
        RGT = Res()
        Rsg = [Res() for _ in range(2)]
        Rqs = [Res() for _ in range(3)]
        Rfg = [Res() for _ in range(2)]
        Rlf = [Res() for _ in range(2)]
        Rkk = [Res() for _ in range(3)]
        Rex = [Res() for _ in range(2)]
        Rqk = [Res() for _ in range(2)]
        RGC = Res()
        RPh = [Res() for _ in range(32)]
        RPall = Res()

        HG0 = 2064
        for h in range(4):
            hsl = slice(h * 128, (h + 1) * 128)
            A("sp", lambda e, hsl=hsl: e.dma_start(out=OML[:], in_=D["lbl"][:, :, 0, hsl]), writes=[Rlb], dma="lbl0")
            A("sp", lambda e, hsl=hsl: e.dma_start(out=LB[:], in_=D["lbl"][:, :, 1, hsl]), writes=[Rlb], dma="lbl1")
            A("dve", lambda e: e.tensor_tensor(out=LB[:], in0=LB[:], in1=OML[:], op=ALU.subtract), reads=[Rlb], writes=[Rlb])
            A("act", lambda e: e.activation(out=LB[:], in_=LB[:], func=AF.Exp), reads=[Rlb], writes=[Rlb])
            A("dve", lambda e: e.tensor_scalar_add(out=LB[:], in0=LB[:], scalar1=1.0), reads=[Rlb], writes=[Rlb])
            A("dve", lambda e: e.reciprocal(out=LB[:], in_=LB[:]), reads=[Rlb], writes=[Rlb])
            A("act", lambda e: e.activation(out=OML[:], in_=LB[:], func=AF.Identity, scale=-1.0, bias=1.0), reads=[Rlb], writes=[Rlb])
            cols = [HG0 + h * 128, HG0 + 512 + h * 128, HG0 + 1024 + h * 128, HG0 + 2048 + h * 128, HG0 + 1536 + h * 128]
            for wi, c0 in enumerate(cols):
                A("pool", lambda e, wi=wi, c0=c0: e.dma_start(out=W5[0][:, wi], in_=w_in_v[:, :, c0:c0 + 128]),
                  writes=[RW5[0]], dma=f"w5_{wi}")

            def st_mm5(t):
                bA = t % 2
                bV = 4 + t % 2

                def mm5(e):
                    for k in range(8):
                        e.matmul(pbs[bA][:, 0:512].rearrange("p (w n) -> p w n", w=4),
                                 lhsT=xT[:, k, t * 128:(t + 1) * 128], rhs=W5[0][:, 0:4, k, :],
                                 start=(k == 0), stop=(k == 7))
                    for k in range(8):
                        ins = e.matmul(pbs[bV][:, 0:128], lhsT=xT[:, k, t * 128:(t + 1) * 128], rhs=W5[0][:, 4, k, :],
                                       start=(k == 0), stop=(k == 7))
                    return ins
                A("pe", mm5, reads=[RxT[t], RW5[0]], writes=[Rpb[bA], Rpb[bV]])

            def st_sig(t):
                par = t % 2
                bA = par
                sg = sgb[par]
                A("act", lambda e: e.activation(out=sg, in_=pbs[bA][:, 0:512], func=AF.Exp, scale=-1.0),
                  reads=[Rpb[bA]], writes=[Rsg[par]])
                A("act", lambda e: e.activation(out=sg, in_=sg, func=AF.Ln, bias=1.0), reads=[Rsg[par]], writes=[Rsg[par]])
                A("act", lambda e: e.activation(out=sg, in_=sg, func=AF.Exp, scale=-1.0), reads=[Rsg[par]], writes=[Rsg[par]])

            def st_dvea(t):
                par = t % 2
                q3 = t % 3
                bA = par
                bV = 4 + par
                fv = fgt[par].rearrange("p (d c) -> p d c", d=2)
                A("dve", lambda e: e.tensor_tensor(
                    out=fv, in0=sgb[par][:, 128:384].rearrange("p (d c) -> p d c", d=2), in1=OML[:], op=ALU.mult),
                  reads=[Rsg[par], Rlb], writes=[Rfg[par]])
                A("dve", lambda e: e.tensor_tensor(out=fv, in0=fv, in1=LB[:], op=ALU.add),
                  reads=[Rfg[par], Rlb], writes=[Rfg[par]])
                A("dve", lambda e: e.scalar_tensor_tensor(out=qsb[q3], in0=pbs[bA][:, 0:128], scalar=QS,
                                                          in1=sgb[par][:, 0:128], op0=ALU.mult, op1=ALU.mult),
                  reads=[Rpb[bA], Rsg[par]], writes=[Rqs[q3]])
                if t >= 2:
                    A("dve", lambda e: e.tensor_tensor(out=GG[:, t - 2, :], in0=pbs[bA][:, 384:512],
                                                       in1=sgb[par][:, 384:512], op=ALU.mult),
                      reads=[Rpb[bA], Rsg[par]], writes=[RGG[t - 2]])
                A("dve", lambda e: e.tensor_copy(out=VH[:, t, :], in_=pbs[bV][:, 0:128]),
                  reads=[Rpb[bV]], writes=[RVH[t]])

            def st_lnf(t):
                par = t % 2
                bE = 2 + par
                A("act", lambda e: e.activation(out=lft[par], in_=fgt[par], func=AF.Ln), reads=[Rfg[par]], writes=[Rlf[par]])
                A("act", lambda e: e.activation(out=kkt[t % 3], in_=fgt[par], func=AF.Identity, scale=-1.0, bias=1.0),
                  reads=[Rfg[par]], writes=[Rkk[t % 3]])
                A("act", lambda e: e.activation(out=lfh[par], in_=lft[par], func=AF.Copy), reads=[Rlf[par]], writes=[Rlh[par]])
                A("dve", lambda e: e.tensor_tensor(out=lfl[par], in0=lft[par], in1=lfh[par], op=ALU.subtract),
                  reads=[Rlf[par], Rlh[par]], writes=[Rlh[par]])

                def mme(e):
                    e.matmul(pbs[bE][:, 0:128], lhsT=Mdb16[:, 0, :], rhs=lfh[par][:, 0:128], start=True, stop=False)
                    e.matmul(pbs[bE][:, 0:128], lhsT=Mdb16[:, 0, :], rhs=lfl[par][:, 0:128], start=False, stop=True)
                    e.matmul(pbs[bE][:, 128:256], lhsT=Mdb16[:, 1, :], rhs=lfh[par][:, 128:256], start=True, stop=False)
                    e.matmul(pbs[bE][:, 128:256], lhsT=Mdb16[:, 1, :], rhs=lfl[par][:, 128:256], start=False, stop=True)
                    e.matmul(pbs[bE][:, 256:258], lhsT=lfh[par][:, 0:128], rhs=cwd16[:, 0:2], start=True, stop=False)
                    e.matmul(pbs[bE][:, 256:258], lhsT=lfl[par][:, 0:128], rhs=cwd16[:, 0:2], start=False, stop=True)
                    e.matmul(pbs[bE][:, 258:260], lhsT=lfh[par][:, 128:256], rhs=cwd16[:, 2:4], start=True, stop=False)
                    return e.matmul(pbs[bE][:, 258:260], lhsT=lfl[par][:, 128:256], rhs=cwd16[:, 2:4], start=False, stop=True)
                A("pe", mme, reads=[Rlh[par], Rc16], writes=[Rpb[bE]])

            def st_s2(t):
                par = t % 2
                q3 = t % 3
                bE = 2 + par
                A("act", lambda e: e.activation(out=ept[par], in_=pbs[bE][:, 0:256], func=AF.Exp),
                  reads=[Rpb[bE]], writes=[Rex[par]])
                A("act", lambda e: e.activation(out=emt[par], in_=pbs[bE][:, 0:256], func=AF.Exp, scale=-1.0),
                  reads=[Rpb[bE]], writes=[Rex[par]])
                A("act", lambda e: e.activation(out=GCb[:, t, :], in_=pbs[bE][:, 256:260], func=AF.Copy),
                  reads=[Rpb[bE]], writes=[RGC])

            def st_s2b(t):
                par = t % 2
                q3 = t % 3
                A("dve", lambda e: e.tensor_tensor(out=qtok[par], in0=ept[par].rearrange("p (d c) -> p d c", d=2),
                                                   in1=qsb[q3].unsqueeze(1).to_broadcast([128, 2, 128]), op=ALU.mult),
                  reads=[Rex[par], Rqs[q3]], writes=[Rqk[par]])
                A("dve", lambda e: e.tensor_tensor(out=KHa[:, t], in0=kkt[q3].rearrange("p (d c) -> p d c", d=2),
                                                   in1=emt[par].rearrange("p (d c) -> p d c", d=2), op=ALU.mult),
                  reads=[Rex[par], Rkk[q3]], writes=[Rqk[par], RKH[t]])

                def trq(e):
                    e.transpose(out=pts[par][:, 0:128], in_=qtok[par][:, 0, :], identity=identb[:])
                    e.transpose(out=pts[par][:, 128:256], in_=KHa[:, t, 0, :], identity=identb[:])
                    e.transpose(out=pts[par][:, 256:384], in_=qtok[par][:, 1, :], identity=identb[:])
                    return e.transpose(out=pts[par][:, 384:512], in_=KHa[:, t, 1, :], identity=identb[:])
                A("pe", trq, reads=[Rqk[par], Rc], writes=[Rpt[par]])
                A("dve", lambda e: e.tensor_copy(out=QKT[:, t, :], in_=pts[par][:, 0:512]),
                  reads=[Rpt[par]], writes=[RQK[t]])

            for i in range(NT + 2):
                if i < NT:
                    st_mm5(i)
                if 0 <= i - 2 < NT:
                    st_s2(i - 2)
                if 0 <= i - 1 < NT:
                    st_lnf(i - 1)
                if 0 <= i - 2 < NT:
                    st_s2b(i - 2)
                if i < NT:
                    st_sig(i)
                    st_dvea(i)
            A("dve", lambda e: e.tensor_tensor(out=GT[0][:, 0:NT - 1], in0=GCb[:, 0:NT - 1, 0], in1=GCb[:, 1:NT, 1],
                                               op=ALU.add), reads=[RGC], writes=[RGC])
            A("dve", lambda e: e.tensor_tensor(out=GT[1][:, 1:NT], in0=GCb[:, 1:NT, 2], in1=GCb[:, 0:NT - 1, 3],
                                               op=ALU.add), reads=[RGC], writes=[RGC])
            A("dve", lambda e: e.tensor_tensor(out=GT[1][:, 0:1], in0=GCb[:, 0, 2:3], in1=GCb[:, NT - 1, 3:4],
                                               op=ALU.add), reads=[RGC], writes=[RGC])
            A("act", lambda e: e.activation(out=GT[0][:, 0:NT - 1], in_=GT[0][:, 0:NT - 1], func=AF.Exp), reads=[RGC], writes=[RGT])
            A("act", lambda e: e.activation(out=GT[1], in_=GT[1], func=AF.Exp), reads=[RGC], writes=[RGT])

            def Gh(d, n):
                tl = (order_f if d == 0 else order_b)[n]
                return GT[d][:, tl:tl + 1]

            seen = set()

            def Pevh(d, tl, ps_ap, bres, seen=seen):
                i = tl - 2
                if i not in seen:
                    seen.add(i)
                    A("act", lambda e: e.activation(out=PSTh[:, i, :], in_=ps_ap, func=AF.Copy), reads=[bres], writes=[RPh[i]])
                else:
                    A("dve", lambda e: e.tensor_tensor(out=PSTh[:, i, :], in0=PSTh[:, i, :], in1=ps_ap, op=ALU.add),
                      reads=[bres, RPh[i]], writes=[RPh[i]])
            run_scans(lambda d, tl: QKT[:, tl, d * 256:d * 256 + 128], lambda d, tl: QKT[:, tl, d * 256 + 128:d * 256 + 256],
                      lambda d, tl: KHa[:, tl, d, :], lambda d, tl: VH[:, tl, :], 128, Gh, Pevh,
                      lambda tl: [RQK[tl], RKH[tl], RVH[tl]], [RGT])
            for i in range(32):
                A("act", lambda e, i=i: e.activation(out=sqj, in_=PSTh[:, i, :], func=AF.Square, accum_out=ssqH[:, i:i + 1]),
                  reads=[RPh[i]], writes=[RPall])
            A("act", lambda e: e.activation(out=ssqH, in_=ssqH, func=AF.Ln, scale=1.0 / 128.0, bias=EPS), reads=[RPall], writes=[RPall])
            A("act", lambda e: e.activation(out=ssqH, in_=ssqH, func=AF.Exp, scale=-0.5), reads=[RPall], writes=[RPall])
            A("dve", lambda e: e.tensor_tensor(out=PSTh, in0=PSTh, in1=ssqH.unsqueeze(2).to_broadcast([128, 32, 128]), op=ALU.mult),
              reads=RPh + [RPall], writes=[RPall])
            A("dve", lambda e: e.tensor_tensor(out=GG, in0=PSTh, in1=GG, op=ALU.mult), reads=[RPall] + RGG, writes=RGG + RPh)
            A("sp", lambda e, h=h: e.dma_start(out=ysc_v[:, :, 512 + h * 128:512 + (h + 1) * 128], in_=GG), reads=RGG,
              writes=[], dma="ysth")

        reset_arenas()
        G2 = fa(1024)

        Dg = fa(1024).rearrange("p (k n) -> p k n", n=128)
        RDg = Res()

        def build_G(Gdst, j0, RGd):
            for k in range(8):
                A("dve", lambda e, k=k: e.tensor_scalar_mul(out=Dg[:, k, :], in0=identf, scalar1=modT[:, j0 + k, 0:1]),
                  reads=[Rprm, Rc], writes=[RDg])
            for half in range(2):
                A("pe", lambda e, half=half: e.matmul(pbs[half][:, 0:512], lhsT=onesf,
                                                      rhs=Dg[:, 4 * half:4 * half + 4, :], start=True, stop=True),
                  reads=[RDg, Rc], writes=[Rpb[half]])
                A("act", lambda e, half=half: e.activation(out=Gdst[:, half * 512:(half + 1) * 512], in_=pbs[half][:, 0:512],
                                                           func=AF.Copy), reads=[Rpb[half]], writes=[RGd])

        wd = ba(22 * 1024).rearrange("p (k n) -> p k n", n=1024)
        Rwd3 = Res()
        G1 = fa(1024)
        RG1 = Res()
        build_G(G1, 16, RG1)
        RG2 = Res()
        build_G(G2, 40, RG2)
        wst4 = [fa(1024) for _ in range(2)]
        Rwst4 = [Res() for _ in range(2)]
        wo = ba(8 * 1024).rearrange("p (k n) -> p k n", n=1024)
        Rwo = Res()
        nwTt = fa(8)
        RnwT = Res()
        A("sp", lambda e: e.dma_start(out=nwTt, in_=D["nwT"][:, :]), writes=[RnwT], dma="nwT")
        wst3 = wst4[0]
        Rwst3 = Rwst4[0]
        for k in range(8):
            A("sp", lambda e, k=k: e.dma_start(out=wst3, in_=D["w_out"][k * 128:(k + 1) * 128, :]), writes=[Rwst3], dma="wst3")
            A("dve", lambda e, k=k: e.scalar_tensor_tensor(out=wo[:, k, :], in0=wst3, scalar=nwTt[:, k:k + 1], in1=G1,
                                                           op0=ALU.mult, op1=ALU.mult),
              reads=[Rwst3, RG1, RnwT], writes=[Rwo])
        xts3 = [fa(1024) for _ in range(3)]
        Rxt3 = [Res() for _ in range(3)]
        junk3 = fa(1024)
        Rjunk3 = Res()
        ssb3 = fa(8)
        Rss3 = [Res() for _ in range(4)]
        xnb3 = [ba(1024) for _ in range(2)]
        Rxn3 = [Res() for _ in range(2)]
        ytl = [ba(1024) for _ in range(2)]
        Ryt = [Res() for _ in range(2)]
        yTt = [ba(1024).rearrange("p (k t) -> p k t", t=128) for _ in range(2)]
        RyT = [Res() for _ in range(2)]
        Rx1 = [Res() for _ in range(32)]
        modtmp3 = [fa(1024) for _ in range(2)]
        Rmt3 = [Res() for _ in range(2)]

        def p3_A1(t):
            s = t % 2
            x3 = t % 3
            A("sp", lambda e: e.dma_start(out=ytl[s], in_=ysc[t * 128:(t + 1) * 128, :]), writes=[Ryt[s]], dma=f"yt{s}")
            A("sp", lambda e: e.dma_start(out=xts3[x3], in_=D["xin"][256 + t * 128:256 + (t + 1) * 128, :]),
              writes=[Rxt3[x3]], dma=f"x1t{x3}")

            def try_(e):
                for k in range(8):
                    ins = e.transpose(out=pts[0][:, k * 128:(k + 1) * 128], in_=ytl[s][:, k * 128:(k + 1) * 128], identity=identb[:])
                return ins
            A("pe", try_, reads=[Ryt[s], Rc], writes=[Rpt[0]])
            A("act", lambda e: e.activation(out=yTt[s].rearrange("p k t -> p (k t)"), in_=pts[0][:, :], func=AF.Copy),
              reads=[Rpt[0]], writes=[RyT[s]])

        def p3_A2(t):
            s = t % 2
            x3 = t % 3
            for half in range(2):
                bank = 2 * s + half

                def mmo(e, half=half, bank=bank):
                    for k in range(8):
                        ins = e.matmul(pbs[bank][:, 0:512], lhsT=yTt[s][:, k, :], rhs=wo[:, k, half * 512:(half + 1) * 512],
                                       start=(k == 0), stop=(k == 7))
                    return ins
                A("pe", mmo, reads=[RyT[s], Rwo], writes=[Rpb[bank]])
                A("dve", lambda e, half=half, bank=bank: e.tensor_tensor(
                    out=xts3[x3][:, half * 512:(half + 1) * 512], in0=xts3[x3][:, half * 512:(half + 1) * 512], in1=pbs[bank][:, 0:512],
                    op=ALU.add), reads=[Rpb[bank], Rxt3[x3]], writes=[Rxt3[x3]])
            A("pool", lambda e: e.dma_start(out=x1sc[t * 128:(t + 1) * 128, :], in_=xts3[x3]), reads=[Rxt3[x3]], writes=[Rx1[t]],
              dma=f"x1w{x3}")
            s4 = t % 4
            ssv = ssb3[:, s4:s4 + 1]
            A("act", lambda e: e.activation(out=junk3, in_=xts3[x3], func=AF.Square, scale=1.0 / 32.0, accum_out=ssv),
              reads=[Rxt3[x3]], writes=[Rjunk3, Rss3[s4]])
            A("act", lambda e: e.activation(out=ssv, in_=ssv, func=AF.Ln, bias=EPS), reads=[Rss3[s4]], writes=[Rss3[s4]])
            A("act", lambda e: e.activation(out=ssv, in_=ssv, func=AF.Exp, scale=-0.5), reads=[Rss3[s4]], writes=[Rss3[s4]])

        def p3_A2b(t):
            s = t % 2
            x3 = t % 3
            s4 = t % 4
            ssv = ssb3[:, s4:s4 + 1]
            A("dve", lambda e: e.tensor_scalar_mul(out=xnb3[s], in0=xts3[x3], scalar1=ssv),
              reads=[Rss3[s4], Rxt3[x3]], writes=[Rxn3[s]])

        def p3_B(t):
            s = t % 2

            def tr2(e):
                for k in range(8):
                    ins = e.transpose(out=pts[1][:, k * 128:(k + 1) * 128], in_=xnb3[s][:, k * 128:(k + 1) * 128], identity=identb[:])
                return ins
            A("pe", tr2, reads=[Rxn3[s], Rc], writes=[Rpt[1]])
            A("dve", lambda e: e.tensor_tensor(out=modtmp3[s].rearrange("p (k c) -> p k c", k=8),
                                               in0=pts[1][:, :].rearrange("p (k c) -> p k c", k=8),
                                               in1=prm[:, 4, :].unsqueeze(2).to_broadcast([128, 8, 128]), op=ALU.mult),
              reads=[Rpt[1], Rprm], writes=[Rmt3[s]])
            A("dve", lambda e: e.tensor_tensor(out=xT[:, :, (t + 2) * 128:(t + 3) * 128],
                                               in0=modtmp3[s].rearrange("p (k c) -> p k c", k=8),
                                               in1=prm[:, 5, :].unsqueeze(2).to_broadcast([128, 8, 128]), op=ALU.add),
              reads=[Rmt3[s], Rprm], writes=[RxT[t + 2]])

        for i in range(34):
            if i < 32:
                p3_A1(i)
            if 0 <= i - 1 < 32:
                p3_A2(i - 1)
            if 0 <= i - 2 < 32:
                p3_B(i - 2)
            if 0 <= i - 1 < 32:
                p3_A2b(i - 1)

        reset_arenas()
        G2 = fa(1024)
        gtmp = [fa(512) for _ in range(2)]
        Rgt = [Res() for _ in range(2)]
        wd = ba(22 * 1024).rearrange("p (k n) -> p k n", n=1024)
        Rwd = Res()
        HT = ba(22 * 512).rearrange("p (j t) -> p j t", t=512)
        RHT = Res()
        UP = [ba(10 * 66).rearrange("p (r c) -> p r c", c=66) for _ in range(2)]
        RUP = [Res() for _ in range(2)]
        WU = [ba(8 * 256).rearrange("p (k n) -> p k n", n=256) for _ in range(2)]
        RWU = [Res() for _ in range(2)]
        DGf = [ba(18 * 128).rearrange("p (w t n) -> p w t n", w=2, n=128) for _ in range(2)]
        RDGf = [Res() for _ in range(2)]
        fcwt = fa(44 * 9).rearrange("p (c t) -> p c t", t=9)
        fnwt = fa(1024)
        Rfc = Res()
        sat = fa(512)
        Rsa = Res()
        x1t = [fa(1024) for _ in range(4)]
        Rx1t = [Res() for _ in range(4)]
        HALO = fa(2816).bitcast(BF16).rearrange("p (j a r c) -> p j a r c", j=22, a=2, r=2)
        RHALO = [[Res() for _ in range(2)] for _ in range(22)]
        junk4 = fa(1024)
        Rjunk4 = Res()
        ssb4 = fa(8)
        Rss4 = [Res() for _ in range(4)]
        A("sp", lambda e: e.dma_start(out=fcwt, in_=D["fcw"][:, :, :]), writes=[Rfc], dma="fcw")
        A("sp", lambda e: e.dma_start(out=fnwt, in_=D["fnw"][:, :]), writes=[Rfc], dma="fnw")
        for ab in range(2):
            A("pool", lambda e, ab=ab: e.memset(UP[ab], 0.0), writes=[RUP[ab]])
        w_up_v = D["w_up"].rearrange("(k p) n -> p k n", p=128)
        it = 0
        NB = 8
        for blk in range(NB):
            R0 = 8 * blk
            if blk == NB - 1:
                for ab in range(2):
                    A("pool", lambda e, ab=ab: e.memset(UP[ab][:, 9, :], 0.0), writes=[RUP[ab]])
            for tt in range(4):
                gt = blk * 4 + tt
                A("sp", lambda e, gt=gt, tt=tt: e.dma_start(out=x1t[tt], in_=x1sc[gt * 128:(gt + 1) * 128, :]), reads=[Rx1[gt]],
                  writes=[Rx1t[tt]], dma=f"x1r{tt}")
            for j in range(22):
                ws = it % 2
                it += 1
                if blk > 0:
                    for ab in range(2):
                        A("dve", lambda e, ab=ab, j=j: e.tensor_copy(out=UP[ab][:, 0:2, 1:65], in_=HALO[:, j, ab]),
                          reads=[RHALO[j][ab]], writes=[RUP[ab]])
                if blk == 0:
                    A("pool", lambda e, j=j: e.dma_start(out=wd[:, j, :], in_=D["w_down"][j * 128:(j + 1) * 128, :]), writes=[Rwd],
                      dma="wdld")
                for ab in range(2):
                    c0 = ab * 2816 + j * 128
                    A("pool", lambda e, ws=ws, ab=ab, c0=c0: e.dma_start(out=WU[ws][:, :, ab * 128:(ab + 1) * 128],
                                                                          in_=w_up_v[:, :, c0:c0 + 128]),
                      writes=[RWU[ws]], dma=f"wu{ws}{ab}")
                    ch = ab * 22 + j
                    A("dve", lambda e, ws=ws, ab=ab, ch=ch: e.tensor_tensor(
                        out=DGf[ws][:, ab], in0=identb[:].unsqueeze(1).to_broadcast([128, 9, 128]),
                        in1=fcwt[:, ch, :].unsqueeze(2).to_broadcast([128, 9, 128]), op=ALU.mult),
                      reads=[Rfc, Rc], writes=[RDGf[ws]])
                if blk == 0:
                    groups = ((1, 8), (9, 1))
                elif blk == NB - 1:
                    groups = ((2, 7),)
                else:
                    groups = ((2, 8),)
                for ab in range(2):
                    for gi, (lo, nrow) in enumerate(groups):
                        tok0 = 256 + (R0 - 1 + lo) * 64
                        ntk = nrow * 64
                        bank = (ab + gi) % 2
                        tiles_needed = [RxT[tt] for tt in range(tok0 // 128, (tok0 + ntk + 127) // 128)]

                        def mmu(e, ws=ws, ab=ab, tok0=tok0, ntk=ntk, bank=bank):
                            for k in range(8):
                                ins = e.matmul(pbs[bank][:, 0:ntk], lhsT=WU[ws][:, k, ab * 128:(ab + 1) * 128],
                                               rhs=xT[:, k, tok0:tok0 + ntk], start=(k == 0), stop=(k == 7))
                            return ins
                        A("pe", mmu, reads=tiles_needed + [RWU[ws]], writes=[Rpb[bank]])
                        A("act", lambda e, ab=ab, lo=lo, nrow=nrow, ntk=ntk, bank=bank: e.activation(
                            out=UP[ab][:, lo:lo + nrow, 1:65], in_=pbs[bank][:, 0:ntk].rearrange("p (r c) -> p r c", c=64),
                            func=AF.Copy), reads=[Rpb[bank]], writes=[RUP[ab]])
                    if blk < NB - 1:
                        A("dve", lambda e, ab=ab, j=j: e.tensor_copy(out=HALO[:, j, ab], in_=UP[ab][:, 8:10, 1:65]),
                          reads=[RUP[ab]], writes=[RHALO[j][ab]])
                for ab in range(2):
                    bank = 2 + ab

                    def mmcf(e, ws=ws, ab=ab, bank=bank):
                        for tap in range(9):
                            dr, dc = tap // 3, tap % 3
                            ins = e.matmul(pbs[bank][:, 0:512].rearrange("p (r c) -> p r c", c=64),
                                           lhsT=DGf[ws][:, ab, tap, :],
                                           rhs=UP[ab][:, dr:dr + 8, dc:dc + 64],
                                           start=(tap == 0), stop=(tap == 8))
                        return ins
                    A("pe", mmcf, reads=[RUP[ab], RDGf[ws]], writes=[Rpb[bank]])
                A("act", lambda e: e.activation(out=sat, in_=pbs[2][:, 0:512], func=AF.Silu), reads=[Rpb[2]], writes=[Rsa])
                A("dve", lambda e, j=j: e.tensor_tensor(out=HT[:, j, :], in0=sat, in1=pbs[3][:, 0:512], op=ALU.mult),
                  reads=[Rsa, Rpb[3]], writes=[RHT])
            for tt in range(4):
                gt = blk * 4 + tt
                s = tt
                for half in range(2):
                    bank = 4 + half

                    def mmd(e, tt=tt, half=half, bank=bank):
                        for j in range(22):
                            ins = e.matmul(pbs[bank][:, 0:512], lhsT=HT[:, j, tt * 128:(tt + 1) * 128],
                                           rhs=wd[:, j, half * 512:(half + 1) * 512], start=(j == 0), stop=(j == 21))
                        return ins
                    A("pe", mmd, reads=[RHT, Rwd], writes=[Rpb[bank]])
                    A("dve", lambda e, half=half, bank=bank: e.tensor_tensor(
                        out=gtmp[half], in0=pbs[bank][:, 0:512], in1=G2[:, half * 512:(half + 1) * 512], op=ALU.mult),
                      reads=[Rpb[bank]], writes=[Rgt[half]])
                    A("dve", lambda e, s=s, half=half: e.tensor_tensor(
                        out=x1t[s][:, half * 512:(half + 1) * 512], in0=x1t[s][:, half * 512:(half + 1) * 512],
                        in1=gtmp[half], op=ALU.add), reads=[Rgt[half], Rx1t[s]], writes=[Rx1t[s]])
                s4 = gt % 4
                ssv = ssb4[:, s4:s4 + 1]
                A("act", lambda e, s=s, ssv=ssv: e.activation(out=junk4, in_=x1t[s], func=AF.Square, scale=1.0 / 32.0, accum_out=ssv),
                  reads=[Rx1t[s]], writes=[Rjunk4, Rss4[s4]])
                A("act", lambda e, ssv=ssv: e.activation(out=ssv, in_=ssv, func=AF.Ln, bias=EPS), reads=[Rss4[s4]], writes=[Rss4[s4]])
                A("act", lambda e, ssv=ssv: e.activation(out=ssv, in_=ssv, func=AF.Exp, scale=-0.5), reads=[Rss4[s4]], writes=[Rss4[s4]])
                A("dve", lambda e, s=s, ssv=ssv: e.scalar_tensor_tensor(out=x1t[s], in0=x1t[s], scalar=ssv, in1=fnwt,
                                                                        op0=ALU.mult, op1=ALU.mult),
                  reads=[Rss4[s4], Rx1t[s], Rfc], writes=[Rx1t[s]])
                A("sp", lambda e, gt=gt, s=s: e.dma_start(out=out[gt * 128:(gt + 1) * 128, :], in_=x1t[s]), reads=[Rx1t[s]], writes=[],
                  dma=f"out{s}")
        P.barrier()
        P.emit(st)
    return nc


def _consts():
    u = np.arange(128)[:, None]
    t = np.arange(128)[None, :]
    cf = np.zeros((8, 128, 128), np.float32)
    cf[0] = np.eye(128)
    cf[1] = 1.0
    cf[2] = (u <= t)
    cf[3] = (u >= t)
    cf[4] = (u > t)
    cf[5] = (u < t)
    cf[6] = (u <= t).astype(np.float32) - (u <= 63).astype(np.float32)
    cf[7] = (u >= t).astype(np.float32) - (u >= 64).astype(np.float32)
    cwd = np.zeros((128, 4), np.float32)
    uu = np.arange(128)
    cwd[:, 0] = uu > 63
    cwd[:, 1] = uu <= 63
    cwd[:, 2] = uu < 64
    cwd[:, 3] = uu >= 64
    return np.ascontiguousarray(cf.transpose(1, 0, 2)), cwd


_NC_CACHE = {}


def kernel(x, c, ctx, c_ctx, w_mod, b_mod, norm1_w, w_in, mlstm_gate_b, mlstm_conv_w, mlstm_norm_w,
           hgrn_lb_logits, hgrn_norm_w, w_out, norm2_w, w_up, ffn_conv_w, w_down, final_norm_w):
    f = lambda a: np.ascontiguousarray(np.asarray(a, dtype=np.float32))
    x, c, ctx, c_ctx = f(x), f(c), f(ctx), f(c_ctx)
    cf, cwd = _consts()
    shared = {
        "w_mod": f(w_mod[0]),
        "b_modT": f(np.asarray(b_mod[0]).reshape(48, 128).T),
        "n1T": f(np.asarray(norm1_w[0]).reshape(8, 128).T),
        "n2T": f(np.asarray(norm2_w[0]).reshape(8, 128).T),
        "fnw": f(np.broadcast_to(np.asarray(final_norm_w)[None, :], (128, 1024))),
        "w_in": f(w_in[0]),
        "gate_b": f(np.broadcast_to(np.asarray(mlstm_gate_b[0])[None, :], (128, 16))),
        "mcw": f(np.asarray(mlstm_conv_w[0]).reshape(9, 8, 128).transpose(2, 1, 0)),
        "fcw": f(np.asarray(ffn_conv_w[0]).reshape(9, 44, 128).transpose(2, 1, 0)),
        "nwT": f(np.concatenate([np.asarray(mlstm_norm_w[0]), np.asarray(hgrn_norm_w[0])]).reshape(8, 128).T),
        "lbl": f(np.broadcast_to(np.asarray(hgrn_lb_logits)[None], (128, 2, 2, 512))),
        "w_out": f(w_out[0]),
        "w_up": f(w_up[0]),
        "w_down": f(w_down[0]),
        "cf32": cf,
        "cwd": cwd,
        "identb": np.eye(128).astype(ml_dtypes.bfloat16),
    }
    in_maps = []
    for b in range(8):
        m = dict(shared)
        m["xin"] = np.ascontiguousarray(np.concatenate([ctx[b], x[b]], axis=0))
        m["c2"] = np.ascontiguousarray(np.stack([c[b], c_ctx], axis=1).reshape(8, 128, 2).transpose(1, 0, 2))
        in_maps.append(m)
    if "nc" not in _NC_CACHE:
        _NC_CACHE["nc"] = build_program()
    res = run_bass_kernel_spmd(_NC_CACHE["nc"], in_maps, core_ids=list(range(8)))
    return np.stack([np.asarray(r["out"], dtype=np.float32) for r in res.results], axis=0)
```

```python
import math
from contextlib import ExitStack

import numpy as np
import ml_dtypes
import concourse.bass as bass
import concourse.mybir as mybir
from concourse.bass_utils import run_bass_kernel_spmd

F32 = mybir.dt.float32
BF16 = mybir.dt.bfloat16
AF = mybir.ActivationFunctionType
ALU = mybir.AluOpType
AX = mybir.AxisListType

ENGS = ("pe", "act", "dve", "pool", "sp")
SIG_ROT = 6000

NT = 34
NTOK = 4352
EPS = 1e-6
LN_SQRT128 = 0.5 * math.log(128.0)
QS = 128.0 ** -0.5


class Res:
    __slots__ = ("name", "last_w", "rd_eng", "rd_dma")

    def __init__(self, name=""):
        self.name = name
        self.last_w = None
        self.rd_eng = {}
        self.rd_dma = []


class Op:
    __slots__ = ("eng", "fn", "deps", "is_dma", "dkey", "dval", "sig", "nsig")

    def __init__(self, eng, fn, is_dma):
        self.eng = eng
        self.fn = fn
        self.deps = []
        self.is_dma = is_dma
        self.dkey = None
        self.dval = 0
        self.sig = None
        self.nsig = False


class Prog:
    def __init__(self, nc):
        self.nc = nc
        self.ops = {e: [] for e in ENGS}
        self.dma_cnt = {}
        self.last_dma = {}

    def add(self, eng, fn, reads=(), writes=(), dma=None):
        op = Op(eng, fn, dma is not None)
        deps = {}
        for r in reads:
            if r.last_w is not None:
                deps[id(r.last_w)] = (r.last_w, True)
        for w in writes:
            if w.last_w is not None and id(w.last_w) not in deps:
                deps[id(w.last_w)] = (w.last_w, False)
            for rd in list(w.rd_eng.values()) + w.rd_dma:
                if id(rd) not in deps:
                    deps[id(rd)] = (rd, False)
        for (p, raw) in deps.values():
            if p.eng == eng and not p.is_dma:
                if eng == "pe" or not raw:
                    continue
            op.deps.append(p)
            p.nsig = True
        for r in reads:
            if op.is_dma:
                r.rd_dma.append(op)
            else:
                r.rd_eng[eng] = op
        for w in writes:
            w.last_w = op
            w.rd_eng = {}
            w.rd_dma = []
        if dma is not None:
            c = self.dma_cnt.get(dma, 0) + 1
            self.dma_cnt[dma] = c
            op.dkey = dma
            op.dval = 16 * c
            self.last_dma[dma] = op
        self.ops[eng].append(op)
        return op

    def barrier(self):
        lasts = []
        for e in ENGS:
            for op in reversed(self.ops[e]):
                if not op.is_dma and op.fn is not None:
                    lasts.append(op)
                    break
        lasts += list(self.last_dma.values())
        for e in ENGS:
            op = Op(e, None, False)
            for p in lasts:
                if p.eng == e and not p.is_dma:
                    continue
                op.deps.append(p)
                p.nsig = True
            self.ops[e].append(op)

    def emit(self, stack):
        nc = self.nc
        nsems = {}
        for e in ENGS:
            cnt = 0
            for op in self.ops[e]:
                if op.is_dma or not op.nsig:
                    continue
                op.sig = (cnt // SIG_ROT, cnt % SIG_ROT + 1)
                cnt += 1
            nsems[e] = (cnt + SIG_ROT - 1) // SIG_ROT
        sems = {}
        for e in ENGS:
            for s in range(nsems[e]):
                sems[(e, s)] = stack.enter_context(nc.semaphore(f"s_{e}_{s}"))
        dsems = {}
        for i, k in enumerate(self.dma_cnt.keys()):
            dsems[k] = stack.enter_context(nc.semaphore(f"d_{i}"))
        block = stack.enter_context(nc.Block())
        engobj = {"pe": block.tensor, "act": block.scalar, "dve": block.vector,
                  "pool": block.gpsimd, "sp": block.sync}

        def make(e):
            def body(eng):
                waited = {}
                for op in self.ops[e]:
                    for p in op.deps:
                        if p.is_dma:
                            key = ("d", p.dkey)
                            sem = dsems[p.dkey]
                            val = p.dval
                        else:
                            key = (p.eng, p.sig[0])
                            sem = sems[key]
                            val = p.sig[1]
                        if waited.get(key, 0) >= val:
                            continue
                        waited[key] = val
                        eng.wait_ge(sem, val)
                    if op.fn is None:
                        continue
                    ins = op.fn(eng)
                    if op.is_dma:
                        ins.then_inc(dsems[op.dkey], 16)
                    elif op.nsig:
                        ins.then_inc(sems[(e, op.sig[0])], 1)
            return body

        for e in ENGS:
            engobj[e](make(e))


def build_program():
    nc = bass.Bass("TRN2", target_bir_lowering=False)
    D = {}

    def din(name, shape, dt=F32):
        D[name] = nc.dram_tensor(name, list(shape), dt, kind="ExternalInput").ap()

    din("xin", [NTOK, 1024])
    din("c2", [128, 8, 2])
    din("w_mod", [1024, 6144])
    din("b_modT", [128, 48])
    din("n1T", [128, 8])
    din("n2T", [128, 8])
    din("fnw", [128, 1024])
    din("w_in", [1024, 4624])
    din("gate_b", [128, 16])
    din("mcw", [128, 8, 9])
    din("fcw", [128, 44, 9])
    din("nwT", [128, 8])
    din("lbl", [128, 2, 2, 512])
    din("w_out", [1024, 1024])
    din("w_up", [1024, 5632])
    din("w_down", [2816, 1024])
    din("cf32", [128, 8, 128])
    din("cwd", [128, 4])
    din("identb", [128, 128], BF16)
    out = nc.dram_tensor("out", [4096, 1024], F32, kind="ExternalOutput").ap()
    ysc = nc.dram_tensor("ysc", [4096, 1024], BF16, kind="Internal").ap()
    x1sc = nc.dram_tensor("x1sc", [4096, 1024], F32, kind="Internal").ap()

    with ExitStack() as st:
        P = Prog(nc)
        A = P.add

        def T(name, shape, dt):
            return st.enter_context(nc.sbuf_tensor(name, list(shape), dt))

        xT = T("xT", [128, 8, NTOK], BF16)
        RxT = [Res(f"xT{t}") for t in range(NT)]
        cf = T("cf", [128, 8, 128], F32)
        identf, onesf, maskf, maskb, SU, SL, Mdf, Mdb = [cf[:, i, :] for i in range(8)]
        cwd = T("cwd_sb", [128, 4], F32)
        identb = T("identb_sb", [128, 128], BF16)
        Rc = Res("consts")
        modT = T("modT", [128, 48, 2], F32)
        prm = T("prm", [128, 6, 8], F32)
        Rprm = Res("prm")
        n12 = T("n12", [128, 2, 8], F32)
        bmT = T("bmT", [128, 48], F32)
        FA = T("FA", [128, 12400], F32)
        BA = T("BA", [128, 44000], BF16)
        pbs = [st.enter_context(nc.psum_tensor(f"pb{i}", [128, 512], F32)) for i in range(8)]
        pts = [pbs[6 + i][:, :].bitcast(BF16) for i in range(2)]
        Rpb = [Res(f"pb{i}") for i in range(8)]
        Rpt = [Rpb[6], Rpb[7]]

        fa_off = [0]
        ba_off = [0]

        def fa(n, shape=None):
            o = fa_off[0]
            fa_off[0] += n
            assert fa_off[0] <= 12400, fa_off[0]
            v = FA[:, o:o + n]
            return v

        def ba(n):
            o = ba_off[0]
            ba_off[0] += n
            assert ba_off[0] <= 44000, ba_off[0]
            return BA[:, o:o + n]

        def reset_arenas():
            P.barrier()
            fa_off[0] = 0
            ba_off[0] = 0

        A("sp", lambda e: e.dma_start(out=cf[:], in_=D["cf32"][:, :, :]), writes=[Rc], dma="c0")
        A("sp", lambda e: e.dma_start(out=cwd[:], in_=D["cwd"][:, :]), writes=[Rc], dma="c1")
        A("sp", lambda e: e.dma_start(out=identb[:], in_=D["identb"][:, :]), writes=[Rc], dma="c2")
        A("sp", lambda e: e.dma_start(out=n12[:, 0, :], in_=D["n1T"][:, :]), writes=[Rprm], dma="c3")
        A("sp", lambda e: e.dma_start(out=n12[:, 1, :], in_=D["n2T"][:, :]), writes=[Rprm], dma="c4")
        A("sp", lambda e: e.dma_start(out=bmT[:], in_=D["b_modT"][:, :]), writes=[Rprm], dma="c5")

        c2t = fa(16).rearrange("p (k m) -> p k m", m=2)
        sct = fa(16).rearrange("p (k m) -> p k m", m=2)
        tmpc = fa(16).rearrange("p (k m) -> p k m", m=2)
        Rc2 = Res()
        A("sp", lambda e: e.dma_start(out=c2t, in_=D["c2"][:, :, :]), writes=[Rc2], dma="c6")
        A("act", lambda e: e.activation(out=tmpc, in_=c2t, func=AF.Exp, scale=-1.0), reads=[Rc2], writes=[Rc2])
        A("dve", lambda e: e.tensor_scalar_add(out=tmpc, in0=tmpc, scalar1=1.0), reads=[Rc2], writes=[Rc2])
        A("dve", lambda e: e.reciprocal(out=tmpc, in_=tmpc), reads=[Rc2], writes=[Rc2])
        A("dve", lambda e: e.tensor_tensor(out=sct, in0=c2t, in1=tmpc, op=ALU.mult), reads=[Rc2], writes=[Rc2])
        wm = [ba(4096).rearrange("p (k n) -> p k n", n=512) for _ in range(3)]
        Rwm = [Res() for _ in range(3)]
        sctb = ba(16).rearrange("p (k m) -> p k m", m=2)
        A("dve", lambda e: e.tensor_copy(out=sctb, in_=sct), reads=[Rc2], writes=[Rc2])
        w_mod_v = D["w_mod"].rearrange("(k p) n -> p k n", p=128)
        for jj in range(12):
            s = jj % 3
            A("pool", lambda e, jj=jj, s=s: e.dma_start(out=wm[s], in_=w_mod_v[:, :, jj * 512:(jj + 1) * 512]),
              writes=[Rwm[s]], dma=f"wm{s}")

            def mm(e, jj=jj, s=s):
                for q in range(4):
                    j = jj * 4 + q
                    for k in range(8):
                        ins = e.matmul(pbs[0][:, 2 * j:2 * j + 2], lhsT=wm[s][:, k, q * 128:(q + 1) * 128], rhs=sctb[:, k, :],
                                       start=(k == 0), stop=(k == 7))
                return ins
            A("pe", mm, reads=[Rwm[s], Rc2], writes=[Rpb[0]])
        A("dve", lambda e: e.tensor_tensor(out=modT[:], in0=pbs[0][:, 0:96].rearrange("p (j m) -> p j m", m=2),
                                           in1=bmT[:].unsqueeze(2).to_broadcast([128, 48, 2]), op=ALU.add),
          reads=[Rpb[0], Rprm], writes=[Rprm])
        for (pi, nidx, scj, col) in ((0, 0, 8, 0), (2, 0, 8, 1), (4, 1, 32, 0)):
            A("dve", lambda e, pi=pi, nidx=nidx, scj=scj, col=col: e.scalar_tensor_tensor(
                out=prm[:, pi, :], in0=modT[:, scj:scj + 8, col], scalar=1.0, in1=n12[:, nidx, :],
                op0=ALU.add, op1=ALU.mult), reads=[Rprm], writes=[Rprm])
        for (pi, shj, col) in ((1, 0, 0), (3, 0, 1), (5, 24, 0)):
            A("dve", lambda e, pi=pi, shj=shj, col=col: e.tensor_copy(out=prm[:, pi, :], in_=modT[:, shj:shj + 8, col]),
              reads=[Rprm], writes=[Rprm])

        xts = [fa(1024) for _ in range(3)]
        Rxt = [Res() for _ in range(3)]
        junk = fa(1024)
        Rjunk = Res()
        ssb = fa(8)
        Rss = [Res() for _ in range(4)]
        xnb = [ba(1024) for _ in range(2)]
        Rxn = [Res() for _ in range(2)]
        modtmp = [fa(1024) for _ in range(2)]
        Rmt = [Res() for _ in range(2)]

        def p1_stageA(t):
            s = t % 3
            s4 = t % 4
            s2 = t % 2
            ssv = ssb[:, s4:s4 + 1]
            A("sp", lambda e: e.dma_start(out=xts[s], in_=D["xin"][t * 128:(t + 1) * 128, :]), writes=[Rxt[s]], dma=f"xt{s}")
            A("act", lambda e: e.activation(out=junk, in_=xts[s], func=AF.Square, scale=1.0 / 32.0, accum_out=ssv),
              reads=[Rxt[s]], writes=[Rjunk, Rss[s4]])
            A("act", lambda e: e.activation(out=ssv, in_=ssv, func=AF.Ln, bias=EPS), reads=[Rss[s4]], writes=[Rss[s4]])
            A("act", lambda e: e.activation(out=ssv, in_=ssv, func=AF.Exp, scale=-0.5), reads=[Rss[s4]], writes=[Rss[s4]])
            A("act", lambda e: e.activation(out=xnb[s2], in_=xts[s], func=AF.Copy, scale=ssv),
              reads=[Rss[s4], Rxt[s]], writes=[Rxn[s2]])

        def p1_stageB(t):
            s2 = t % 2
            pa, psh = (2, 3) if t < 2 else (0, 1)

            def tr(e):
                for k in range(8):
                    ins = e.transpose(out=pts[s2][:, k * 128:(k + 1) * 128], in_=xnb[s2][:, k * 128:(k + 1) * 128],
                                      identity=identb[:])
                return ins
            A("pe", tr, reads=[Rxn[s2], Rc], writes=[Rpt[s2]])

            A("dve", lambda e: e.tensor_tensor(out=modtmp[s2].rearrange("p (k c) -> p k c", k=8),
                                               in0=pts[s2][:, :].rearrange("p (k c) -> p k c", k=8),
                                               in1=prm[:, pa, :].unsqueeze(2).to_broadcast([128, 8, 128]), op=ALU.mult),
              reads=[Rpt[s2], Rprm], writes=[Rmt[s2]])
            A("dve", lambda e: e.tensor_tensor(out=xT[:, :, t * 128:(t + 1) * 128],
                                               in0=modtmp[s2].rearrange("p (k c) -> p k c", k=8),
                                               in1=prm[:, psh, :].unsqueeze(2).to_broadcast([128, 8, 128]), op=ALU.add),
              reads=[Rmt[s2], Rprm], writes=[RxT[t]])

        for i in range(NT + 1):
            if i < NT:
                p1_stageA(i)
            if i >= 1:
                p1_stageB(i - 1)

        reset_arenas()
        WK = fa(NT * 8).rearrange("p (t g) -> p t g", g=8)
        FL = fa(NT * 8).rearrange("p (t g) -> p t g", g=8)
        DC = fa(NT * 8).rearrange("p (t g) -> p t g", g=8)
        Rgs = Res("gatescal")
        gbt = fa(16)
        Rgb = Res()
        A("sp", lambda e: e.dma_start(out=gbt, in_=D["gate_b"][:, :]), writes=[Rgb], dma="gb")
        wg = ba(128).rearrange("p (k n) -> p k n", n=16)
        Rwg = Res()
        w_in_v = D["w_in"].rearrange("(k p) n -> p k n", p=128)
        A("pool", lambda e: e.dma_start(out=wg, in_=w_in_v[:, :, 2048:2064]), writes=[Rwg], dma="wg")
        gps = [fa(16) for _ in range(2)]
        nls = [fa(16) for _ in range(2)]
        tm8 = [fa(8) for _ in range(2)]
        Rgp = [Res() for _ in range(2)]
        def p2a_A(t):
            s = t % 2
            b0 = 0 if s == 0 else 2

            def mmg(e):
                for k in range(8):
                    ins = e.matmul(pbs[b0][:, 0:16], lhsT=xT[:, k, t * 128:(t + 1) * 128], rhs=wg[:, k, :],
                                   start=(k == 0), stop=(k == 7))
                return ins
            A("pe", mmg, reads=[RxT[t], Rwg], writes=[Rpb[b0]])
            A("dve", lambda e: e.tensor_tensor(out=gps[s], in0=pbs[b0][:, 0:16], in1=gbt, op=ALU.add),
              reads=[Rpb[b0], Rgb], writes=[Rgp[s]])
            A("act", lambda e: e.activation(out=nls[s], in_=gps[s], func=AF.Exp, scale=-1.0),
              reads=[Rgp[s]], writes=[Rnl[s]])
            A("act", lambda e: e.activation(out=nls[s], in_=nls[s], func=AF.Ln, bias=1.0),
              reads=[Rnl[s]], writes=[Rnl[s]])

        def p2a_B(t):
            s = t % 2
            b1 = 1 if s == 0 else 3

            def mmc(e):
                e.matmul(pbs[b1][:, 0:16], lhsT=SU, rhs=nls[s], start=True, stop=True)
                e.matmul(pbs[b1][:, 16:32], lhsT=SL, rhs=nls[s], start=True, stop=True)
                return e.matmul(pbs[b1][:, 32:48], lhsT=onesf, rhs=nls[s], start=True, stop=True)
            A("pe", mmc, reads=[Rnl[s], Rc], writes=[Rpb[b1]])
            A("dve", lambda e: e.tensor_tensor(out=tm8[s][:, 0:4], in0=gps[s][:, 0:4], in1=pbs[b1][:, 4:8],
                                               op=ALU.subtract), reads=[Rgp[s], Rpb[b1]], writes=[Rtm[s]])
            A("dve", lambda e: e.tensor_tensor(out=tm8[s][:, 4:8], in0=gps[s][:, 8:12], in1=pbs[b1][:, 28:32],
                                               op=ALU.subtract), reads=[Rgp[s], Rpb[b1]], writes=[Rtm[s]])
            A("act", lambda e: e.activation(out=WK[:, t, :], in_=tm8[s], func=AF.Exp),
              reads=[Rtm[s]], writes=[Rgs])
            A("act", lambda e: e.activation(
                out=FL[:, t, :].rearrange("p (a b) -> p a b", b=4),
                in_=pbs[b1][:, 4:52].rearrange("p (a b) -> p a b", b=24)[:, :, 0:4],
                func=AF.Exp, scale=-1.0, bias=LN_SQRT128), reads=[Rpb[b1]], writes=[Rgs])
            A("act", lambda e: e.activation(
                out=DC[:, t, :].rearrange("p (a b) -> p a b", b=4),
                in_=pbs[b1][:, 36:52].rearrange("p (a b) -> p a b", b=8)[:, :, 0:4],
                func=AF.Exp, scale=-1.0, bias=0.0), reads=[Rpb[b1]], writes=[Rgs])

        Rnl = [Res() for _ in range(2)]
        Rtm = [Res() for _ in range(2)]
        for i in range(NT + 1):
            if i < NT:
                p2a_A(i)
            if i >= 1:
                p2a_B(i - 1)

        RING = 4
        LA = 2

        def alloc_scan(Nv):
            return {
                "Z": [[fa(Nv) for _ in range(RING)] for _ in range(2)],
                "U": [[fa(Nv) for _ in range(RING)] for _ in range(2)],
                "S": [[ba(Nv) for _ in range(RING)] for _ in range(2)],
                "M": [[ba(128) for _ in range(RING)] for _ in range(2)],
                "RZ": [[Res() for _ in range(RING)] for _ in range(2)],
                "RU": [[Res() for _ in range(RING)] for _ in range(2)],
                "RS": [[Res() for _ in range(RING)] for _ in range(2)],
                "RM": [[Res() for _ in range(RING)] for _ in range(2)],
            }

        scan_bufs = alloc_scan(129)
        order_f = list(range(NT))
        order_b = [1, 0] + list(range(NT - 1, 1, -1))

        def run_scans(qT, kT, ktok, vt, Nv, Gfn, Pevac, Rin, Rg):
            sb = scan_bufs
            LA1 = 1
            for it_ in range(NT + LA1):
                m = it_
                if m < NT:
                    for d in range(2):
                        tl = (order_f if d == 0 else order_b)[m]
                        r = m % RING
                        bsc = 0 if d == 0 else 3
                        bU = ((2, 6) if d == 0 else (5, 7))[m % 2]
                        mask = maskf if d == 0 else maskb
                        if tl >= 2:
                            A("pe", lambda e, d=d, tl=tl, bsc=bsc: e.matmul(pbs[bsc][:, 0:128], lhsT=kT(d, tl), rhs=qT(d, tl),
                                                                            start=True, stop=True),
                              reads=Rin(tl), writes=[Rpb[bsc]])
                            A("dve", lambda e, d=d, r=r, bsc=bsc, mask=mask: e.tensor_tensor(
                                out=sb["M"][d][r], in0=pbs[bsc][:, 0:128], in1=mask, op=ALU.mult),
                              reads=[Rpb[bsc], Rc], writes=[sb["RM"][d][r]])
                        A("pe", lambda e, d=d, tl=tl, bU=bU: e.matmul(pbs[bU][:, 0:Nv], lhsT=ktok(d, tl), rhs=vt(d, tl),
                                                                      start=True, stop=True),
                          reads=Rin(tl), writes=[Rpb[bU]])
                m = it_ - LA1
                if m >= 0:
                    for d in range(2):
                        tl = (order_f if d == 0 else order_b)[m]
                        r = m % RING
                        rp = (m - 1) % RING
                        rn = (m + 1) % RING
                        bP = 1 if d == 0 else 4
                        bU = ((2, 6) if d == 0 else (5, 7))[m % 2]
                        if m > 0:
                            gprev = Gfn(d, m - 1)
                            A("dve", lambda e, d=d, r=r, rp=rp, gprev=gprev, bU=bU: e.scalar_tensor_tensor(
                                out=sb["Z"][d][r], in0=sb["Z"][d][rp], scalar=gprev, in1=pbs[bU][:, 0:Nv],
                                op0=ALU.mult, op1=ALU.add),
                              reads=[sb["RZ"][d][rp], Rpb[bU]] + Rg, writes=[sb["RZ"][d][r]])
                        else:
                            A("dve", lambda e, d=d, r=r, bU=bU: e.tensor_copy(out=sb["Z"][d][r], in_=pbs[bU][:, 0:Nv]),
                              reads=[Rpb[bU]], writes=[sb["RZ"][d][r]])
                        if tl >= 2:
                            def mmP(e, d=d, tl=tl, bP=bP, m=m, r=r):
                                ins = e.matmul(pbs[bP][:, 0:Nv], lhsT=sb["M"][d][r], rhs=vt(d, tl), start=True, stop=(m == 0))
                                if m > 0:
                                    ins = e.matmul(pbs[bP][:, 0:Nv], lhsT=qT(d, tl), rhs=sb["S"][d][r], start=False, stop=True)
                                return ins
                            A("pe", mmP, reads=[sb["RM"][d][r], sb["RS"][d][r]] + Rin(tl), writes=[Rpb[bP]])
                            Pevac(d, tl, pbs[bP][:, 0:Nv], Rpb[bP])
                        if m < NT - 1:
                            gthis = Gfn(d, m)
                            A("act", lambda e, d=d, r=r, rn=rn, gthis=gthis: e.activation(
                                out=sb["S"][d][rn], in_=sb["Z"][d][r], func=AF.Copy, scale=gthis),
                              reads=[sb["RZ"][d][r]] + Rg, writes=[sb["RS"][d][rn]])

        PAD = ba(66 * 66).rearrange("p (r c) -> p r c", c=66)
        PADC = ba(258)
        RPAD = Res()
        A("pool", lambda e: e.memset(PAD, 0.0), writes=[RPAD])
        A("pool", lambda e: e.memset(PADC, 0.0), writes=[RPAD])
        qkT = [ba(NTOK) for _ in range(2)]
        KTOK = ba(NT * 128).rearrange("p (t c) -> p t c", c=128)
        VT = [ba(NT * 129).rearrange("p (t c) -> p t c", c=129) for _ in range(2)]
        OGs = [ba(32 * 128).rearrange("p (t c) -> p t c", c=128) for _ in range(2)]
        W4 = [ba(4 * 8 * 128).rearrange("p (w k n) -> p w k n", w=4, n=128)] * 2
        DGm = ba(2 * 9 * 128).rearrange("p (w t n) -> p w t n", w=2, n=128)
        PST = [fa(32 * 129).rearrange("p (t c) -> p t c", c=129) for _ in range(2)]
        mcwt = fa(72).rearrange("p (c t) -> p c t", t=9)
        sqjM = fa(128)
        dent = [fa(32) for _ in range(2)]
        ssqM = fa(32)
        Rmcw = Res()
        RW4 = [Res()] * 2
        RDG = Res()
        Rhd = Res("headdata")
        ROGs = [Res("og0"), Res("og1")]
        RssM = Res()
        RPST = Res()
        Ryst = Res()
        A("sp", lambda e: e.dma_start(out=mcwt, in_=D["mcw"][:, :, :]), writes=[Rmcw], dma="mcw")
        ysc_v = ysc.rearrange("(t p) c -> p t c", p=128)

        def silu_evac(ps_ap, ps_view_fn, dst_ap, n, bank, scale=1.0):
            A("act", lambda e: e.activation(out=dst_ap, in_=ps_ap, func=AF.Silu), reads=[Rpb[bank]], writes=[Rhd])

        def fin_gen(h):
            OG = OGs[h % 2]
            ROG = ROGs[h % 2]
            for d in range(2):
                A("act", lambda e, d=d: e.activation(out=dent[d], in_=PST[d][:, :, 128], func=AF.Abs), reads=[RPST], writes=[Rdn[d]])
                yield
                A("dve", lambda e, d=d, h=h: e.tensor_tensor(out=dent[d], in0=dent[d], in1=FL[:, 2:NT, h + 4 * d], op=ALU.max),
                  reads=[Rdn[d], Rgs], writes=[Rdn[d]])
                A("dve", lambda e, d=d: e.reciprocal(out=dent[d], in_=dent[d]), reads=[Rdn[d]], writes=[Rdn[d]])
                yield
                A("dve", lambda e, d=d: e.tensor_tensor(out=PST[d][:, :, 0:128], in0=PST[d][:, :, 0:128],
                                                        in1=dent[d].unsqueeze(2).to_broadcast([128, 32, 128]), op=ALU.mult),
                  reads=[RPST, Rdn[d]], writes=[RPST])
                yield
            hs = PST[0][:, :, 0:128]
            A("dve", lambda e: e.tensor_tensor(out=hs, in0=hs, in1=PST[1][:, :, 0:128], op=ALU.add), reads=[RPST], writes=[RPST])
            yield
            for i in range(32):
                A("act", lambda e, i=i: e.activation(out=sqjM, in_=PST[0][:, i, 0:128], func=AF.Square, accum_out=ssqM[:, i:i + 1]),
                  reads=[RPST], writes=[RssM])
                if i % 4 == 3:
                    yield
            A("act", lambda e: e.activation(out=ssqM, in_=ssqM, func=AF.Ln, scale=1.0 / 128.0, bias=EPS), reads=[RssM], writes=[RssM])
            A("act", lambda e: e.activation(out=ssqM, in_=ssqM, func=AF.Exp, scale=-0.5, bias=math.log(0.5)), reads=[RssM], writes=[RssM])
            yield
            A("dve", lambda e: e.tensor_tensor(out=hs, in0=hs, in1=ssqM.unsqueeze(2).to_broadcast([128, 32, 128]), op=ALU.mult),
              reads=[RPST, RssM], writes=[RPST])
            yield
            A("dve", lambda e: e.scalar_tensor_tensor(out=OG, in0=OG, scalar=1.0, in1=hs, op0=ALU.add, op1=ALU.mult),
              reads=[RPST, ROG], writes=[ROG])
            A("sp", lambda e, h=h: e.dma_start(out=ysc_v[:, :, h * 128:(h + 1) * 128], in_=OG), reads=[ROG], writes=[],
              dma="yst")
            yield

        pending_fin = [None]
        Rdn = [Res(), Res()]

        def next_fin():
            g = pending_fin[0]
            if g is None:
                return
            try:
                next(g)
            except StopIteration:
                pending_fin[0] = None

        def drain_fin():
            while pending_fin[0] is not None:
                next_fin()

        for h in range(4):
            wslot = h % 2
            for wi, c0 in enumerate((h * 128, 512 + h * 128, 1024 + h * 128, 1536 + h * 128)):
                A("pool", lambda e, wi=wi, c0=c0, wslot=wslot: e.dma_start(out=W4[wslot][:, wi], in_=w_in_v[:, :, c0:c0 + 128]),
                  writes=[RW4[wslot]], dma=f"w4_{wslot}_{wi}")
            for wi in range(2):
                ch = h + 4 * wi
                A("dve", lambda e, wi=wi, ch=ch: e.tensor_tensor(
                    out=DGm[:, wi], in0=identb[:].unsqueeze(1).to_broadcast([128, 9, 128]),
                    in1=mcwt[:, ch, :].unsqueeze(2).to_broadcast([128, 9, 128]), op=ALU.mult),
                  reads=[Rmcw, Rc], writes=[RDG])
            def v_step(t, wslot=wslot, h=h):
                bank = 4 + t % 2

                def mmv(e):
                    for wi2, c0 in ((2, 0), (3, 128)):
                        if wi2 == 3 and t < 2:
                            continue
                        for k in range(8):
                            ins = e.matmul(pbs[bank][:, c0:c0 + 128], lhsT=xT[:, k, t * 128:(t + 1) * 128],
                                           rhs=W4[wslot][:, wi2, k, :], start=(k == 0), stop=(k == 7))
                    return ins
                A("pe", mmv, reads=[RxT[t], RW4[wslot]], writes=[Rpb[bank]])
                for d in range(2):
                    A("act", lambda e, d=d: e.activation(
                        out=VT[d][:, t, 0:128], in_=pbs[bank][:, 0:128], func=AF.Copy, scale=WK[:, t, h + 4 * d:h + 4 * d + 1]),
                      reads=[Rpb[bank], Rgs], writes=[Rhd])
                if t >= 2:
                    A("act", lambda e, OGh=OGs[h % 2]: e.activation(out=OGh[:, t - 2, :], in_=pbs[bank][:, 128:256], func=AF.Tanh, scale=0.5),
                      reads=[Rpb[bank]], writes=[ROGs[h % 2]])

            vcnt = [0]

            def next_v():
                if vcnt[0] < NT:
                    v_step(vcnt[0])
                    vcnt[0] += 1

            for wi in range(2):
                W = W4[wslot][:, wi]
                dstT = qkT[wi]
                for i in range(9):
                    bank = i % 2
                    if i == 0:
                        tok0, ntk = 0, 256
                    else:
                        tok0, ntk = 256 + (i - 1) * 512, 512
                    tiles_needed = [RxT[tt] for tt in range(tok0 // 128, (tok0 + ntk) // 128)]

                    def mmq(e, W=W, tok0=tok0, ntk=ntk, bank=bank):
                        for k in range(8):
                            ins = e.matmul(pbs[bank][:, 0:ntk], lhsT=W[:, k, :], rhs=xT[:, k, tok0:tok0 + ntk],
                                           start=(k == 0), stop=(k == 7))
                        return ins
                    A("pe", mmq, reads=tiles_needed + [RW4[wslot]], writes=[Rpb[bank]])
                    if i == 0:
                        A("act", lambda e, bank=bank: e.activation(out=PADC[:, 1:257], in_=pbs[bank][:, 0:256], func=AF.Copy),
                          reads=[Rpb[bank]], writes=[RPAD])
                    else:
                        r0 = 1 + 8 * (i - 1)
                        A("act", lambda e, bank=bank, r0=r0: e.activation(
                            out=PAD[:, r0:r0 + 8, 1:65], in_=pbs[bank][:, 0:512].rearrange("p (r c) -> p r c", c=64), func=AF.Copy),
                          reads=[Rpb[bank]], writes=[RPAD])
                    next_v()
                    next_fin()
                for i in range(9):
                    bank = 2 + i % 2
                    if i == 0:
                        def mmc0(e, wi=wi, bank=bank):
                            for j, tap in enumerate((3, 4, 5)):
                                ins = e.matmul(pbs[bank][:, 0:256], lhsT=DGm[:, wi, tap, :], rhs=PADC[:, j:j + 256],
                                               start=(j == 0), stop=(j == 2))
                            return ins
                        A("pe", mmc0, reads=[RPAD, RDG], writes=[Rpb[bank]])
                        silu_evac(pbs[bank][:, 0:256], None, dstT[:, 0:256], 256, bank)
                    else:
                        r0 = 8 * (i - 1)

                        def mmc1(e, wi=wi, bank=bank, r0=r0):
                            for tap in range(9):
                                dr, dc = tap // 3, tap % 3
                                ins = e.matmul(pbs[bank][:, 0:512].rearrange("p (r c) -> p r c", c=64),
                                               lhsT=DGm[:, wi, tap, :], rhs=PAD[:, r0 + dr:r0 + dr + 8, dc:dc + 64],
                                               start=(tap == 0), stop=(tap == 8))
                            return ins
                        A("pe", mmc1, reads=[RPAD, RDG], writes=[Rpb[bank]])
                        t0 = 256 + (i - 1) * 512
                        silu_evac(pbs[bank][:, 0:512], None, dstT[:, t0:t0 + 512], 512, bank)
                    next_v()
                    next_fin()
            while vcnt[0] < NT:
                next_v()
            drain_fin()
            for g in range(5):
                t0g = g * 8
                ng = min(8, NT - t0g)
                pslot = g % 2

                def trk(e, t0g=t0g, ng=ng, pslot=pslot):
                    for j in range(ng):
                        ins = e.transpose(out=pts[pslot][:, j * 128:(j + 1) * 128],
                                          in_=qkT[1][:, (t0g + j) * 128:(t0g + j + 1) * 128], identity=identb[:])
                    return ins
                A("pe", trk, reads=[Rhd, Rc], writes=[Rpt[pslot]])
                A("act", lambda e, t0g=t0g, ng=ng, pslot=pslot: e.activation(
                    out=KTOK[:, t0g:t0g + ng, :], in_=pts[pslot][:, 0:ng * 128].rearrange("p (t c) -> p t c", c=128), func=AF.Copy),
                  reads=[Rpt[pslot]], writes=[Rhd])
            for d in range(2):
                A("dve", lambda e, d=d, h=h: e.tensor_copy(out=VT[d][:, :, 128:129], in_=WK[:, :, h + 4 * d:h + 4 * d + 1]),
                  reads=[Rgs], writes=[Rhd])

            def Gm(d, n, h=h):
                tln = (order_f if d == 0 else order_b)[n + 1]
                return DC[:, tln, h + 4 * d:h + 4 * d + 1]

            def Pev(d, tl, ps_ap, bres):
                A("act", lambda e: e.activation(out=PST[d][:, tl - 2, :], in_=ps_ap, func=AF.Copy), reads=[bres], writes=[RPST])
            run_scans(lambda d, tl: qkT[0][:, tl * 128:(tl + 1) * 128], lambda d, tl: qkT[1][:, tl * 128:(tl + 1) * 128],
                      lambda d, tl: KTOK[:, tl, :], lambda d, tl: VT[d][:, tl, :], 129, Gm, Pev, lambda tl: [Rhd], [Rgs])

            pending_fin[0] = fin_gen(h)
        drain_fin()

        reset_arenas()
        scan_bufs = alloc_scan(128)
        QKT = ba(NT * 512).rearrange("p (t c) -> p t c", c=512)
        KHa = ba(NT * 256).rearrange("p (t d c) -> p t d c", d=2, c=128)
        VH = ba(NT * 128).rearrange("p (t c) -> p t c", c=128)
        GG = ba(32 * 128).rearrange("p (t c) -> p t c", c=128)
        W5 = [ba(5 * 8 * 128).rearrange("p (w k n) -> p w k n", w=5, n=128)] * 2
        qtok = [ba(256).rearrange("p (d c) -> p d c", d=2) for _ in range(2)]
        PSTh = fa(32 * 128).rearrange("p (t c) -> p t c", c=128)
        sqj = fa(128)
        GCb = fa(NT * 4).rearrange("p (t c) -> p t c", c=4)
        GT = [fa(NT) for _ in range(2)]
        LB = fa(256).rearrange("p (d c) -> p d c", d=2)
        OML = fa(256).rearrange("p (d c) -> p d c", d=2)
        sgb = [fa(512) for _ in range(2)]
        qsb = [fa(128) for _ in range(3)]
        fgt = [fa(256) for _ in range(2)]
        lft = [fa(256) for _ in range(2)]
        kkt = [fa(256) for _ in range(3)]
        ept = [fa(256) for _ in range(2)]
        emt = [fa(256) for _ in range(2)]
        ssqH = fa(32)
        lfh = [ba(256) for _ in range(2)]
        lfl = [ba(256) for _ in range(2)]
        Rlh = [Res() for _ in range(2)]
        Mdb16 = ba(256).rearrange("p (d c) -> p d c", d=2)
        cwd16 = ba(4)
        Rc16 = Res()
        A("dve", lambda e: e.tensor_copy(out=Mdb16[:, 0, :], in_=Mdf), reads=[Rc], writes=[Rc16])
        A("dve", lambda e: e.tensor_copy(out=Mdb16[:, 1, :], in_=Mdb), reads=[Rc], writes=[Rc16])
        A("dve", lambda e: e.tensor_copy(out=cwd16, in_=cwd[:]), reads=[Rc], writes=[Rc16])
        Rlb = Res()
        RW5 = [Res()] * 2
        RGG = [Res() for _ in range(32)]
        RVH = [Res() for _ in range(NT)]
        RKH = [Res() for _ in range(NT)]
        RQK = [Res() for _ in range(NT)]
        RGT = Res()
        Rsg = [Res() for _ in range(2)]
        Rqs = [Res() for _ in range(3)]
        Rfg = [Res() for _ in range(2)]
        Rlf = [Res() for _ in range(2)]
        Rkk = [Res() for _ in range(3)]
        Rex = [Res() for _ in range(2)]
        Rqk = [Res() for _ in range(2)]
        RGC = Res()
        RPh = [Res() for _ in range(32)]
        RPall = Res()

        HG0 = 2064
        for h in range(4):
            hsl = slice(h * 128, (h + 1) * 128)
            A("sp", lambda e, hsl=hsl: e.dma_start(out=OML[:], in_=D["lbl"][:, :, 0, hsl]), writes=[Rlb], dma="lbl0")
            A("sp", lambda e, hsl=hsl: e.dma_start(out=LB[:], in_=D["lbl"][:, :, 1, hsl]), writes=[Rlb], dma="lbl1")
            A("dve", lambda e: e.tensor_tensor(out=LB[:], in0=LB[:], in1=OML[:], op=ALU.subtract), reads=[Rlb], writes=[Rlb])
            A("act", lambda e: e.activation(out=LB[:], in_=LB[:], func=AF.Exp), reads=[Rlb], writes=[Rlb])
            A("dve", lambda e: e.tensor_scalar_add(out=LB[:], in0=LB[:], scalar1=1.0), reads=[Rlb], writes=[Rlb])
            A("dve", lambda e: e.reciprocal(out=LB[:], in_=LB[:]), reads=[Rlb], writes=[Rlb])
            A("act", lambda e: e.activation(out=OML[:], in_=LB[:], func=AF.Identity, scale=-1.0, bias=1.0), reads=[Rlb], writes=[Rlb])
            cols = [HG0 + h * 128, HG0 + 512 + h * 128, HG0 + 1024 + h * 128, HG0 + 2048 + h * 128, HG0 + 1536 + h * 128]
            for wi, c0 in enumerate(cols):
                A("pool", lambda e, wi=wi, c0=c0: e.dma_start(out=W5[0][:, wi], in_=w_in_v[:, :, c0:c0 + 128]),
                  writes=[RW5[0]], dma=f"w5_{wi}")

            def st_mm5(t):
                bA = t % 2
                bV = 4 + t % 2

                def mm5(e):
                    for k in range(8):
                        e.matmul(pbs[bA][:, 0:512].rearrange("p (w n) -> p w n", w=4),
                                 lhsT=xT[:, k, t * 128:(t + 1) * 128], rhs=W5[0][:, 0:4, k, :],
                                 start=(k == 0), stop=(k == 7))
                    for k in range(8):
                        ins = e.matmul(pbs[bV][:, 0:128], lhsT=xT[:, k, t * 128:(t + 1) * 128], rhs=W5[0][:, 4, k, :],
                                       start=(k == 0), stop=(k == 7))
                    return ins
                A("pe", mm5, reads=[RxT[t], RW5[0]], writes=[Rpb[bA], Rpb[bV]])

            def st_sig(t):
                par = t % 2
                bA = par
                sg = sgb[par]
                A("act", lambda e: e.activation(out=sg, in_=pbs[bA][:, 0:512], func=AF.Exp, scale=-1.0),
                  reads=[Rpb[bA]], writes=[Rsg[par]])
                A("act", lambda e: e.activation(out=sg, in_=sg, func=AF.Ln, bias=1.0), reads=[Rsg[par]], writes=[Rsg[par]])
                A("act", lambda e: e.activation(out=sg, in_=sg, func=AF.Exp, scale=-1.0), reads=[Rsg[par]], writes=[Rsg[par]])

            def st_dvea(t):
                par = t % 2
                q3 = t % 3
                bA = par
                bV = 4 + par
                fv = fgt[par].rearrange("p (d c) -> p d c", d=2)
                A("dve", lambda e: e.tensor_tensor(
                    out=fv, in0=sgb[par][:, 128:384].rearrange("p (d c) -> p d c", d=2), in1=OML[:], op=ALU.mult),
                  reads=[Rsg[par], Rlb], writes=[Rfg[par]])
                A("dve", lambda e: e.tensor_tensor(out=fv, in0=fv, in1=LB[:], op=ALU.add),
                  reads=[Rfg[par], Rlb], writes=[Rfg[par]])
                A("dve", lambda e: e.scalar_tensor_tensor(out=qsb[q3], in0=pbs[bA][:, 0:128], scalar=QS,
                                                          in1=sgb[par][:, 0:128], op0=ALU.mult, op1=ALU.mult),
                  reads=[Rpb[bA], Rsg[par]], writes=[Rqs[q3]])
                if t >= 2:
                    A("dve", lambda e: e.tensor_tensor(out=GG[:, t - 2, :], in0=pbs[bA][:, 384:512],
                                                       in1=sgb[par][:, 384:512], op=ALU.mult),
                      reads=[Rpb[bA], Rsg[par]], writes=[RGG[t - 2]])
                A("dve", lambda e: e.tensor_copy(out=VH[:, t, :], in_=pbs[bV][:, 0:128]),
                  reads=[Rpb[bV]], writes=[RVH[t]])

            def st_lnf(t):
                par = t % 2
                bE = 2 + par
                A("act", lambda e: e.activation(out=lft[par], in_=fgt[par], func=AF.Ln), reads=[Rfg[par]], writes=[Rlf[par]])
                A("act", lambda e: e.activation(out=kkt[t % 3], in_=fgt[par], func=AF.Identity, scale=-1.0, bias=1.0),
                  reads=[Rfg[par]], writes=[Rkk[t % 3]])
                A("dve", lambda e: e.tensor_copy(out=lfh[par], in_=lft[par]), reads=[Rlf[par]], writes=[Rlh[par]])
                A("dve", lambda e: e.tensor_tensor(out=lfl[par], in0=lft[par], in1=lfh[par], op=ALU.subtract),
                  reads=[Rlf[par], Rlh[par]], writes=[Rlh[par]])

                def mme(e):
                    e.matmul(pbs[bE][:, 0:128], lhsT=Mdb16[:, 0, :], rhs=lfh[par][:, 0:128], start=True, stop=False)
                    e.matmul(pbs[bE][:, 0:128], lhsT=Mdb16[:, 0, :], rhs=lfl[par][:, 0:128], start=False, stop=True)
                    e.matmul(pbs[bE][:, 128:256], lhsT=Mdb16[:, 1, :], rhs=lfh[par][:, 128:256], start=True, stop=False)
                    e.matmul(pbs[bE][:, 128:256], lhsT=Mdb16[:, 1, :], rhs=lfl[par][:, 128:256], start=False, stop=True)
                    e.matmul(pbs[bE][:, 256:258], lhsT=lfh[par][:, 0:128], rhs=cwd16[:, 0:2], start=True, stop=False)
                    e.matmul(pbs[bE][:, 256:258], lhsT=lfl[par][:, 0:128], rhs=cwd16[:, 0:2], start=False, stop=True)
                    e.matmul(pbs[bE][:, 258:260], lhsT=lfh[par][:, 128:256], rhs=cwd16[:, 2:4], start=True, stop=False)
                    return e.matmul(pbs[bE][:, 258:260], lhsT=lfl[par][:, 128:256], rhs=cwd16[:, 2:4], start=False, stop=True)
                A("pe", mme, reads=[Rlh[par], Rc16], writes=[Rpb[bE]])

            def st_s2(t):
                par = t % 2
                q3 = t % 3
                bE = 2 + par
                A("act", lambda e: e.activation(out=ept[par], in_=pbs[bE][:, 0:256], func=AF.Exp),
                  reads=[Rpb[bE]], writes=[Rex[par]])
                A("act", lambda e: e.activation(out=emt[par], in_=pbs[bE][:, 0:256], func=AF.Exp, scale=-1.0),
                  reads=[Rpb[bE]], writes=[Rex[par]])
                A("act", lambda e: e.activation(out=GCb[:, t, :], in_=pbs[bE][:, 256:260], func=AF.Copy),
                  reads=[Rpb[bE]], writes=[RGC])

            def st_s2b(t):
                par = t % 2
                q3 = t % 3
                A("dve", lambda e: e.tensor_tensor(out=qtok[par], in0=ept[par].rearrange("p (d c) -> p d c", d=2),
                                                   in1=qsb[q3].unsqueeze(1).to_broadcast([128, 2, 128]), op=ALU.mult),
                  reads=[Rex[par], Rqs[q3]], writes=[Rqk[par]])
                A("dve", lambda e: e.tensor_tensor(out=KHa[:, t], in0=kkt[q3].rearrange("p (d c) -> p d c", d=2),
                                                   in1=emt[par].rearrange("p (d c) -> p d c", d=2), op=ALU.mult),
                  reads=[Rex[par], Rkk[q3]], writes=[Rqk[par], RKH[t]])

                def trq(e):
                    e.transpose(out=pts[par][:, 0:128], in_=qtok[par][:, 0, :], identity=identb[:])
                    e.transpose(out=pts[par][:, 128:256], in_=KHa[:, t, 0, :], identity=identb[:])
                    e.transpose(out=pts[par][:, 256:384], in_=qtok[par][:, 1, :], identity=identb[:])
                    return e.transpose(out=pts[par][:, 384:512], in_=KHa[:, t, 1, :], identity=identb[:])
                A("pe", trq, reads=[Rqk[par], Rc], writes=[Rpt[par]])
                A("dve", lambda e: e.tensor_copy(out=QKT[:, t, :], in_=pts[par][:, 0:512]),
                  reads=[Rpt[par]], writes=[RQK[t]])

            for i in range(NT + 2):
                if i < NT:
                    st_mm5(i)
                if 0 <= i - 2 < NT:
                    st_s2(i - 2)
                if 0 <= i - 1 < NT:
                    st_lnf(i - 1)
                if 0 <= i - 2 < NT:
                    st_s2b(i - 2)
                if i < NT:
                    st_sig(i)
                    st_dvea(i)
            A("dve", lambda e: e.tensor_tensor(out=GT[0][:, 0:NT - 1], in0=GCb[:, 0:NT - 1, 0], in1=GCb[:, 1:NT, 1],
                                               op=ALU.add), reads=[RGC], writes=[RGC])
            A("dve", lambda e: e.tensor_tensor(out=GT[1][:, 1:NT], in0=GCb[:, 1:NT, 2], in1=GCb[:, 0:NT - 1, 3],
                                               op=ALU.add), reads=[RGC], writes=[RGC])
            A("dve", lambda e: e.tensor_tensor(out=GT[1][:, 0:1], in0=GCb[:, 0, 2:3], in1=GCb[:, NT - 1, 3:4],
                                               op=ALU.add), reads=[RGC], writes=[RGC])
            A("act", lambda e: e.activation(out=GT[0][:, 0:NT - 1], in_=GT[0][:, 0:NT - 1], func=AF.Exp), reads=[RGC], writes=[RGT])
            A("act", lambda e: e.activation(out=GT[1], in_=GT[1], func=AF.Exp), reads=[RGC], writes=[RGT])

            def Gh(d, n):
                tl = (order_f if d == 0 else order_b)[n]
                return GT[d][:, tl:tl + 1]

            seen = set()

            def Pevh(d, tl, ps_ap, bres, seen=seen):
                i = tl - 2
                if i not in seen:
                    seen.add(i)
                    A("act", lambda e: e.activation(out=PSTh[:, i, :], in_=ps_ap, func=AF.Copy), reads=[bres], writes=[RPh[i]])
                else:
                    A("dve", lambda e: e.tensor_tensor(out=PSTh[:, i, :], in0=PSTh[:, i, :], in1=ps_ap, op=ALU.add),
                      reads=[bres, RPh[i]], writes=[RPh[i]])
            run_scans(lambda d, tl: QKT[:, tl, d * 256:d * 256 + 128], lambda d, tl: QKT[:, tl, d * 256 + 128:d * 256 + 256],
                      lambda d, tl: KHa[:, tl, d, :], lambda d, tl: VH[:, tl, :], 128, Gh, Pevh,
                      lambda tl: [RQK[tl], RKH[tl], RVH[tl]], [RGT])
            for i in range(32):
                A("act", lambda e, i=i: e.activation(out=sqj, in_=PSTh[:, i, :], func=AF.Square, accum_out=ssqH[:, i:i + 1]),
                  reads=[RPh[i]], writes=[RPall])
            A("act", lambda e: e.activation(out=ssqH, in_=ssqH, func=AF.Ln, scale=1.0 / 128.0, bias=EPS), reads=[RPall], writes=[RPall])
            A("act", lambda e: e.activation(out=ssqH, in_=ssqH, func=AF.Exp, scale=-0.5), reads=[RPall], writes=[RPall])
            A("dve", lambda e: e.tensor_tensor(out=PSTh, in0=PSTh, in1=ssqH.unsqueeze(2).to_broadcast([128, 32, 128]), op=ALU.mult),
              reads=RPh + [RPall], writes=[RPall])
            A("dve", lambda e: e.tensor_tensor(out=GG, in0=PSTh, in1=GG, op=ALU.mult), reads=[RPall] + RGG, writes=RGG + RPh)
            A("sp", lambda e, h=h: e.dma_start(out=ysc_v[:, :, 512 + h * 128:512 + (h + 1) * 128], in_=GG), reads=RGG,
              writes=[], dma="ysth")

        reset_arenas()
        G2 = fa(1024)

        Dg = fa(1024).rearrange("p (k n) -> p k n", n=128)
        RDg = Res()

        def build_G(Gdst, j0, RGd):
            for k in range(8):
                A("dve", lambda e, k=k: e.tensor_scalar_mul(out=Dg[:, k, :], in0=identf, scalar1=modT[:, j0 + k, 0:1]),
                  reads=[Rprm, Rc], writes=[RDg])
            for half in range(2):
                A("pe", lambda e, half=half: e.matmul(pbs[half][:, 0:512], lhsT=onesf,
                                                      rhs=Dg[:, 4 * half:4 * half + 4, :], start=True, stop=True),
                  reads=[RDg, Rc], writes=[Rpb[half]])
                A("act", lambda e, half=half: e.activation(out=Gdst[:, half * 512:(half + 1) * 512], in_=pbs[half][:, 0:512],
                                                           func=AF.Copy), reads=[Rpb[half]], writes=[RGd])

        wd = ba(22 * 1024).rearrange("p (k n) -> p k n", n=1024)
        Rwd3 = Res()
        G1 = fa(1024)
        RG1 = Res()
        build_G(G1, 16, RG1)
        RG2 = Res()
        build_G(G2, 40, RG2)
        wst4 = [fa(1024) for _ in range(2)]
        Rwst4 = [Res() for _ in range(2)]
        wo = ba(8 * 1024).rearrange("p (k n) -> p k n", n=1024)
        Rwo = Res()
        nwTt = fa(8)
        RnwT = Res()
        A("sp", lambda e: e.dma_start(out=nwTt, in_=D["nwT"][:, :]), writes=[RnwT], dma="nwT")
        wst3 = wst4[0]
        Rwst3 = Rwst4[0]
        for k in range(8):
            A("sp", lambda e, k=k: e.dma_start(out=wst3, in_=D["w_out"][k * 128:(k + 1) * 128, :]), writes=[Rwst3], dma="wst3")
            A("dve", lambda e, k=k: e.scalar_tensor_tensor(out=wo[:, k, :], in0=wst3, scalar=nwTt[:, k:k + 1], in1=G1,
                                                           op0=ALU.mult, op1=ALU.mult),
              reads=[Rwst3, RG1, RnwT], writes=[Rwo])
        xts3 = [fa(1024) for _ in range(3)]
        Rxt3 = [Res() for _ in range(3)]
        junk3 = fa(1024)
        Rjunk3 = Res()
        ssb3 = fa(8)
        Rss3 = [Res() for _ in range(4)]
        xnb3 = [ba(1024) for _ in range(2)]
        Rxn3 = [Res() for _ in range(2)]
        ytl = [ba(1024) for _ in range(2)]
        Ryt = [Res() for _ in range(2)]
        yTt = [ba(1024).rearrange("p (k t) -> p k t", t=128) for _ in range(2)]
        RyT = [Res() for _ in range(2)]
        Rx1 = [Res() for _ in range(32)]
        modtmp3 = [fa(1024) for _ in range(2)]
        Rmt3 = [Res() for _ in range(2)]

        def p3_A1(t):
            s = t % 2
            x3 = t % 3
            A("sp", lambda e: e.dma_start(out=ytl[s], in_=ysc[t * 128:(t + 1) * 128, :]), writes=[Ryt[s]], dma=f"yt{s}")
            A("sp", lambda e: e.dma_start(out=xts3[x3], in_=D["xin"][256 + t * 128:256 + (t + 1) * 128, :]),
              writes=[Rxt3[x3]], dma=f"x1t{x3}")

            def try_(e):
                for k in range(8):
                    ins = e.transpose(out=pts[0][:, k * 128:(k + 1) * 128], in_=ytl[s][:, k * 128:(k + 1) * 128], identity=identb[:])
                return ins
            A("pe", try_, reads=[Ryt[s], Rc], writes=[Rpt[0]])
            A("act", lambda e: e.activation(out=yTt[s].rearrange("p k t -> p (k t)"), in_=pts[0][:, :], func=AF.Copy),
              reads=[Rpt[0]], writes=[RyT[s]])

        def p3_A2(t):
            s = t % 2
            x3 = t % 3
            for half in range(2):
                bank = 2 * s + half

                def mmo(e, half=half, bank=bank):
                    for k in range(8):
                        ins = e.matmul(pbs[bank][:, 0:512], lhsT=yTt[s][:, k, :], rhs=wo[:, k, half * 512:(half + 1) * 512],
                                       start=(k == 0), stop=(k == 7))
                    return ins
                A("pe", mmo, reads=[RyT[s], Rwo], writes=[Rpb[bank]])
                A("dve", lambda e, half=half, bank=bank: e.tensor_tensor(
                    out=xts3[x3][:, half * 512:(half + 1) * 512], in0=xts3[x3][:, half * 512:(half + 1) * 512], in1=pbs[bank][:, 0:512],
                    op=ALU.add), reads=[Rpb[bank], Rxt3[x3]], writes=[Rxt3[x3]])
            A("pool", lambda e: e.dma_start(out=x1sc[t * 128:(t + 1) * 128, :], in_=xts3[x3]), reads=[Rxt3[x3]], writes=[Rx1[t]],
              dma=f"x1w{x3}")
            s4 = t % 4
            ssv = ssb3[:, s4:s4 + 1]
            A("act", lambda e: e.activation(out=junk3, in_=xts3[x3], func=AF.Square, scale=1.0 / 32.0, accum_out=ssv),
              reads=[Rxt3[x3]], writes=[Rjunk3, Rss3[s4]])
            A("act", lambda e: e.activation(out=ssv, in_=ssv, func=AF.Ln, bias=EPS), reads=[Rss3[s4]], writes=[Rss3[s4]])
            A("act", lambda e: e.activation(out=ssv, in_=ssv, func=AF.Exp, scale=-0.5), reads=[Rss3[s4]], writes=[Rss3[s4]])

        def p3_A2b(t):
            s = t % 2
            x3 = t % 3
            s4 = t % 4
            ssv = ssb3[:, s4:s4 + 1]
            A("dve", lambda e: e.tensor_scalar_mul(out=xnb3[s], in0=xts3[x3], scalar1=ssv),
              reads=[Rss3[s4], Rxt3[x3]], writes=[Rxn3[s]])

        def p3_B(t):
            s = t % 2

            def tr2(e):
                for k in range(8):
                    ins = e.transpose(out=pts[1][:, k * 128:(k + 1) * 128], in_=xnb3[s][:, k * 128:(k + 1) * 128], identity=identb[:])
                return ins
            A("pe", tr2, reads=[Rxn3[s], Rc], writes=[Rpt[1]])
            A("dve", lambda e: e.tensor_tensor(out=modtmp3[s].rearrange("p (k c) -> p k c", k=8),
                                               in0=pts[1][:, :].rearrange("p (k c) -> p k c", k=8),
                                               in1=prm[:, 4, :].unsqueeze(2).to_broadcast([128, 8, 128]), op=ALU.mult),
              reads=[Rpt[1], Rprm], writes=[Rmt3[s]])
            A("dve", lambda e: e.tensor_tensor(out=xT[:, :, (t + 2) * 128:(t + 3) * 128],
                                               in0=modtmp3[s].rearrange("p (k c) -> p k c", k=8),
                                               in1=prm[:, 5, :].unsqueeze(2).to_broadcast([128, 8, 128]), op=ALU.add),
              reads=[Rmt3[s], Rprm], writes=[RxT[t + 2]])

        for i in range(34):
            if i < 32:
                p3_A1(i)
            if 0 <= i - 1 < 32:
                p3_A2(i - 1)
            if 0 <= i - 2 < 32:
                p3_B(i - 2)
            if 0 <= i - 1 < 32:
                p3_A2b(i - 1)

        reset_arenas()
        G2 = fa(1024)
        gtmp = [fa(512) for _ in range(2)]
        Rgt = [Res() for _ in range(2)]
        wd = ba(22 * 1024).rearrange("p (k n) -> p k n", n=1024)
        Rwd = Res()
        HT = ba(22 * 512).rearrange("p (j t) -> p j t", t=512)
        RHT = Res()
        UP = [ba(10 * 66).rearrange("p (r c) -> p r c", c=66) for _ in range(2)]
        RUP = [Res() for _ in range(2)]
        WU = [ba(8 * 256).rearrange("p (k n) -> p k n", n=256) for _ in range(2)]
        RWU = [Res() for _ in range(2)]
        DGf = [ba(18 * 128).rearrange("p (w t n) -> p w t n", w=2, n=128) for _ in range(2)]
        RDGf = [Res() for _ in range(2)]
        fcwt = fa(44 * 9).rearrange("p (c t) -> p c t", t=9)
        fnwt = fa(1024)
        Rfc = Res()
        sat = fa(512)
        Rsa = Res()
        x1t = [fa(1024) for _ in range(4)]
        Rx1t = [Res() for _ in range(4)]
        HALO = fa(2816).bitcast(BF16).rearrange("p (j a r c) -> p j a r c", j=22, a=2, r=2)
        RHALO = [[Res() for _ in range(2)] for _ in range(22)]
        junk4 = fa(1024)
        Rjunk4 = Res()
        ssb4 = fa(8)
        Rss4 = [Res() for _ in range(4)]
        A("sp", lambda e: e.dma_start(out=fcwt, in_=D["fcw"][:, :, :]), writes=[Rfc], dma="fcw")
        A("sp", lambda e: e.dma_start(out=fnwt, in_=D["fnw"][:, :]), writes=[Rfc], dma="fnw")
        for ab in range(2):
            A("pool", lambda e, ab=ab: e.memset(UP[ab], 0.0), writes=[RUP[ab]])
        w_up_v = D["w_up"].rearrange("(k p) n -> p k n", p=128)
        it = 0
        NB = 8
        for blk in range(NB):
            R0 = 8 * blk
            if blk == NB - 1:
                for ab in range(2):
                    A("pool", lambda e, ab=ab: e.memset(UP[ab][:, 9, :], 0.0), writes=[RUP[ab]])
            for tt in range(4):
                gt = blk * 4 + tt
                A("sp", lambda e, gt=gt, tt=tt: e.dma_start(out=x1t[tt], in_=x1sc[gt * 128:(gt + 1) * 128, :]), reads=[Rx1[gt]],
                  writes=[Rx1t[tt]], dma=f"x1r{tt}")
            for j in range(22):
                ws = it % 2
                it += 1
                if blk > 0:
                    for ab in range(2):
                        A("dve", lambda e, ab=ab, j=j: e.tensor_copy(out=UP[ab][:, 0:2, 1:65], in_=HALO[:, j, ab]),
                          reads=[RHALO[j][ab]], writes=[RUP[ab]])
                if blk == 0:
                    A("pool", lambda e, j=j: e.dma_start(out=wd[:, j, :], in_=D["w_down"][j * 128:(j + 1) * 128, :]), writes=[Rwd],
                      dma="wdld")
                for ab in range(2):
                    c0 = ab * 2816 + j * 128
                    A("pool", lambda e, ws=ws, ab=ab, c0=c0: e.dma_start(out=WU[ws][:, :, ab * 128:(ab + 1) * 128],
                                                                          in_=w_up_v[:, :, c0:c0 + 128]),
                      writes=[RWU[ws]], dma=f"wu{ws}{ab}")
                    ch = ab * 22 + j
                    A("dve", lambda e, ws=ws, ab=ab, ch=ch: e.tensor_tensor(
                        out=DGf[ws][:, ab], in0=identb[:].unsqueeze(1).to_broadcast([128, 9, 128]),
                        in1=fcwt[:, ch, :].unsqueeze(2).to_broadcast([128, 9, 128]), op=ALU.mult),
                      reads=[Rfc, Rc], writes=[RDGf[ws]])
                if blk == 0:
                    groups = ((1, 8), (9, 1))
                elif blk == NB - 1:
                    groups = ((2, 7),)
                else:
                    groups = ((2, 8),)
                for ab in range(2):
                    for gi, (lo, nrow) in enumerate(groups):
                        tok0 = 256 + (R0 - 1 + lo) * 64
                        ntk = nrow * 64
                        bank = (ab + gi) % 2
                        tiles_needed = [RxT[tt] for tt in range(tok0 // 128, (tok0 + ntk + 127) // 128)]

                        def mmu(e, ws=ws, ab=ab, tok0=tok0, ntk=ntk, bank=bank):
                            for k in range(8):
                                ins = e.matmul(pbs[bank][:, 0:ntk], lhsT=WU[ws][:, k, ab * 128:(ab + 1) * 128],
                                               rhs=xT[:, k, tok0:tok0 + ntk], start=(k == 0), stop=(k == 7))
                            return ins
                        A("pe", mmu, reads=tiles_needed + [RWU[ws]], writes=[Rpb[bank]])
                        A("act", lambda e, ab=ab, lo=lo, nrow=nrow, ntk=ntk, bank=bank: e.activation(
                            out=UP[ab][:, lo:lo + nrow, 1:65], in_=pbs[bank][:, 0:ntk].rearrange("p (r c) -> p r c", c=64),
                            func=AF.Copy), reads=[Rpb[bank]], writes=[RUP[ab]])
                    if blk < NB - 1:
                        A("dve", lambda e, ab=ab, j=j: e.tensor_copy(out=HALO[:, j, ab], in_=UP[ab][:, 8:10, 1:65]),
                          reads=[RUP[ab]], writes=[RHALO[j][ab]])
                for ab in range(2):
                    bank = 2 + ab

                    def mmcf(e, ws=ws, ab=ab, bank=bank):
                        for tap in range(9):
                            dr, dc = tap // 3, tap % 3
                            ins = e.matmul(pbs[bank][:, 0:512].rearrange("p (r c) -> p r c", c=64),
                                           lhsT=DGf[ws][:, ab, tap, :],
                                           rhs=UP[ab][:, dr:dr + 8, dc:dc + 64],
                                           start=(tap == 0), stop=(tap == 8))
                        return ins
                    A("pe", mmcf, reads=[RUP[ab], RDGf[ws]], writes=[Rpb[bank]])
                A("act", lambda e: e.activation(out=sat, in_=pbs[2][:, 0:512], func=AF.Silu), reads=[Rpb[2]], writes=[Rsa])
                A("dve", lambda e, j=j: e.tensor_tensor(out=HT[:, j, :], in0=sat, in1=pbs[3][:, 0:512], op=ALU.mult),
                  reads=[Rsa, Rpb[3]], writes=[RHT])
            for tt in range(4):
                gt = blk * 4 + tt
                s = tt
                for half in range(2):
                    bank = 4 + half

                    def mmd(e, tt=tt, half=half, bank=bank):
                        for j in range(22):
                            ins = e.matmul(pbs[bank][:, 0:512], lhsT=HT[:, j, tt * 128:(tt + 1) * 128],
                                           rhs=wd[:, j, half * 512:(half + 1) * 512], start=(j == 0), stop=(j == 21))
                        return ins
                    A("pe", mmd, reads=[RHT, Rwd], writes=[Rpb[bank]])
                    A("dve", lambda e, half=half, bank=bank: e.tensor_tensor(
                        out=gtmp[half], in0=pbs[bank][:, 0:512], in1=G2[:, half * 512:(half + 1) * 512], op=ALU.mult),
                      reads=[Rpb[bank]], writes=[Rgt[half]])
                    A("dve", lambda e, s=s, half=half: e.tensor_tensor(
                        out=x1t[s][:, half * 512:(half + 1) * 512], in0=x1t[s][:, half * 512:(half + 1) * 512],
                        in1=gtmp[half], op=ALU.add), reads=[Rgt[half], Rx1t[s]], writes=[Rx1t[s]])
                s4 = gt % 4
                ssv = ssb4[:, s4:s4 + 1]
                A("act", lambda e, s=s, ssv=ssv: e.activation(out=junk4, in_=x1t[s], func=AF.Square, scale=1.0 / 32.0, accum_out=ssv),
                  reads=[Rx1t[s]], writes=[Rjunk4, Rss4[s4]])
                A("act", lambda e, ssv=ssv: e.activation(out=ssv, in_=ssv, func=AF.Ln, bias=EPS), reads=[Rss4[s4]], writes=[Rss4[s4]])
                A("act", lambda e, ssv=ssv: e.activation(out=ssv, in_=ssv, func=AF.Exp, scale=-0.5), reads=[Rss4[s4]], writes=[Rss4[s4]])
                A("dve", lambda e, s=s, ssv=ssv: e.scalar_tensor_tensor(out=x1t[s], in0=x1t[s], scalar=ssv, in1=fnwt,
                                                                        op0=ALU.mult, op1=ALU.mult),
                  reads=[Rss4[s4], Rx1t[s], Rfc], writes=[Rx1t[s]])
                A("sp", lambda e, gt=gt, s=s: e.dma_start(out=out[gt * 128:(gt + 1) * 128, :], in_=x1t[s]), reads=[Rx1t[s]], writes=[],
                  dma=f"out{s}")
        P.barrier()
        P.emit(st)
    return nc


def _consts():
    u = np.arange(128)[:, None]
    t = np.arange(128)[None, :]
    cf = np.zeros((8, 128, 128), np.float32)
    cf[0] = np.eye(128)
    cf[1] = 1.0
    cf[2] = (u <= t)
    cf[3] = (u >= t)
    cf[4] = (u > t)
    cf[5] = (u < t)
    cf[6] = (u <= t).astype(np.float32) - (u <= 63).astype(np.float32)
    cf[7] = (u >= t).astype(np.float32) - (u >= 64).astype(np.float32)
    cwd = np.zeros((128, 4), np.float32)
    uu = np.arange(128)
    cwd[:, 0] = uu > 63
    cwd[:, 1] = uu <= 63
    cwd[:, 2] = uu < 64
    cwd[:, 3] = uu >= 64
    return np.ascontiguousarray(cf.transpose(1, 0, 2)), cwd


_NC_CACHE = {}


def kernel(x, c, ctx, c_ctx, w_mod, b_mod, norm1_w, w_in, mlstm_gate_b, mlstm_conv_w, mlstm_norm_w,
           hgrn_lb_logits, hgrn_norm_w, w_out, norm2_w, w_up, ffn_conv_w, w_down, final_norm_w):
    f = lambda a: np.ascontiguousarray(np.asarray(a, dtype=np.float32))
    x, c, ctx, c_ctx = f(x), f(c), f(ctx), f(c_ctx)
    cf, cwd = _consts()
    shared = {
        "w_mod": f(w_mod[0]),
        "b_modT": f(np.asarray(b_mod[0]).reshape(48, 128).T),
        "n1T": f(np.asarray(norm1_w[0]).reshape(8, 128).T),
        "n2T": f(np.asarray(norm2_w[0]).reshape(8, 128).T),
        "fnw": f(np.broadcast_to(np.asarray(final_norm_w)[None, :], (128, 1024))),
        "w_in": f(w_in[0]),
        "gate_b": f(np.broadcast_to(np.asarray(mlstm_gate_b[0])[None, :], (128, 16))),
        "mcw": f(np.asarray(mlstm_conv_w[0]).reshape(9, 8, 128).transpose(2, 1, 0)),
        "fcw": f(np.asarray(ffn_conv_w[0]).reshape(9, 44, 128).transpose(2, 1, 0)),
        "nwT": f(np.concatenate([np.asarray(mlstm_norm_w[0]), np.asarray(hgrn_norm_w[0])]).reshape(8, 128).T),
        "lbl": f(np.broadcast_to(np.asarray(hgrn_lb_logits)[None], (128, 2, 2, 512))),
        "w_out": f(w_out[0]),
        "w_up": f(w_up[0]),
        "w_down": f(w_down[0]),
        "cf32": cf,
        "cwd": cwd,
        "identb": np.eye(128).astype(ml_dtypes.bfloat16),
    }
    in_maps = []
    for b in range(8):
        m = dict(shared)
        m["xin"] = np.ascontiguousarray(np.concatenate([ctx[b], x[b]], axis=0))
        m["c2"] = np.ascontiguousarray(np.stack([c[b], c_ctx], axis=1).reshape(8, 128, 2).transpose(1, 0, 2))
        in_maps.append(m)
    if "nc" not in _NC_CACHE:
        _NC_CACHE["nc"] = build_program()
    res = run_bass_kernel_spmd(_NC_CACHE["nc"], in_maps, core_ids=list(range(8)))
    return np.stack([np.asarray(r["out"], dtype=np.float32) for r in res.results], axis=0)
```

```python
import math
from contextlib import ExitStack

import numpy as np
import ml_dtypes
import concourse.bass as bass
import concourse.mybir as mybir
from concourse.bass_utils import run_bass_kernel_spmd

F32 = mybir.dt.float32
BF16 = mybir.dt.bfloat16
AF = mybir.ActivationFunctionType
ALU = mybir.AluOpType
AX = mybir.AxisListType

ENGS = ("pe", "act", "dve", "pool", "sp")
SIG_ROT = 6000

NT = 34
NTOK = 4352
EPS = 1e-6
LN_SQRT128 = 0.5 * math.log(128.0)
QS = 128.0 ** -0.5


class Res:
    __slots__ = ("name", "last_w", "rd_eng", "rd_dma")

    def __init__(self, name=""):
        self.name = name
        self.last_w = None
        self.rd_eng = {}
        self.rd_dma = []


class Op:
    __slots__ = ("eng", "fn", "deps", "is_dma", "dkey", "dval", "sig", "nsig")

    def __init__(self, eng, fn, is_dma):
        self.eng = eng
        self.fn = fn
        self.deps = []
        self.is_dma = is_dma
        self.dkey = None
        self.dval = 0
        self.sig = None
        self.nsig = False


class Prog:
    def __init__(self, nc):
        self.nc = nc
        self.ops = {e: [] for e in ENGS}
        self.dma_cnt = {}
        self.last_dma = {}

    def add(self, eng, fn, reads=(), writes=(), dma=None):
        op = Op(eng, fn, dma is not None)
        deps = {}
        for r in reads:
            if r.last_w is not None:
                deps[id(r.last_w)] = (r.last_w, True)
        for w in writes:
            if w.last_w is not None and id(w.last_w) not in deps:
                deps[id(w.last_w)] = (w.last_w, False)
            for rd in list(w.rd_eng.values()) + w.rd_dma:
                if id(rd) not in deps:
                    deps[id(rd)] = (rd, False)
        for (p, raw) in deps.values():
            if p.eng == eng and not p.is_dma:
                if eng == "pe" or not raw:
                    continue
            op.deps.append(p)
            p.nsig = True
        for r in reads:
            if op.is_dma:
                r.rd_dma.append(op)
            else:
                r.rd_eng[eng] = op
        for w in writes:
            w.last_w = op
            w.rd_eng = {}
            w.rd_dma = []
        if dma is not None:
            c = self.dma_cnt.get(dma, 0) + 1
            self.dma_cnt[dma] = c
            op.dkey = dma
            op.dval = 16 * c
            self.last_dma[dma] = op
        self.ops[eng].append(op)
        return op

    def barrier(self):
        lasts = []
        for e in ENGS:
            for op in reversed(self.ops[e]):
                if not op.is_dma and op.fn is not None:
                    lasts.append(op)
                    break
        lasts += list(self.last_dma.values())
        for e in ENGS:
            op = Op(e, None, False)
            for p in lasts:
                if p.eng == e and not p.is_dma:
                    continue
                op.deps.append(p)
                p.nsig = True
            self.ops[e].append(op)

    def emit(self, stack):
        nc = self.nc
        nsems = {}
        for e in ENGS:
            cnt = 0
            for op in self.ops[e]:
                if op.is_dma or not op.nsig:
                    continue
                op.sig = (cnt // SIG_ROT, cnt % SIG_ROT + 1)
                cnt += 1
            nsems[e] = (cnt + SIG_ROT - 1) // SIG_ROT
        sems = {}
        for e in ENGS:
            for s in range(nsems[e]):
                sems[(e, s)] = stack.enter_context(nc.semaphore(f"s_{e}_{s}"))
        dsems = {}
        for i, k in enumerate(self.dma_cnt.keys()):
            dsems[k] = stack.enter_context(nc.semaphore(f"d_{i}"))
        block = stack.enter_context(nc.Block())
        engobj = {"pe": block.tensor, "act": block.scalar, "dve": block.vector,
                  "pool": block.gpsimd, "sp": block.sync}

        def make(e):
            def body(eng):
                waited = {}
                for op in self.ops[e]:
                    for p in op.deps:
                        if p.is_dma:
                            key = ("d", p.dkey)
                            sem = dsems[p.dkey]
                            val = p.dval
                        else:
                            key = (p.eng, p.sig[0])
                            sem = sems[key]
                            val = p.sig[1]
                        if waited.get(key, 0) >= val:
                            continue
                        waited[key] = val
                        eng.wait_ge(sem, val)
                    if op.fn is None:
                        continue
                    ins = op.fn(eng)
                    if op.is_dma:
                        ins.then_inc(dsems[op.dkey], 16)
                    elif op.nsig:
                        ins.then_inc(sems[(e, op.sig[0])], 1)
            return body

        for e in ENGS:
            engobj[e](make(e))


def build_program():
    nc = bass.Bass("TRN2", target_bir_lowering=False)
    D = {}

    def din(name, shape, dt=F32):
        D[name] = nc.dram_tensor(name, list(shape), dt, kind="ExternalInput").ap()

    din("xin", [NTOK, 1024])
    din("c2", [128, 8, 2])
    din("w_mod", [1024, 6144])
    din("b_modT", [128, 48])
    din("n1T", [128, 8])
    din("n2T", [128, 8])
    din("fnw", [128, 1024])
    din("w_in", [1024, 4624])
    din("gate_b", [128, 16])
    din("mcw", [128, 8, 9])
    din("fcw", [128, 44, 9])
    din("nwT", [128, 8])
    din("lbl", [128, 2, 2, 512])
    din("w_out", [1024, 1024])
    din("w_up", [1024, 5632])
    din("w_down", [2816, 1024])
    din("cf32", [128, 8, 128])
    din("cwd", [128, 4])
    din("identb", [128, 128], BF16)
    out = nc.dram_tensor("out", [4096, 1024], F32, kind="ExternalOutput").ap()
    ysc = nc.dram_tensor("ysc", [4096, 1024], BF16, kind="Internal").ap()
    x1sc = nc.dram_tensor("x1sc", [4096, 1024], F32, kind="Internal").ap()

    with ExitStack() as st:
        P = Prog(nc)
        A = P.add

        def T(name, shape, dt):
            return st.enter_context(nc.sbuf_tensor(name, list(shape), dt))

        xT = T("xT", [128, 8, NTOK], BF16)
        RxT = [Res(f"xT{t}") for t in range(NT)]
        cf = T("cf", [128, 8, 128], F32)
        identf, onesf, maskf, maskb, SU, SL, Mdf, Mdb = [cf[:, i, :] for i in range(8)]
        cwd = T("cwd_sb", [128, 4], F32)
        identb = T("identb_sb", [128, 128], BF16)
        Rc = Res("consts")
        modT = T("modT", [128, 48, 2], F32)
        prm = T("prm", [128, 6, 8], F32)
        Rprm = Res("prm")
        n12 = T("n12", [128, 2, 8], F32)
        bmT = T("bmT", [128, 48], F32)
        FA = T("FA", [128, 12400], F32)
        BA = T("BA", [128, 44000], BF16)
        pbs = [st.enter_context(nc.psum_tensor(f"pb{i}", [128, 512], F32)) for i in range(8)]
        pts = [pbs[6 + i][:, :].bitcast(BF16) for i in range(2)]
        Rpb = [Res(f"pb{i}") for i in range(8)]
        Rpt = [Rpb[6], Rpb[7]]

        fa_off = [0]
        ba_off = [0]

        def fa(n, shape=None):
            o = fa_off[0]
            fa_off[0] += n
            assert fa_off[0] <= 12400, fa_off[0]
            v = FA[:, o:o + n]
            return v

        def ba(n):
            o = ba_off[0]
            ba_off[0] += n
            assert ba_off[0] <= 44000, ba_off[0]
            return BA[:, o:o + n]

        def reset_arenas():
            P.barrier()
            fa_off[0] = 0
            ba_off[0] = 0

        A("sp", lambda e: e.dma_start(out=cf[:], in_=D["cf32"][:, :, :]), writes=[Rc], dma="c0")
        A("sp", lambda e: e.dma_start(out=cwd[:], in_=D["cwd"][:, :]), writes=[Rc], dma="c1")
        A("sp", lambda e: e.dma_start(out=identb[:], in_=D["identb"][:, :]), writes=[Rc], dma="c2")
        A("sp", lambda e: e.dma_start(out=n12[:, 0, :], in_=D["n1T"][:, :]), writes=[Rprm], dma="c3")
        A("sp", lambda e: e.dma_start(out=n12[:, 1, :], in_=D["n2T"][:, :]), writes=[Rprm], dma="c4")
        A("sp", lambda e: e.dma_start(out=bmT[:], in_=D["b_modT"][:, :]), writes=[Rprm], dma="c5")

        c2t = fa(16).rearrange("p (k m) -> p k m", m=2)
        sct = fa(16).rearrange("p (k m) -> p k m", m=2)
        tmpc = fa(16).rearrange("p (k m) -> p k m", m=2)
        Rc2 = Res()
        A("sp", lambda e: e.dma_start(out=c2t, in_=D["c2"][:, :, :]), writes=[Rc2], dma="c6")
        A("act", lambda e: e.activation(out=tmpc, in_=c2t, func=AF.Exp, scale=-1.0), reads=[Rc2], writes=[Rc2])
        A("dve", lambda e: e.tensor_scalar_add(out=tmpc, in0=tmpc, scalar1=1.0), reads=[Rc2], writes=[Rc2])
        A("dve", lambda e: e.reciprocal(out=tmpc, in_=tmpc), reads=[Rc2], writes=[Rc2])
        A("dve", lambda e: e.tensor_tensor(out=sct, in0=c2t, in1=tmpc, op=ALU.mult), reads=[Rc2], writes=[Rc2])
        wm = [ba(4096).rearrange("p (k n) -> p k n", n=512) for _ in range(3)]
        Rwm = [Res() for _ in range(3)]
        sctb = ba(16).rearrange("p (k m) -> p k m", m=2)
        A("dve", lambda e: e.tensor_copy(out=sctb, in_=sct), reads=[Rc2], writes=[Rc2])
        w_mod_v = D["w_mod"].rearrange("(k p) n -> p k n", p=128)
        for jj in range(12):
            s = jj % 3
            A("pool", lambda e, jj=jj, s=s: e.dma_start(out=wm[s], in_=w_mod_v[:, :, jj * 512:(jj + 1) * 512]),
              writes=[Rwm[s]], dma=f"wm{s}")

            def mm(e, jj=jj, s=s):
                for q in range(4):
                    j = jj * 4 + q
                    for k in range(8):
                        ins = e.matmul(pbs[0][:, 2 * j:2 * j + 2], lhsT=wm[s][:, k, q * 128:(q + 1) * 128], rhs=sctb[:, k, :],
                                       start=(k == 0), stop=(k == 7))
                return ins
            A("pe", mm, reads=[Rwm[s], Rc2], writes=[Rpb[0]])
        A("dve", lambda e: e.tensor_tensor(out=modT[:], in0=pbs[0][:, 0:96].rearrange("p (j m) -> p j m", m=2),
                                           in1=bmT[:].unsqueeze(2).to_broadcast([128, 48, 2]), op=ALU.add),
          reads=[Rpb[0], Rprm], writes=[Rprm])
        for (pi, nidx, scj, col) in ((0, 0, 8, 0), (2, 0, 8, 1), (4, 1, 32, 0)):
            A("dve", lambda e, pi=pi, nidx=nidx, scj=scj, col=col: e.scalar_tensor_tensor(
                out=prm[:, pi, :], in0=modT[:, scj:scj + 8, col], scalar=1.0, in1=n12[:, nidx, :],
                op0=ALU.add, op1=ALU.mult), reads=[Rprm], writes=[Rprm])
        for (pi, shj, col) in ((1, 0, 0), (3, 0, 1), (5, 24, 0)):
            A("dve", lambda e, pi=pi, shj=shj, col=col: e.tensor_copy(out=prm[:, pi, :], in_=modT[:, shj:shj + 8, col]),
              reads=[Rprm], writes=[Rprm])

        xts = [fa(1024) for _ in range(3)]
        Rxt = [Res() for _ in range(3)]
        junk = fa(1024)
        Rjunk = Res()
        ssb = fa(8)
        Rss = [Res() for _ in range(4)]
        xnb = [ba(1024) for _ in range(2)]
        Rxn = [Res() for _ in range(2)]
        modtmp = [fa(1024) for _ in range(2)]
        Rmt = [Res() for _ in range(2)]

        def p1_stageA(t):
            s = t % 3
            s4 = t % 4
            s2 = t % 2
            ssv = ssb[:, s4:s4 + 1]
            A("sp", lambda e: e.dma_start(out=xts[s], in_=D["xin"][t * 128:(t + 1) * 128, :]), writes=[Rxt[s]], dma=f"xt{s}")
            A("act", lambda e: e.activation(out=junk, in_=xts[s], func=AF.Square, scale=1.0 / 32.0, accum_out=ssv),
              reads=[Rxt[s]], writes=[Rjunk, Rss[s4]])
            A("act", lambda e: e.activation(out=ssv, in_=ssv, func=AF.Ln, bias=EPS), reads=[Rss[s4]], writes=[Rss[s4]])
            A("act", lambda e: e.activation(out=ssv, in_=ssv, func=AF.Exp, scale=-0.5), reads=[Rss[s4]], writes=[Rss[s4]])
            A("dve", lambda e: e.tensor_scalar_mul(out=xnb[s2], in0=xts[s], scalar1=ssv),
              reads=[Rss[s4], Rxt[s]], writes=[Rxn[s2]])

        def p1_stageB(t):
            s2 = t % 2
            pa, psh = (2, 3) if t < 2 else (0, 1)

            def tr(e):
                for k in range(8):
                    ins = e.transpose(out=pts[s2][:, k * 128:(k + 1) * 128], in_=xnb[s2][:, k * 128:(k + 1) * 128],
                                      identity=identb[:])
                return ins
            A("pe", tr, reads=[Rxn[s2], Rc], writes=[Rpt[s2]])

            A("dve", lambda e: e.tensor_tensor(out=modtmp[s2].rearrange("p (k c) -> p k c", k=8),
                                               in0=pts[s2][:, :].rearrange("p (k c) -> p k c", k=8),
                                               in1=prm[:, pa, :].unsqueeze(2).to_broadcast([128, 8, 128]), op=ALU.mult),
              reads=[Rpt[s2], Rprm], writes=[Rmt[s2]])
            A("dve", lambda e: e.tensor_tensor(out=xT[:, :, t * 128:(t + 1) * 128],
                                               in0=modtmp[s2].rearrange("p (k c) -> p k c", k=8),
                                               in1=prm[:, psh, :].unsqueeze(2).to_broadcast([128, 8, 128]), op=ALU.add),
              reads=[Rmt[s2], Rprm], writes=[RxT[t]])

        for i in range(NT + 1):
            if i < NT:
                p1_stageA(i)
            if i >= 1:
                p1_stageB(i - 1)

        reset_arenas()
        WK = fa(NT * 8).rearrange("p (t g) -> p t g", g=8)
        FL = fa(NT * 8).rearrange("p (t g) -> p t g", g=8)
        DC = fa(NT * 8).rearrange("p (t g) -> p t g", g=8)
        Rgs = Res("gatescal")
        gbt = fa(16)
        Rgb = Res()
        A("sp", lambda e: e.dma_start(out=gbt, in_=D["gate_b"][:, :]), writes=[Rgb], dma="gb")
        wg = ba(128).rearrange("p (k n) -> p k n", n=16)
        Rwg = Res()
        w_in_v = D["w_in"].rearrange("(k p) n -> p k n", p=128)
        A("pool", lambda e: e.dma_start(out=wg, in_=w_in_v[:, :, 2048:2064]), writes=[Rwg], dma="wg")
        gps = [fa(16) for _ in range(2)]
        nls = [fa(16) for _ in range(2)]
        tm8 = [fa(8) for _ in range(2)]
        Rgp = [Res() for _ in range(2)]
        def p2a_A(t):
            s = t % 2
            b0 = 0 if s == 0 else 2

            def mmg(e):
                for k in range(8):
                    ins = e.matmul(pbs[b0][:, 0:16], lhsT=xT[:, k, t * 128:(t + 1) * 128], rhs=wg[:, k, :],
                                   start=(k == 0), stop=(k == 7))
                return ins
            A("pe", mmg, reads=[RxT[t], Rwg], writes=[Rpb[b0]])
            A("dve", lambda e: e.tensor_tensor(out=gps[s], in0=pbs[b0][:, 0:16], in1=gbt, op=ALU.add),
              reads=[Rpb[b0], Rgb], writes=[Rgp[s]])
            A("act", lambda e: e.activation(out=nls[s], in_=gps[s], func=AF.Exp, scale=-1.0),
              reads=[Rgp[s]], writes=[Rnl[s]])
            A("act", lambda e: e.activation(out=nls[s], in_=nls[s], func=AF.Ln, bias=1.0),
              reads=[Rnl[s]], writes=[Rnl[s]])

        def p2a_B(t):
            s = t % 2
            b1 = 1 if s == 0 else 3

            def mmc(e):
                e.matmul(pbs[b1][:, 0:16], lhsT=SU, rhs=nls[s], start=True, stop=True)
                e.matmul(pbs[b1][:, 16:32], lhsT=SL, rhs=nls[s], start=True, stop=True)
                return e.matmul(pbs[b1][:, 32:48], lhsT=onesf, rhs=nls[s], start=True, stop=True)
            A("pe", mmc, reads=[Rnl[s], Rc], writes=[Rpb[b1]])
            A("dve", lambda e: e.tensor_tensor(out=tm8[s][:, 0:4], in0=gps[s][:, 0:4], in1=pbs[b1][:, 4:8],
                                               op=ALU.subtract), reads=[Rgp[s], Rpb[b1]], writes=[Rtm[s]])
            A("dve", lambda e: e.tensor_tensor(out=tm8[s][:, 4:8], in0=gps[s][:, 8:12], in1=pbs[b1][:, 28:32],
                                               op=ALU.subtract), reads=[Rgp[s], Rpb[b1]], writes=[Rtm[s]])
            A("act", lambda e: e.activation(out=WK[:, t, :], in_=tm8[s], func=AF.Exp),
              reads=[Rtm[s]], writes=[Rgs])
            for (dst, c0, c1, bias) in ((FL, 4, 0, LN_SQRT128), (FL, 28, 4, LN_SQRT128), (DC, 36, 0, 0.0), (DC, 44, 4, 0.0)):
                A("act", lambda e, dst=dst, c0=c0, c1=c1, bias=bias: e.activation(
                    out=dst[:, t, c1:c1 + 4], in_=pbs[b1][:, c0:c0 + 4], func=AF.Exp, scale=-1.0, bias=bias),
                  reads=[Rpb[b1]], writes=[Rgs])

        Rnl = [Res() for _ in range(2)]
        Rtm = [Res() for _ in range(2)]
        for i in range(NT + 1):
            if i < NT:
                p2a_A(i)
            if i >= 1:
                p2a_B(i - 1)

        RING = 4
        LA = 2

        def alloc_scan(Nv):
            return {
                "Z": [[fa(Nv) for _ in range(RING)] for _ in range(2)],
                "U": [[fa(Nv) for _ in range(RING)] for _ in range(2)],
                "S": [[ba(Nv) for _ in range(RING)] for _ in range(2)],
                "M": [[ba(128) for _ in range(RING)] for _ in range(2)],
                "RZ": [[Res() for _ in range(RING)] for _ in range(2)],
                "RU": [[Res() for _ in range(RING)] for _ in range(2)],
                "RS": [[Res() for _ in range(RING)] for _ in range(2)],
                "RM": [[Res() for _ in range(RING)] for _ in range(2)],
            }

        scan_bufs = alloc_scan(129)
        order_f = list(range(NT))
        order_b = [1, 0] + list(range(NT - 1, 1, -1))

        def run_scans(qT, kT, ktok, vt, Nv, Gfn, Pevac, Rin, Rg):
            sb = scan_bufs
            LA1 = 1
            for it_ in range(NT + LA1):
                m = it_
                if m < NT:
                    for d in range(2):
                        tl = (order_f if d == 0 else order_b)[m]
                        r = m % RING
                        bsc = 0 if d == 0 else 3
                        bU = ((2, 6) if d == 0 else (5, 7))[m % 2]
                        mask = maskf if d == 0 else maskb
                        if tl >= 2:
                            A("pe", lambda e, d=d, tl=tl, bsc=bsc: e.matmul(pbs[bsc][:, 0:128], lhsT=kT(d, tl), rhs=qT(d, tl),
                                                                            start=True, stop=True),
                              reads=Rin(tl), writes=[Rpb[bsc]])
                            A("dve", lambda e, d=d, r=r, bsc=bsc, mask=mask: e.tensor_tensor(
                                out=sb["M"][d][r], in0=pbs[bsc][:, 0:128], in1=mask, op=ALU.mult),
                              reads=[Rpb[bsc], Rc], writes=[sb["RM"][d][r]])
                        A("pe", lambda e, d=d, tl=tl, bU=bU: e.matmul(pbs[bU][:, 0:Nv], lhsT=ktok(d, tl), rhs=vt(d, tl),
                                                                      start=True, stop=True),
                          reads=Rin(tl), writes=[Rpb[bU]])
                m = it_ - LA1
                if m >= 0:
                    for d in range(2):
                        tl = (order_f if d == 0 else order_b)[m]
                        r = m % RING
                        rp = (m - 1) % RING
                        rn = (m + 1) % RING
                        bP = 1 if d == 0 else 4
                        bU = ((2, 6) if d == 0 else (5, 7))[m % 2]
                        if m > 0:
                            gprev = Gfn(d, m - 1)
                            A("dve", lambda e, d=d, r=r, rp=rp, gprev=gprev, bU=bU: e.scalar_tensor_tensor(
                                out=sb["Z"][d][r], in0=sb["Z"][d][rp], scalar=gprev, in1=pbs[bU][:, 0:Nv],
                                op0=ALU.mult, op1=ALU.add),
                              reads=[sb["RZ"][d][rp], Rpb[bU]] + Rg, writes=[sb["RZ"][d][r]])
                        else:
                            A("dve", lambda e, d=d, r=r, bU=bU: e.tensor_copy(out=sb["Z"][d][r], in_=pbs[bU][:, 0:Nv]),
                              reads=[Rpb[bU]], writes=[sb["RZ"][d][r]])
                        if tl >= 2:
                            def mmP(e, d=d, tl=tl, bP=bP, m=m, r=r):
                                ins = e.matmul(pbs[bP][:, 0:Nv], lhsT=sb["M"][d][r], rhs=vt(d, tl), start=True, stop=(m == 0))
                                if m > 0:
                                    ins = e.matmul(pbs[bP][:, 0:Nv], lhsT=qT(d, tl), rhs=sb["S"][d][r], start=False, stop=True)
                                return ins
                            A("pe", mmP, reads=[sb["RM"][d][r], sb["RS"][d][r]] + Rin(tl), writes=[Rpb[bP]])
                            Pevac(d, tl, pbs[bP][:, 0:Nv], Rpb[bP])
                        if m < NT - 1:
                            gthis = Gfn(d, m)
                            A("act", lambda e, d=d, r=r, rn=rn, gthis=gthis: e.activation(
                                out=sb["S"][d][rn], in_=sb["Z"][d][r], func=AF.Copy, scale=gthis),
                              reads=[sb["RZ"][d][r]] + Rg, writes=[sb["RS"][d][rn]])

        PAD = ba(66 * 66).rearrange("p (r c) -> p r c", c=66)
        PADC = ba(258)
        RPAD = Res()
        A("pool", lambda e: e.memset(PAD, 0.0), writes=[RPAD])
        A("pool", lambda e: e.memset(PADC, 0.0), writes=[RPAD])
        qkT = [ba(NTOK) for _ in range(2)]
        KTOK = ba(NT * 128).rearrange("p (t c) -> p t c", c=128)
        VT = [ba(NT * 129).rearrange("p (t c) -> p t c", c=129) for _ in range(2)]
        OGs = [ba(32 * 128).rearrange("p (t c) -> p t c", c=128) for _ in range(2)]
        W4 = [ba(4 * 8 * 128).rearrange("p (w k n) -> p w k n", w=4, n=128)] * 2
        DGm = ba(2 * 9 * 128).rearrange("p (w t n) -> p w t n", w=2, n=128)
        PST = [fa(32 * 129).rearrange("p (t c) -> p t c", c=129) for _ in range(2)]
        mcwt = fa(72).rearrange("p (c t) -> p c t", t=9)
        sqjM = fa(128)
        dent = [fa(32) for _ in range(2)]
        ssqM = fa(32)
        Rmcw = Res()
        RW4 = [Res()] * 2
        RDG = Res()
        Rhd = Res("headdata")
        ROGs = [Res("og0"), Res("og1")]
        RssM = Res()
        RPST = Res()
        Ryst = Res()
        A("sp", lambda e: e.dma_start(out=mcwt, in_=D["mcw"][:, :, :]), writes=[Rmcw], dma="mcw")
        ysc_v = ysc.rearrange("(t p) c -> p t c", p=128)

        def silu_evac(ps_ap, ps_view_fn, dst_ap, n, bank, scale=1.0):
            A("act", lambda e: e.activation(out=dst_ap, in_=ps_ap, func=AF.Silu), reads=[Rpb[bank]], writes=[Rhd])

        def fin_gen(h):
            OG = OGs[h % 2]
            ROG = ROGs[h % 2]
            for d in range(2):
                A("act", lambda e, d=d: e.activation(out=dent[d], in_=PST[d][:, :, 128], func=AF.Abs), reads=[RPST], writes=[Rdn[d]])
                yield
                A("dve", lambda e, d=d, h=h: e.tensor_tensor(out=dent[d], in0=dent[d], in1=FL[:, 2:NT, h + 4 * d], op=ALU.max),
                  reads=[Rdn[d], Rgs], writes=[Rdn[d]])
                A("dve", lambda e, d=d: e.reciprocal(out=dent[d], in_=dent[d]), reads=[Rdn[d]], writes=[Rdn[d]])
                yield
                A("dve", lambda e, d=d: e.tensor_tensor(out=PST[d][:, :, 0:128], in0=PST[d][:, :, 0:128],
                                                        in1=dent[d].unsqueeze(2).to_broadcast([128, 32, 128]), op=ALU.mult),
                  reads=[RPST, Rdn[d]], writes=[RPST])
                yield
            hs = PST[0][:, :, 0:128]
            A("dve", lambda e: e.tensor_tensor(out=hs, in0=hs, in1=PST[1][:, :, 0:128], op=ALU.add), reads=[RPST], writes=[RPST])
            yield
            for i in range(32):
                A("act", lambda e, i=i: e.activation(out=sqjM, in_=PST[0][:, i, 0:128], func=AF.Square, accum_out=ssqM[:, i:i + 1]),
                  reads=[RPST], writes=[RssM])
                if i % 4 == 3:
                    yield
            A("act", lambda e: e.activation(out=ssqM, in_=ssqM, func=AF.Ln, scale=1.0 / 128.0, bias=EPS), reads=[RssM], writes=[RssM])
            A("act", lambda e: e.activation(out=ssqM, in_=ssqM, func=AF.Exp, scale=-0.5, bias=math.log(0.5)), reads=[RssM], writes=[RssM])
            yield
            A("dve", lambda e: e.tensor_tensor(out=hs, in0=hs, in1=ssqM.unsqueeze(2).to_broadcast([128, 32, 128]), op=ALU.mult),
              reads=[RPST, RssM], writes=[RPST])
            yield
            A("dve", lambda e: e.scalar_tensor_tensor(out=OG, in0=OG, scalar=1.0, in1=hs, op0=ALU.add, op1=ALU.mult),
              reads=[RPST, ROG], writes=[ROG])
            A("sp", lambda e, h=h: e.dma_start(out=ysc_v[:, :, h * 128:(h + 1) * 128], in_=OG), reads=[ROG], writes=[],
              dma="yst")
            yield

        pending_fin = [None]
        Rdn = [Res(), Res()]

        def next_fin():
            g = pending_fin[0]
            if g is None:
                return
            try:
                next(g)
            except StopIteration:
                pending_fin[0] = None

        def drain_fin():
            while pending_fin[0] is not None:
                next_fin()

        for h in range(4):
            wslot = h % 2
            for wi, c0 in enumerate((h * 128, 512 + h * 128, 1024 + h * 128, 1536 + h * 128)):
                A("pool", lambda e, wi=wi, c0=c0, wslot=wslot: e.dma_start(out=W4[wslot][:, wi], in_=w_in_v[:, :, c0:c0 + 128]),
                  writes=[RW4[wslot]], dma=f"w4_{wslot}_{wi}")
            for wi in range(2):
                ch = h + 4 * wi
                A("dve", lambda e, wi=wi, ch=ch: e.tensor_tensor(
                    out=DGm[:, wi], in0=identb[:].unsqueeze(1).to_broadcast([128, 9, 128]),
                    in1=mcwt[:, ch, :].unsqueeze(2).to_broadcast([128, 9, 128]), op=ALU.mult),
                  reads=[Rmcw, Rc], writes=[RDG])
            def v_step(t, wslot=wslot, h=h):
                bank = 4 + t % 2

                def mmv(e):
                    for wi2, c0 in ((2, 0), (3, 128)):
                        if wi2 == 3 and t < 2:
                            continue
                        for k in range(8):
                            ins = e.matmul(pbs[bank][:, c0:c0 + 128], lhsT=xT[:, k, t * 128:(t + 1) * 128],
                                           rhs=W4[wslot][:, wi2, k, :], start=(k == 0), stop=(k == 7))
                    return ins
                A("pe", mmv, reads=[RxT[t], RW4[wslot]], writes=[Rpb[bank]])
                for d in range(2):
                    A("act", lambda e, d=d: e.activation(
                        out=VT[d][:, t, 0:128], in_=pbs[bank][:, 0:128], func=AF.Copy, scale=WK[:, t, h + 4 * d:h + 4 * d + 1]),
                      reads=[Rpb[bank], Rgs], writes=[Rhd])
                if t >= 2:
                    A("act", lambda e, OGh=OGs[h % 2]: e.activation(out=OGh[:, t - 2, :], in_=pbs[bank][:, 128:256], func=AF.Tanh, scale=0.5),
                      reads=[Rpb[bank]], writes=[ROGs[h % 2]])

            vcnt = [0]

            def next_v():
                if vcnt[0] < NT:
                    v_step(vcnt[0])
                    vcnt[0] += 1

            for wi in range(2):
                W = W4[wslot][:, wi]
                dstT = qkT[wi]
                for i in range(9):
                    bank = i % 2
                    if i == 0:
                        tok0, ntk = 0, 256
                    else:
                        tok0, ntk = 256 + (i - 1) * 512, 512
                    tiles_needed = [RxT[tt] for tt in range(tok0 // 128, (tok0 + ntk) // 128)]

                    def mmq(e, W=W, tok0=tok0, ntk=ntk, bank=bank):
                        for k in range(8):
                            ins = e.matmul(pbs[bank][:, 0:ntk], lhsT=W[:, k, :], rhs=xT[:, k, tok0:tok0 + ntk],
                                           start=(k == 0), stop=(k == 7))
                        return ins
                    A("pe", mmq, reads=tiles_needed + [RW4[wslot]], writes=[Rpb[bank]])
                    if i == 0:
                        A("act", lambda e, bank=bank: e.activation(out=PADC[:, 1:257], in_=pbs[bank][:, 0:256], func=AF.Copy),
                          reads=[Rpb[bank]], writes=[RPAD])
                    else:
                        r0 = 1 + 8 * (i - 1)
                        A("act", lambda e, bank=bank, r0=r0: e.activation(
                            out=PAD[:, r0:r0 + 8, 1:65], in_=pbs[bank][:, 0:512].rearrange("p (r c) -> p r c", c=64), func=AF.Copy),
                          reads=[Rpb[bank]], writes=[RPAD])
                    next_v()
                    next_fin()
                for i in range(9):
                    bank = 2 + i % 2
                    if i == 0:
                        def mmc0(e, wi=wi, bank=bank):
                            for j, tap in enumerate((3, 4, 5)):
                                ins = e.matmul(pbs[bank][:, 0:256], lhsT=DGm[:, wi, tap, :], rhs=PADC[:, j:j + 256],
                                               start=(j == 0), stop=(j == 2))
                            return ins
                        A("pe", mmc0, reads=[RPAD, RDG], writes=[Rpb[bank]])
                        silu_evac(pbs[bank][:, 0:256], None, dstT[:, 0:256], 256, bank)
                    else:
                        r0 = 8 * (i - 1)

                        def mmc1(e, wi=wi, bank=bank, r0=r0):
                            for tap in range(9):
                                dr, dc = tap // 3, tap % 3
                                ins = e.matmul(pbs[bank][:, 0:512].rearrange("p (r c) -> p r c", c=64),
                                               lhsT=DGm[:, wi, tap, :], rhs=PAD[:, r0 + dr:r0 + dr + 8, dc:dc + 64],
                                               start=(tap == 0), stop=(tap == 8))
                            return ins
                        A("pe", mmc1, reads=[RPAD, RDG], writes=[Rpb[bank]])
                        t0 = 256 + (i - 1) * 512
                        silu_evac(pbs[bank][:, 0:512], None, dstT[:, t0:t0 + 512], 512, bank)
                    next_v()
                    next_fin()
            while vcnt[0] < NT:
                next_v()
            drain_fin()
            for g in range(5):
                t0g = g * 8
                ng = min(8, NT - t0g)
                pslot = g % 2

                def trk(e, t0g=t0g, ng=ng, pslot=pslot):
                    for j in range(ng):
                        ins = e.transpose(out=pts[pslot][:, j * 128:(j + 1) * 128],
                                          in_=qkT[1][:, (t0g + j) * 128:(t0g + j + 1) * 128], identity=identb[:])
                    return ins
                A("pe", trk, reads=[Rhd, Rc], writes=[Rpt[pslot]])
                A("act", lambda e, t0g=t0g, ng=ng, pslot=pslot: e.activation(
                    out=KTOK[:, t0g:t0g + ng, :], in_=pts[pslot][:, 0:ng * 128].rearrange("p (t c) -> p t c", c=128), func=AF.Copy),
                  reads=[Rpt[pslot]], writes=[Rhd])
            for d in range(2):
                A("dve", lambda e, d=d, h=h: e.tensor_copy(out=VT[d][:, :, 128:129], in_=WK[:, :, h + 4 * d:h + 4 * d + 1]),
                  reads=[Rgs], writes=[Rhd])

            def Gm(d, n, h=h):
                tln = (order_f if d == 0 else order_b)[n + 1]
                return DC[:, tln, h + 4 * d:h + 4 * d + 1]

            def Pev(d, tl, ps_ap, bres):
                A("act", lambda e: e.activation(out=PST[d][:, tl - 2, :], in_=ps_ap, func=AF.Copy), reads=[bres], writes=[RPST])
            run_scans(lambda d, tl: qkT[0][:, tl * 128:(tl + 1) * 128], lambda d, tl: qkT[1][:, tl * 128:(tl + 1) * 128],
                      lambda d, tl: KTOK[:, tl, :], lambda d, tl: VT[d][:, tl, :], 129, Gm, Pev, lambda tl: [Rhd], [Rgs])

            pending_fin[0] = fin_gen(h)
        drain_fin()

        reset_arenas()
        scan_bufs = alloc_scan(128)
        QKT = ba(NT * 512).rearrange("p (t c) -> p t c", c=512)
        KHa = ba(NT * 256).rearrange("p (t d c) -> p t d c", d=2, c=128)
        VH = ba(NT * 128).rearrange("p (t c) -> p t c", c=128)
        GG = ba(32 * 128).rearrange("p (t c) -> p t c", c=128)
        W5 = [ba(5 * 8 * 128).rearrange("p (w k n) -> p w k n", w=5, n=128)] * 2
        qtok = [ba(256).rearrange("p (d c) -> p d c", d=2) for _ in range(2)]
        PSTh = fa(32 * 128).rearrange("p (t c) -> p t c", c=128)
        sqj = fa(128)
        GCb = fa(NT * 4).rearrange("p (t c) -> p t c", c=4)
        GT = [fa(NT) for _ in range(2)]
        LB = fa(256).rearrange("p (d c) -> p d c", d=2)
        OML = fa(256).rearrange("p (d c) -> p d c", d=2)
        sgb = [fa(512) for _ in range(2)]
        qsb = [fa(128) for _ in range(3)]
        fgt = [fa(256) for _ in range(2)]
        lft = [fa(256) for _ in range(2)]
        kkt = [fa(256) for _ in range(3)]
        ept = [fa(256) for _ in range(2)]
        emt = [fa(256) for _ in range(2)]
        ssqH = fa(32)
        lfh = [ba(256) for _ in range(2)]
        lfl = [ba(256) for _ in range(2)]
        Rlh = [Res() for _ in range(2)]
        Mdb16 = ba(256).rearrange("p (d c) -> p d c", d=2)
        cwd16 = ba(4)
        Rc16 = Res()
        A("dve", lambda e: e.tensor_copy(out=Mdb16[:, 0, :], in_=Mdf), reads=[Rc], writes=[Rc16])
        A("dve", lambda e: e.tensor_copy(out=Mdb16[:, 1, :], in_=Mdb), reads=[Rc], writes=[Rc16])
        A("dve", lambda e: e.tensor_copy(out=cwd16, in_=cwd[:]), reads=[Rc], writes=[Rc16])
        Rlb = Res()
        RW5 = [Res()] * 2
        RGG = [Res() for _ in range(32)]
        RVH = [Res() for _ in range(NT)]
        RKH = [Res() for _ in range(NT)]
        RQK = [Res() for _ in range(NT)]
        RGT = Res()
        Rsg = [Res() for _ in range(2)]
        Rqs = [Res() for _ in range(3)]
        Rfg = [Res() for _ in range(2)]
        Rlf = [Res() for _ in range(2)]
        Rkk = [Res() for _ in range(3)]
        Rex = [Res() for _ in range(2)]
        Rqk = [Res() for _ in range(2)]
        RGC = Res()
        RPh = [Res() for _ in range(32)]
        RPall = Res()

        HG0 = 2064
        for h in range(4):
            hsl = slice(h * 128, (h + 1) * 128)
            A("sp", lambda e, hsl=hsl: e.dma_start(out=OML[:], in_=D["lbl"][:, :, 0, hsl]), writes=[Rlb], dma="lbl0")
            A("sp", lambda e, hsl=hsl: e.dma_start(out=LB[:], in_=D["lbl"][:, :, 1, hsl]), writes=[Rlb], dma="lbl1")
            A("dve", lambda e: e.tensor_tensor(out=LB[:], in0=LB[:], in1=OML[:], op=ALU.subtract), reads=[Rlb], writes=[Rlb])
            A("act", lambda e: e.activation(out=LB[:], in_=LB[:], func=AF.Exp), reads=[Rlb], writes=[Rlb])
            A("dve", lambda e: e.tensor_scalar_add(out=LB[:], in0=LB[:], scalar1=1.0), reads=[Rlb], writes=[Rlb])
            A("dve", lambda e: e.reciprocal(out=LB[:], in_=LB[:]), reads=[Rlb], writes=[Rlb])
            A("act", lambda e: e.activation(out=OML[:], in_=LB[:], func=AF.Identity, scale=-1.0, bias=1.0), reads=[Rlb], writes=[Rlb])
            cols = [HG0 + h * 128, HG0 + 512 + h * 128, HG0 + 1024 + h * 128, HG0 + 2048 + h * 128, HG0 + 1536 + h * 128]
            for wi, c0 in enumerate(cols):
                A("pool", lambda e, wi=wi, c0=c0: e.dma_start(out=W5[0][:, wi], in_=w_in_v[:, :, c0:c0 + 128]),
                  writes=[RW5[0]], dma=f"w5_{wi}")

            def st_mm5(t):
                bA = t % 2
                bV = 4 + t % 2

                def mm5(e):
                    for k in range(8):
                        e.matmul(pbs[bA][:, 0:512].rearrange("p (w n) -> p w n", w=4),
                                 lhsT=xT[:, k, t * 128:(t + 1) * 128], rhs=W5[0][:, 0:4, k, :],
                                 start=(k == 0), stop=(k == 7))
                    for k in range(8):
                        ins = e.matmul(pbs[bV][:, 0:128], lhsT=xT[:, k, t * 128:(t + 1) * 128], rhs=W5[0][:, 4, k, :],
                                       start=(k == 0), stop=(k == 7))
                    return ins
                A("pe", mm5, reads=[RxT[t], RW5[0]], writes=[Rpb[bA], Rpb[bV]])

            def st_sig(t):
                par = t % 2
                bA = par
                sg = sgb[par]
                A("act", lambda e: e.activation(out=sg, in_=pbs[bA][:, 0:512], func=AF.Exp, scale=-1.0),
                  reads=[Rpb[bA]], writes=[Rsg[par]])
                A("act", lambda e: e.activation(out=sg, in_=sg, func=AF.Ln, bias=1.0), reads=[Rsg[par]], writes=[Rsg[par]])
                A("act", lambda e: e.activation(out=sg, in_=sg, func=AF.Exp, scale=-1.0), reads=[Rsg[par]], writes=[Rsg[par]])

            def st_dvea(t):
                par = t % 2
                q3 = t % 3
                bA = par
                bV = 4 + par
                fv = fgt[par].rearrange("p (d c) -> p d c", d=2)
                A("dve", lambda e: e.tensor_tensor(
                    out=fv, in0=sgb[par][:, 128:384].rearrange("p (d c) -> p d c", d=2), in1=OML[:], op=ALU.mult),
                  reads=[Rsg[par], Rlb], writes=[Rfg[par]])
                A("dve", lambda e: e.tensor_tensor(out=fv, in0=fv, in1=LB[:], op=ALU.add),
                  reads=[Rfg[par], Rlb], writes=[Rfg[par]])
                A("dve", lambda e: e.scalar_tensor_tensor(out=qsb[q3], in0=pbs[bA][:, 0:128], scalar=QS,
                                                          in1=sgb[par][:, 0:128], op0=ALU.mult, op1=ALU.mult),
                  reads=[Rpb[bA], Rsg[par]], writes=[Rqs[q3]])
                if t >= 2:
                    A("dve", lambda e: e.tensor_tensor(out=GG[:, t - 2, :], in0=pbs[bA][:, 384:512],
                                                       in1=sgb[par][:, 384:512], op=ALU.mult),
                      reads=[Rpb[bA], Rsg[par]], writes=[RGG[t - 2]])
                A("dve", lambda e: e.tensor_copy(out=VH[:, t, :], in_=pbs[bV][:, 0:128]),
                  reads=[Rpb[bV]], writes=[RVH[t]])

            def st_lnf(t):
                par = t % 2
                bE = 2 + par
                A("act", lambda e: e.activation(out=lft[par], in_=fgt[par], func=AF.Ln), reads=[Rfg[par]], writes=[Rlf[par]])
                A("act", lambda e: e.activation(out=kkt[t % 3], in_=fgt[par], func=AF.Identity, scale=-1.0, bias=1.0),
                  reads=[Rfg[par]], writes=[Rkk[t % 3]])
                A("dve", lambda e: e.tensor_copy(out=lfh[par], in_=lft[par]), reads=[Rlf[par]], writes=[Rlh[par]])
                A("dve", lambda e: e.tensor_tensor(out=lfl[par], in0=lft[par], in1=lfh[par], op=ALU.subtract),
                  reads=[Rlf[par], Rlh[par]], writes=[Rlh[par]])

                def mme(e):
                    e.matmul(pbs[bE][:, 0:128], lhsT=Mdb16[:, 0, :], rhs=lfh[par][:, 0:128], start=True, stop=False)
                    e.matmul(pbs[bE][:, 0:128], lhsT=Mdb16[:, 0, :], rhs=lfl[par][:, 0:128], start=False, stop=True)
                    e.matmul(pbs[bE][:, 128:256], lhsT=Mdb16[:, 1, :], rhs=lfh[par][:, 128:256], start=True, stop=False)
                    e.matmul(pbs[bE][:, 128:256], lhsT=Mdb16[:, 1, :], rhs=lfl[par][:, 128:256], start=False, stop=True)
                    e.matmul(pbs[bE][:, 256:258], lhsT=lfh[par][:, 0:128], rhs=cwd16[:, 0:2], start=True, stop=False)
                    e.matmul(pbs[bE][:, 256:258], lhsT=lfl[par][:, 0:128], rhs=cwd16[:, 0:2], start=False, stop=True)
                    e.matmul(pbs[bE][:, 258:260], lhsT=lfh[par][:, 128:256], rhs=cwd16[:, 2:4], start=True, stop=False)
                    return e.matmul(pbs[bE][:, 258:260], lhsT=lfl[par][:, 128:256], rhs=cwd16[:, 2:4], start=False, stop=True)
                A("pe", mme, reads=[Rlh[par], Rc16], writes=[Rpb[bE]])

            def st_s2(t):
                par = t % 2
                q3 = t % 3
                bE = 2 + par
                A("act", lambda e: e.activation(out=ept[par], in_=pbs[bE][:, 0:256], func=AF.Exp),
                  reads=[Rpb[bE]], writes=[Rex[par]])
                A("act", lambda e: e.activation(out=emt[par], in_=pbs[bE][:, 0:256], func=AF.Exp, scale=-1.0),
                  reads=[Rpb[bE]], writes=[Rex[par]])
                A("act", lambda e: e.activation(out=GCb[:, t, :], in_=pbs[bE][:, 256:260], func=AF.Copy),
                  reads=[Rpb[bE]], writes=[RGC])

            def st_s2b(t):
                par = t % 2
                q3 = t % 3
                A("dve", lambda e: e.tensor_tensor(out=qtok[par], in0=ept[par].rearrange("p (d c) -> p d c", d=2),
                                                   in1=qsb[q3].unsqueeze(1).to_broadcast([128, 2, 128]), op=ALU.mult),
                  reads=[Rex[par], Rqs[q3]], writes=[Rqk[par]])
                A("dve", lambda e: e.tensor_tensor(out=KHa[:, t], in0=kkt[q3].rearrange("p (d c) -> p d c", d=2),
                                                   in1=emt[par].rearrange("p (d c) -> p d c", d=2), op=ALU.mult),
                  reads=[Rex[par], Rkk[q3]], writes=[Rqk[par], RKH[t]])

                def trq(e):
                    e.transpose(out=pts[par][:, 0:128], in_=qtok[par][:, 0, :], identity=identb[:])
                    e.transpose(out=pts[par][:, 128:256], in_=KHa[:, t, 0, :], identity=identb[:])
                    e.transpose(out=pts[par][:, 256:384], in_=qtok[par][:, 1, :], identity=identb[:])
                    return e.transpose(out=pts[par][:, 384:512], in_=KHa[:, t, 1, :], identity=identb[:])
                A("pe", trq, reads=[Rqk[par], Rc], writes=[Rpt[par]])
                A("dve", lambda e: e.tensor_copy(out=QKT[:, t, :], in_=pts[par][:, 0:512]),
                  reads=[Rpt[par]], writes=[RQK[t]])

            for i in range(NT + 2):
                if i < NT:
                    st_mm5(i)
                if 0 <= i - 2 < NT:
                    st_s2(i - 2)
                if 0 <= i - 1 < NT:
                    st_lnf(i - 1)
                if 0 <= i - 2 < NT:
                    st_s2b(i - 2)
                if i < NT:
                    st_sig(i)
                    st_dvea(i)
            A("dve", lambda e: e.tensor_tensor(out=GT[0][:, 0:NT - 1], in0=GCb[:, 0:NT - 1, 0], in1=GCb[:, 1:NT, 1],
                                               op=ALU.add), reads=[RGC], writes=[RGC])
            A("dve", lambda e: e.tensor_tensor(out=GT[1][:, 1:NT], in0=GCb[:, 1:NT, 2], in1=GCb[:, 0:NT - 1, 3],
                                               op=ALU.add), reads=[RGC], writes=[RGC])
            A("dve", lambda e: e.tensor_tensor(out=GT[1][:, 0:1], in0=GCb[:, 0, 2:3], in1=GCb[:, NT - 1, 3:4],
                                               op=ALU.add), reads=[RGC], writes=[RGC])
            A("act", lambda e: e.activation(out=GT[0][:, 0:NT - 1], in_=GT[0][:, 0:NT - 1], func=AF.Exp), reads=[RGC], writes=[RGT])
            A("act", lambda e: e.activation(out=GT[1], in_=GT[1], func=AF.Exp), reads=[RGC], writes=[RGT])

            def Gh(d, n):
                tl = (order_f if d == 0 else order_b)[n]
                return GT[d][:, tl:tl + 1]

            seen = set()

            def Pevh(d, tl, ps_ap, bres, seen=seen):
                i = tl - 2
                if i not in seen:
                    seen.add(i)
                    A("act", lambda e: e.activation(out=PSTh[:, i, :], in_=ps_ap, func=AF.Copy), reads=[bres], writes=[RPh[i]])
                else:
                    A("dve", lambda e: e.tensor_tensor(out=PSTh[:, i, :], in0=PSTh[:, i, :], in1=ps_ap, op=ALU.add),
                      reads=[bres, RPh[i]], writes=[RPh[i]])
            run_scans(lambda d, tl: QKT[:, tl, d * 256:d * 256 + 128], lambda d, tl: QKT[:, tl, d * 256 + 128:d * 256 + 256],
                      lambda d, tl: KHa[:, tl, d, :], lambda d, tl: VH[:, tl, :], 128, Gh, Pevh,
                      lambda tl: [RQK[tl], RKH[tl], RVH[tl]], [RGT])
            for i in range(32):
                A("act", lambda e, i=i: e.activation(out=sqj, in_=PSTh[:, i, :], func=AF.Square, accum_out=ssqH[:, i:i + 1]),
                  reads=[RPh[i]], writes=[RPall])
            A("act", lambda e: e.activation(out=ssqH, in_=ssqH, func=AF.Ln, scale=1.0 / 128.0, bias=EPS), reads=[RPall], writes=[RPall])
            A("act", lambda e: e.activation(out=ssqH, in_=ssqH, func=AF.Exp, scale=-0.5), reads=[RPall], writes=[RPall])
            A("dve", lambda e: e.tensor_tensor(out=PSTh, in0=PSTh, in1=ssqH.unsqueeze(2).to_broadcast([128, 32, 128]), op=ALU.mult),
              reads=RPh + [RPall], writes=[RPall])
            A("dve", lambda e: e.tensor_tensor(out=GG, in0=PSTh, in1=GG, op=ALU.mult), reads=[RPall] + RGG, writes=RGG + RPh)
            A("sp", lambda e, h=h: e.dma_start(out=ysc_v[:, :, 512 + h * 128:512 + (h + 1) * 128], in_=GG), reads=RGG,
              writes=[], dma="ysth")

        reset_arenas()
        G2 = fa(1024)

        Dg = fa(1024).rearrange("p (k n) -> p k n", n=128)
        RDg = Res()

        def build_G(Gdst, j0, RGd):
            for k in range(8):
                A("dve", lambda e, k=k: e.tensor_scalar_mul(out=Dg[:, k, :], in0=identf, scalar1=modT[:, j0 + k, 0:1]),
                  reads=[Rprm, Rc], writes=[RDg])
            for half in range(2):
                A("pe", lambda e, half=half: e.matmul(pbs[half][:, 0:512], lhsT=onesf,
                                                      rhs=Dg[:, 4 * half:4 * half + 4, :], start=True, stop=True),
                  reads=[RDg, Rc], writes=[Rpb[half]])
                A("act", lambda e, half=half: e.activation(out=Gdst[:, half * 512:(half + 1) * 512], in_=pbs[half][:, 0:512],
                                                           func=AF.Copy), reads=[Rpb[half]], writes=[RGd])

        wd = ba(22 * 1024).rearrange("p (k n) -> p k n", n=1024)
        Rwd3 = Res()
        G1 = fa(1024)
        RG1 = Res()
        build_G(G1, 16, RG1)
        RG2 = Res()
        build_G(G2, 40, RG2)
        wst4 = [fa(1024) for _ in range(2)]
        Rwst4 = [Res() for _ in range(2)]
        wo = ba(8 * 1024).rearrange("p (k n) -> p k n", n=1024)
        Rwo = Res()
        nwTt = fa(8)
        RnwT = Res()
        A("sp", lambda e: e.dma_start(out=nwTt, in_=D["nwT"][:, :]), writes=[RnwT], dma="nwT")
        wst3 = wst4[0]
        Rwst3 = Rwst4[0]
        for k in range(8):
            A("sp", lambda e, k=k: e.dma_start(out=wst3, in_=D["w_out"][k * 128:(k + 1) * 128, :]), writes=[Rwst3], dma="wst3")
            A("dve", lambda e, k=k: e.scalar_tensor_tensor(out=wo[:, k, :], in0=wst3, scalar=nwTt[:, k:k + 1], in1=G1,
                                                           op0=ALU.mult, op1=ALU.mult),
              reads=[Rwst3, RG1, RnwT], writes=[Rwo])
        xts3 = [fa(1024) for _ in range(4)]
        Rxt3 = [Res() for _ in range(4)]
        junk3 = fa(1024)
        Rjunk3 = Res()
        ssb3 = fa(8)
        Rss3 = [Res() for _ in range(4)]
        xnb3 = [ba(1024) for _ in range(2)]
        Rxn3 = [Res() for _ in range(2)]
        ytl = [ba(1024) for _ in range(3)]
        Ryt = [Res() for _ in range(3)]
        yTt = [ba(1024).rearrange("p (k t) -> p k t", t=128) for _ in range(2)]
        RyT = [Res() for _ in range(2)]
        Rx1 = [Res() for _ in range(32)]
        modtmp3 = [fa(1024) for _ in range(2)]
        Rmt3 = [Res() for _ in range(2)]

        def p3_L(t):
            y3 = t % 3
            x3 = t % 4
            A("sp", lambda e: e.dma_start(out=ytl[y3], in_=ysc[t * 128:(t + 1) * 128, :]), writes=[Ryt[y3]], dma=f"yt{y3}")
            A("sp", lambda e: e.dma_start(out=xts3[x3], in_=D["xin"][256 + t * 128:256 + (t + 1) * 128, :]),
              writes=[Rxt3[x3]], dma=f"x1t{x3}")

        def p3_A1(t):
            s = t % 2
            y3 = t % 3

            def try_(e):
                for k in range(8):
                    ins = e.transpose(out=pts[0][:, k * 128:(k + 1) * 128], in_=ytl[y3][:, k * 128:(k + 1) * 128], identity=identb[:])
                return ins
            A("pe", try_, reads=[Ryt[y3], Rc], writes=[Rpt[0]])
            A("act", lambda e: e.activation(out=yTt[s].rearrange("p k t -> p (k t)"), in_=pts[0][:, :], func=AF.Copy),
              reads=[Rpt[0]], writes=[RyT[s]])

        def p3_A2(t):
            s = t % 2
            x3 = t % 4
            for half in range(2):
                bank = 2 * s + half

                def mmo(e, half=half, bank=bank):
                    for k in range(8):
                        ins = e.matmul(pbs[bank][:, 0:512], lhsT=yTt[s][:, k, :], rhs=wo[:, k, half * 512:(half + 1) * 512],
                                       start=(k == 0), stop=(k == 7))
                    return ins
                A("pe", mmo, reads=[RyT[s], Rwo], writes=[Rpb[bank]])
                A("dve", lambda e, half=half, bank=bank: e.tensor_tensor(
                    out=xts3[x3][:, half * 512:(half + 1) * 512], in0=xts3[x3][:, half * 512:(half + 1) * 512], in1=pbs[bank][:, 0:512],
                    op=ALU.add), reads=[Rpb[bank], Rxt3[x3]], writes=[Rxt3[x3]])
            A("pool", lambda e: e.dma_start(out=x1sc[t * 128:(t + 1) * 128, :], in_=xts3[x3]), reads=[Rxt3[x3]], writes=[Rx1[t]],
              dma=f"x1w{x3}")
            s4 = t % 4
            ssv = ssb3[:, s4:s4 + 1]
            A("act", lambda e: e.activation(out=junk3, in_=xts3[x3], func=AF.Square, scale=1.0 / 32.0, accum_out=ssv),
              reads=[Rxt3[x3]], writes=[Rjunk3, Rss3[s4]])
            A("act", lambda e: e.activation(out=ssv, in_=ssv, func=AF.Ln, bias=EPS), reads=[Rss3[s4]], writes=[Rss3[s4]])
            A("act", lambda e: e.activation(out=ssv, in_=ssv, func=AF.Exp, scale=-0.5), reads=[Rss3[s4]], writes=[Rss3[s4]])

        def p3_A2b(t):
            s = t % 2
            x3 = t % 4
            s4 = t % 4
            ssv = ssb3[:, s4:s4 + 1]
            A("dve", lambda e: e.tensor_scalar_mul(out=xnb3[s], in0=xts3[x3], scalar1=ssv),
              reads=[Rss3[s4], Rxt3[x3]], writes=[Rxn3[s]])

        def p3_B(t):
            s = t % 2

            def tr2(e):
                for k in range(8):
                    ins = e.transpose(out=pts[1][:, k * 128:(k + 1) * 128], in_=xnb3[s][:, k * 128:(k + 1) * 128], identity=identb[:])
                return ins
            A("pe", tr2, reads=[Rxn3[s], Rc], writes=[Rpt[1]])
            A("dve", lambda e: e.tensor_tensor(out=modtmp3[s].rearrange("p (k c) -> p k c", k=8),
                                               in0=pts[1][:, :].rearrange("p (k c) -> p k c", k=8),
                                               in1=prm[:, 4, :].unsqueeze(2).to_broadcast([128, 8, 128]), op=ALU.mult),
              reads=[Rpt[1], Rprm], writes=[Rmt3[s]])
            A("dve", lambda e: e.tensor_tensor(out=xT[:, :, (t + 2) * 128:(t + 3) * 128],
                                               in0=modtmp3[s].rearrange("p (k c) -> p k c", k=8),
                                               in1=prm[:, 5, :].unsqueeze(2).to_broadcast([128, 8, 128]), op=ALU.add),
              reads=[Rmt3[s], Rprm], writes=[RxT[t + 2]])

        for i in range(35):
            if i < 32:
                p3_L(i)
            if 0 <= i - 1 < 32:
                p3_A1(i - 1)
            if 0 <= i - 2 < 32:
                p3_A2(i - 2)
            if 0 <= i - 3 < 32:
                p3_B(i - 3)
            if 0 <= i - 2 < 32:
                p3_A2b(i - 2)

        reset_arenas()
        G2 = fa(1024)
        gtmp = [fa(512) for _ in range(2)]
        Rgt = [Res() for _ in range(2)]
        wd = ba(22 * 1024).rearrange("p (k n) -> p k n", n=1024)
        Rwd = Res()
        HT = ba(22 * 512).rearrange("p (j t) -> p j t", t=512)
        RHT = Res()
        UP = [ba(10 * 66).rearrange("p (r c) -> p r c", c=66) for _ in range(2)]
        RUP = [Res() for _ in range(2)]
        WU = [ba(8 * 256).rearrange("p (k n) -> p k n", n=256) for _ in range(2)]
        RWU = [Res() for _ in range(2)]
        DGf = [ba(18 * 128).rearrange("p (w t n) -> p w t n", w=2, n=128) for _ in range(2)]
        RDGf = [Res() for _ in range(2)]
        fcwt = fa(44 * 9).rearrange("p (c t) -> p c t", t=9)
        fnwt = fa(1024)
        Rfc = Res()
        sat = fa(512)
        Rsa = Res()
        x1t = [fa(1024) for _ in range(4)]
        Rx1t = [Res() for _ in range(4)]
        HALO = fa(2816).bitcast(BF16).rearrange("p (j a r c) -> p j a r c", j=22, a=2, r=2)
        RHALO = [[Res() for _ in range(2)] for _ in range(22)]
        junk4 = fa(1024)
        Rjunk4 = Res()
        ssb4 = fa(8)
        Rss4 = [Res() for _ in range(4)]
        A("sp", lambda e: e.dma_start(out=fcwt, in_=D["fcw"][:, :, :]), writes=[Rfc], dma="fcw")
        A("sp", lambda e: e.dma_start(out=fnwt, in_=D["fnw"][:, :]), writes=[Rfc], dma="fnw")
        for ab in range(2):
            A("pool", lambda e, ab=ab: e.memset(UP[ab], 0.0), writes=[RUP[ab]])
        w_up_v = D["w_up"].rearrange("(k p) n -> p k n", p=128)
        it = 0
        NB = 8
        for blk in range(NB):
            R0 = 8 * blk
            if blk == NB - 1:
                for ab in range(2):
                    A("pool", lambda e, ab=ab: e.memset(UP[ab][:, 9, :], 0.0), writes=[RUP[ab]])
            for tt in range(4):
                gt = blk * 4 + tt
                A("sp", lambda e, gt=gt, tt=tt: e.dma_start(out=x1t[tt], in_=x1sc[gt * 128:(gt + 1) * 128, :]), reads=[Rx1[gt]],
                  writes=[Rx1t[tt]], dma=f"x1r{tt}")
            for j in range(22):
                ws = it % 2
                it += 1
                if blk > 0:
                    for ab in range(2):
                        A("dve", lambda e, ab=ab, j=j: e.tensor_copy(out=UP[ab][:, 0:2, 1:65], in_=HALO[:, j, ab]),
                          reads=[RHALO[j][ab]], writes=[RUP[ab]])
                if blk == 0:
                    A("pool", lambda e, j=j: e.dma_start(out=wd[:, j, :], in_=D["w_down"][j * 128:(j + 1) * 128, :]), writes=[Rwd],
                      dma="wdld")
                for ab in range(2):
                    c0 = ab * 2816 + j * 128
                    A("pool", lambda e, ws=ws, ab=ab, c0=c0: e.dma_start(out=WU[ws][:, :, ab * 128:(ab + 1) * 128],
                                                                          in_=w_up_v[:, :, c0:c0 + 128]),
                      writes=[RWU[ws]], dma=f"wu{ws}{ab}")
                    ch = ab * 22 + j
                    A("dve", lambda e, ws=ws, ab=ab, ch=ch: e.tensor_tensor(
                        out=DGf[ws][:, ab], in0=identb[:].unsqueeze(1).to_broadcast([128, 9, 128]),
                        in1=fcwt[:, ch, :].unsqueeze(2).to_broadcast([128, 9, 128]), op=ALU.mult),
                      reads=[Rfc, Rc], writes=[RDGf[ws]])
                if blk == 0:
                    groups = ((1, 8), (9, 1))
                elif blk == NB - 1:
                    groups = ((2, 7),)
                else:
                    groups = ((2, 8),)
                for ab in range(2):
                    for gi, (lo, nrow) in enumerate(groups):
                        tok0 = 256 + (R0 - 1 + lo) * 64
                        ntk = nrow * 64
                        bank = (ab + gi) % 2
                        tiles_needed = [RxT[tt] for tt in range(tok0 // 128, (tok0 + ntk + 127) // 128)]

                        def mmu(e, ws=ws, ab=ab, tok0=tok0, ntk=ntk, bank=bank):
                            for k in range(8):
                                ins = e.matmul(pbs[bank][:, 0:ntk], lhsT=WU[ws][:, k, ab * 128:(ab + 1) * 128],
                                               rhs=xT[:, k, tok0:tok0 + ntk], start=(k == 0), stop=(k == 7))
                            return ins
                        A("pe", mmu, reads=tiles_needed + [RWU[ws]], writes=[Rpb[bank]])
                        A("act", lambda e, ab=ab, lo=lo, nrow=nrow, ntk=ntk, bank=bank: e.activation(
                            out=UP[ab][:, lo:lo + nrow, 1:65], in_=pbs[bank][:, 0:ntk].rearrange("p (r c) -> p r c", c=64),
                            func=AF.Copy), reads=[Rpb[bank]], writes=[RUP[ab]])
                    if blk < NB - 1:
                        A("dve", lambda e, ab=ab, j=j: e.tensor_copy(out=HALO[:, j, ab], in_=UP[ab][:, 8:10, 1:65]),
                          reads=[RUP[ab]], writes=[RHALO[j][ab]])
                for ab in range(2):
                    bank = 2 + ab

                    def mmcf(e, ws=ws, ab=ab, bank=bank):
                        for tap in range(9):
                            dr, dc = tap // 3, tap % 3
                            ins = e.matmul(pbs[bank][:, 0:512].rearrange("p (r c) -> p r c", c=64),
                                           lhsT=DGf[ws][:, ab, tap, :],
                                           rhs=UP[ab][:, dr:dr + 8, dc:dc + 64],
                                           start=(tap == 0), stop=(tap == 8))
                        return ins
                    A("pe", mmcf, reads=[RUP[ab], RDGf[ws]], writes=[Rpb[bank]])
                A("act", lambda e: e.activation(out=sat, in_=pbs[2][:, 0:512], func=AF.Silu), reads=[Rpb[2]], writes=[Rsa])
                A("dve", lambda e, j=j: e.tensor_tensor(out=HT[:, j, :], in0=sat, in1=pbs[3][:, 0:512], op=ALU.mult),
                  reads=[Rsa, Rpb[3]], writes=[RHT])
            for tt in range(4):
                gt = blk * 4 + tt
                s = tt
                for half in range(2):
                    bank = 4 + half

                    def mmd(e, tt=tt, half=half, bank=bank):
                        for j in range(22):
                            ins = e.matmul(pbs[bank][:, 0:512], lhsT=HT[:, j, tt * 128:(tt + 1) * 128],
                                           rhs=wd[:, j, half * 512:(half + 1) * 512], start=(j == 0), stop=(j == 21))
                        return ins
                    A("pe", mmd, reads=[RHT, Rwd], writes=[Rpb[bank]])
                    A("dve", lambda e, half=half, bank=bank: e.tensor_tensor(
                        out=gtmp[half], in0=pbs[bank][:, 0:512], in1=G2[:, half * 512:(half + 1) * 512], op=ALU.mult),
                      reads=[Rpb[bank]], writes=[Rgt[half]])
                    A("dve", lambda e, s=s, half=half: e.tensor_tensor(
                        out=x1t[s][:, half * 512:(half + 1) * 512], in0=x1t[s][:, half * 512:(half + 1) * 512],
                        in1=gtmp[half], op=ALU.add), reads=[Rgt[half], Rx1t[s]], writes=[Rx1t[s]])
                s4 = gt % 4
                ssv = ssb4[:, s4:s4 + 1]
                A("act", lambda e, s=s, ssv=ssv: e.activation(out=junk4, in_=x1t[s], func=AF.Square, scale=1.0 / 32.0, accum_out=ssv),
                  reads=[Rx1t[s]], writes=[Rjunk4, Rss4[s4]])
                A("act", lambda e, ssv=ssv: e.activation(out=ssv, in_=ssv, func=AF.Ln, bias=EPS), reads=[Rss4[s4]], writes=[Rss4[s4]])
                A("act", lambda e, ssv=ssv: e.activation(out=ssv, in_=ssv, func=AF.Exp, scale=-0.5), reads=[Rss4[s4]], writes=[Rss4[s4]])
                A("dve", lambda e, s=s, ssv=ssv: e.scalar_tensor_tensor(out=x1t[s], in0=x1t[s], scalar=ssv, in1=fnwt,
                                                                        op0=ALU.mult, op1=ALU.mult),
                  reads=[Rss4[s4], Rx1t[s], Rfc], writes=[Rx1t[s]])
                A("sp", lambda e, gt=gt, s=s: e.dma_start(out=out[gt * 128:(gt + 1) * 128, :], in_=x1t[s]), reads=[Rx1t[s]], writes=[],
                  dma=f"out{s}")
        P.barrier()
        P.emit(st)
    return nc


def _consts():
    u = np.arange(128)[:, None]
    t = np.arange(128)[None, :]
    cf = np.zeros((8, 128, 128), np.float32)
    cf[0] = np.eye(128)
    cf[1] = 1.0
    cf[2] = (u <= t)
    cf[3] = (u >= t)
    cf[4] = (u > t)
    cf[5] = (u < t)
    cf[6] = (u <= t).astype(np.float32) - (u <= 63).astype(np.float32)
    cf[7] = (u >= t).astype(np.float32) - (u >= 64).astype(np.float32)
    cwd = np.zeros((128, 4), np.float32)
    uu = np.arange(128)
    cwd[:, 0] = uu > 63
    cwd[:, 1] = uu <= 63
    cwd[:, 2] = uu < 64
    cwd[:, 3] = uu >= 64
    return np.ascontiguousarray(cf.transpose(1, 0, 2)), cwd


_NC_CACHE = {}


def kernel(x, c, ctx, c_ctx, w_mod, b_mod, norm1_w, w_in, mlstm_gate_b, mlstm_conv_w, mlstm_norm_w,
           hgrn_lb_logits, hgrn_norm_w, w_out, norm2_w, w_up, ffn_conv_w, w_down, final_norm_w):
    f = lambda a: np.ascontiguousarray(np.asarray(a, dtype=np.float32))
    x, c, ctx, c_ctx = f(x), f(c), f(ctx), f(c_ctx)
    cf, cwd = _consts()
    shared = {
        "w_mod": f(w_mod[0]),
        "b_modT": f(np.asarray(b_mod[0]).reshape(48, 128).T),
        "n1T": f(np.asarray(norm1_w[0]).reshape(8, 128).T),
        "n2T": f(np.asarray(norm2_w[0]).reshape(8, 128).T),
        "fnw": f(np.broadcast_to(np.asarray(final_norm_w)[None, :], (128, 1024))),
        "w_in": f(w_in[0]),
        "gate_b": f(np.broadcast_to(np.asarray(mlstm_gate_b[0])[None, :], (128, 16))),
        "mcw": f(np.asarray(mlstm_conv_w[0]).reshape(9, 8, 128).transpose(2, 1, 0)),
        "fcw": f(np.asarray(ffn_conv_w[0]).reshape(9, 44, 128).transpose(2, 1, 0)),
        "nwT": f(np.concatenate([np.asarray(mlstm_norm_w[0]), np.asarray(hgrn_norm_w[0])]).reshape(8, 128).T),
        "lbl": f(np.broadcast_to(np.asarray(hgrn_lb_logits)[None], (128, 2, 2, 512))),
        "w_out": f(w_out[0]),
        "w_up": f(w_up[0]),
        "w_down": f(w_down[0]),
        "cf32": cf,
        "cwd": cwd,
        "identb": np.eye(128).astype(ml_dtypes.bfloat16),
    }
    in_maps = []
    for b in range(8):
        m = dict(shared)
        m["xin"] = np.ascontiguousarray(np.concatenate([ctx[b], x[b]], axis=0))
        m["c2"] = np.ascontiguousarray(np.stack([c[b], c_ctx], axis=1).reshape(8, 128, 2).transpose(1, 0, 2))
        in_maps.append(m)
    if "nc" not in _NC_CACHE:
        _NC_CACHE["nc"] = build_program()
    res = run_bass_kernel_spmd(_NC_CACHE["nc"], in_maps, core_ids=list(range(8)))
    return np.stack([np.asarray(r["out"], dtype=np.float32) for r in res.results], axis=0)
```
